# Optimizing a Trainium2 kernel written in Bass

```python
import jax
import jax.numpy as jnp
from jax import lax
import numpy as np


D_MODEL = 1024
BATCH = 8
SEQ = 2048
DEPTH = 1

CHUNK = 64
HG_HEADS = 8
HG_DK = 128
HG_DV = 128
HG_KEY_WIDTH = HG_HEADS * HG_DK
HG_WIDTH = HG_HEADS * HG_DV
SB_HEADS = 16
SB_DH = 64
SB_WIDTH = SB_HEADS * SB_DH
SB_BLOCK = 128
FF_HIDDEN = (8 * D_MODEL + 3 * 256 - 1) // (3 * 256) * 256
IN_SPLITS = (HG_KEY_WIDTH, HG_KEY_WIDTH, HG_WIDTH, HG_WIDTH, SB_WIDTH, SB_WIDTH, SB_WIDTH, D_MODEL, D_MODEL)
IN_WIDTH = sum(IN_SPLITS)
EPS = 1e-6

kernel_name = 'hybrid_hgrn2_stickbreaking_block'


def rms_norm(x, gain):
    x32 = x.astype(jnp.float32)
    y = x32 * lax.rsqrt(jnp.mean(x32 * x32, axis=-1, keepdims=True) + EPS) * gain.astype(jnp.float32)
    return y.astype(x.dtype)


def split_cols(t, widths):
    outs, start = [], 0
    for w in widths:
        outs.append(t[..., start:start + w])
        start += w
    return outs


def hgrn2_mixer(q, f_raw, i, lb):
    b, s, _ = q.shape
    n = s // CHUNK
    lb = lb.astype(jnp.float32)
    f = lb + (1.0 - lb) * jax.nn.sigmoid(f_raw.astype(jnp.float32))
    log_f = jnp.log(f)
    k = 1.0 - f

    def chunks(t, d):
        return t.astype(jnp.float32).reshape(b, n, CHUNK, HG_HEADS, d).transpose(1, 0, 3, 2, 4)

    qc, kc, gc = chunks(q, HG_DK), chunks(k, HG_DK), chunks(log_f, HG_DK)
    vc = chunks(i, HG_DV)
    causal = jnp.tril(jnp.ones((CHUNK, CHUNK), dtype=bool))[:, :, None]

    def step(state, inp):
        q_c, k_c, v_c, g_c = inp
        cum = jnp.cumsum(g_c, axis=-2)
        diff = cum[:, :, :, None, :] - cum[:, :, None, :, :]
        decay = jnp.exp(jnp.where(causal, diff, -jnp.inf))
        scores = jnp.einsum('bhtd,bhtsd,bhsd->bhts', q_c, decay, k_c)
        o = (jnp.einsum('bhts,bhsv->bhtv', scores, v_c)
             + jnp.einsum('bhtd,bhdv->bhtv', q_c * jnp.exp(cum), state))
        last = cum[:, :, -1:, :]
        state = (jnp.exp(last[:, :, 0, :, None]) * state
                 + jnp.einsum('bhsd,bhsv->bhdv', k_c * jnp.exp(last - cum), v_c))
        return state, o

    s0 = jnp.zeros((b, HG_HEADS, HG_DK, HG_DV), jnp.float32)
    _, o = lax.scan(step, s0, (qc, kc, vc, gc))
    return o.transpose(1, 0, 3, 2, 4).reshape(b, s, HG_HEADS, HG_DV)


def stick_breaking(q, k, v):
    s_len = q.shape[2]
    scale = 1.0 / float(np.sqrt(SB_DH))
    outs = []
    for blk in range(s_len // SB_BLOCK):
        q0 = blk * SB_BLOCK
        end = q0 + SB_BLOCK
        z = jnp.einsum('bhqd,bhkd->bhqk', q[:, :, q0:end], k[:, :, :end]) * scale
        qpos = q0 + jnp.arange(SB_BLOCK)
        kpos = jnp.arange(end)
        mask = kpos[None, :] < qpos[:, None]
        log_keep = jnp.where(mask, jax.nn.log_sigmoid(-z), 0.0)
        log_w = jax.nn.log_sigmoid(z) + lax.cumsum(log_keep, axis=3, reverse=True) - log_keep
        a = jnp.where(mask, jnp.exp(log_w), 0.0)
        outs.append(jnp.einsum('bhqk,bhkd->bhqd', a, v[:, :, :end]))
    return jnp.concatenate(outs, axis=2)


def setup_inputs(seed: int = 0) -> dict:
    key = jax.random.key(seed)
    ks = jax.random.split(key, 14)
    f32 = jnp.float32
    nrm = lambda k, shape, fan: jax.random.normal(k, shape, f32) * (fan ** -0.5)
    gain = lambda k, shape: 1.0 + 0.02 * jax.random.normal(k, shape, f32)
    return {
        'x': jax.random.normal(ks[0], (BATCH, SEQ, D_MODEL), f32),
        'norm1_gain': gain(ks[1], (DEPTH, D_MODEL)),
        'w_in': nrm(ks[2], (DEPTH, D_MODEL, IN_WIDTH), D_MODEL),
        'lb_logits': 0.5 * jax.random.normal(ks[3], (DEPTH + 1, HG_KEY_WIDTH), f32),
        'hg_out_norm': gain(ks[4], (DEPTH, HG_HEADS, HG_DV)),
        'sb_q_norm': gain(ks[5], (DEPTH, SB_HEADS, SB_DH)),
        'sb_k_norm': gain(ks[6], (DEPTH, SB_HEADS, SB_DH)),
        'w_hg_out': nrm(ks[7], (DEPTH, HG_WIDTH, D_MODEL), HG_WIDTH),
        'w_sb_out': nrm(ks[8], (DEPTH, SB_WIDTH, D_MODEL), SB_WIDTH),
        'w_o': nrm(ks[9], (DEPTH, D_MODEL, D_MODEL), D_MODEL),
        'norm2_gain': gain(ks[10], (DEPTH, D_MODEL)),
        'w_ffn_in': nrm(ks[11], (DEPTH, D_MODEL, 2 * FF_HIDDEN), D_MODEL),
        'w_ffn_out': nrm(ks[12], (DEPTH, FF_HIDDEN, D_MODEL), FF_HIDDEN),
    }


def reference(x, norm1_gain, w_in, lb_logits, hg_out_norm, sb_q_norm, sb_k_norm,
              w_hg_out, w_sb_out, w_o, norm2_gain, w_ffn_in, w_ffn_out):
    b, s, _ = x.shape
    lower_bounds = jnp.cumsum(jax.nn.softmax(lb_logits.astype(jnp.float32), axis=0), axis=0)
    for layer in range(DEPTH):
        h = rms_norm(x, norm1_gain[layer])
        proj = h @ w_in[layer]
        hq, hf, hi, hg, sq, sk, sv, ga, gb = split_cols(proj, IN_SPLITS)

        o_hg = hgrn2_mixer(hq, hf, hi, lower_bounds[layer])
        o_hg = rms_norm(o_hg, hg_out_norm[layer]).reshape(b, s, HG_WIDTH)
        o_hg = o_hg * jax.nn.sigmoid(hg.astype(jnp.float32))
        y_hg = o_hg.astype(x.dtype) @ w_hg_out[layer]

        def heads(t):
            return t.astype(jnp.float32).reshape(b, s, SB_HEADS, SB_DH)
        q = rms_norm(heads(sq), sb_q_norm[layer]).transpose(0, 2, 1, 3)
        k = rms_norm(heads(sk), sb_k_norm[layer]).transpose(0, 2, 1, 3)
        v = heads(sv).transpose(0, 2, 1, 3)
        o_sb = stick_breaking(q, k, v).transpose(0, 2, 1, 3).reshape(b, s, SB_WIDTH)
        y_sb = o_sb.astype(x.dtype) @ w_sb_out[layer]

        mixed = jax.nn.sigmoid(ga) * y_hg + jax.nn.sigmoid(gb) * y_sb
        x = x + mixed @ w_o[layer]

        h2 = rms_norm(x, norm2_gain[layer])
        gate, up = split_cols(h2 @ w_ffn_in[layer], (FF_HIDDEN, FF_HIDDEN))
        x = x + (jax.nn.silu(gate) * up) @ w_ffn_out[layer]
    return x
```

```python
import contextlib
import numpy as np
import ml_dtypes
import concourse.bass as bass
import concourse.mybir as mybir
from concourse.bass_utils import run_bass_kernel_spmd

F32 = mybir.dt.float32
BF16 = mybir.dt.bfloat16
AF = mybir.ActivationFunctionType
ALU = mybir.AluOpType

ENGS = ("sp", "act", "dve", "pool", "pe")
S = 2048
D = 1024
FF = 2816
NJ = 22
EPS = 1e-6
BIG = 30000.0


class _Stop(Exception):
    pass


class Op:
    __slots__ = ("eng", "fn", "deps", "signal", "sigval", "dma", "dsem", "dval")

    def __init__(self, eng, fn, dma=False):
        self.eng = eng
        self.fn = fn
        self.deps = []
        self.signal = False
        self.sigval = 0
        self.dma = dma
        self.dsem = None
        self.dval = 0


class Prog:
    def __init__(self, nc):
        self.nc = nc
        self.ops = {e: [] for e in ENGS}
        self.last_w = {}
        self.readers = {}
        self.dma_keys = {}
        self.barrier_deps = {e: [] for e in ENGS}
        self.dma_since_barrier = []

    @staticmethod
    def _need(d, op, raw):
        if d is op:
            return False
        if d.dma or op.dma:
            return True
        if d.eng == op.eng:
            if op.eng == "pe":
                return False
            return raw
        return True

    def add(self, eng, fn, reads=(), writes=(), dma_key=None):
        op = Op(eng, fn, dma=dma_key is not None)
        deps = {}
        for k in reads:
            w = self.last_w.get(k)
            if w is not None:
                deps[id(w)] = (w, True)
        for k in writes:
            w = self.last_w.get(k)
            if w is not None and id(w) not in deps:
                deps[id(w)] = (w, False)
            for r in self.readers.get(k, ()):
                if id(r) not in deps:
                    deps[id(r)] = (r, False)
        for d in self.barrier_deps[eng]:
            if id(d) not in deps:
                deps[id(d)] = (d, True)
        self.barrier_deps[eng] = []
        for d, raw in deps.values():
            if self._need(d, op, raw):
                op.deps.append(d)
                d.signal = True
        for k in reads:
            self.readers.setdefault(k, []).append(op)
        for k in writes:
            self.last_w[k] = op
            self.readers[k] = []
        if op.dma:
            cnt = self.dma_keys.setdefault(dma_key, [0])
            cnt[0] += 16
            op.dsem = dma_key
            op.dval = cnt[0]
            self.dma_since_barrier.append(op)
        self.ops[eng].append(op)
        return op

    def barrier(self):
        lasts = [self.ops[e][-1] for e in ENGS if self.ops[e]]
        lasts += self.dma_since_barrier
        self.dma_since_barrier = []
        for e in ENGS:
            self.barrier_deps[e] = list(lasts)
        self.last_w = {}
        self.readers = {}

    def emit(self):
        nc = self.nc
        for e in ENGS:
            cnt = 0
            for op in self.ops[e]:
                if op.signal and not op.dma:
                    cnt += 1
                    op.sigval = cnt
        with contextlib.ExitStack() as st:
            esem = {e: st.enter_context(nc.semaphore("s_" + e)) for e in ENGS}
            dsem = {k: st.enter_context(nc.semaphore("d_%d" % i))
                    for i, k in enumerate(self.dma_keys)}
            block = st.enter_context(nc.Block())
            prog = self

            def run(engname, engine):
                waited = {}
                for op in prog.ops[engname]:
                    need = {}
                    for d in op.deps:
                        if d.dma:
                            s, v = dsem[d.dsem], d.dval
                        else:
                            s, v = esem[d.eng], d.sigval
                        if need.get(id(s), (None, 0))[1] < v:
                            need[id(s)] = (s, v)
                    for key, (s, v) in need.items():
                        if waited.get(key, 0) >= v:
                            continue
                        waited[key] = v
                        engine.wait_ge(s, v)
                    inst = op.fn(engine)
                    if op.dma:
                        inst.then_inc(dsem[op.dsem], 16)
                    elif op.signal:
                        inst.then_inc(esem[engname], 1)

            @block.sync
            def _(eng):
                run("sp", eng)

            @block.scalar
            def _(eng):
                run("act", eng)

            @block.vector
            def _(eng):
                run("dve", eng)

            @block.gpsimd
            def _(eng):
                run("pool", eng)

            @block.tensor
            def _(eng):
                run("pe", eng)


CF_G1 = 0
CF_G2 = 1024
CF_RST = 2048
CF_M64 = 2560
CF_LBL = 2688
CF_GQ = 2704
CF_GK = 2712
CF_GHG = 2720
CF_EPS = 2728
CF_ONE = 2729
CF_N = 2736
CB_N = 6 * 128


def build_nc(dump=None, upto=99):
    dump = dump or ()
    nc = bass.Bass("TRN2", target_bir_lowering=False)
    dram = lambda n, s, dt=F32: nc.dram_tensor(n, s, dt, kind="ExternalInput").ap()
    x = dram("x", [S, D])
    cf_d = dram("cf", [128, CF_N])
    cb_d = dram("cb", [128, CB_N], BF16)
    wiv_d = dram("wiv", [4, 128, 8, 512])
    whg_d = dram("whg", [8, 128, 8, 384])
    wsb_d = dram("wsb", [8, 128, 8, 256])
    we_d = dram("we", [8, 128, 8, 512])
    wo_d = dram("wo", [2, 128, 8, 512])
    wfi_d = dram("wfi", [NJ, 128, 8, 256])
    wfo_d = dram("wfo", [128, NJ, 1024])
    out = nc.dram_tensor("out", [S, D], F32, kind="ExternalOutput").ap()
    dumps = {}
    for name in dump:
        shp = {"hT": [128, 8 * S], "ohg": [128, 8 * S], "osb": [128, 8 * S],
               "mix": [128, 8 * S]}[name]
        dumps[name] = nc.dram_tensor("dump_" + name, shp, BF16, kind="ExternalOutput").ap()

    with contextlib.ExitStack() as st:
        ARENA = 206 * 1024
        arena = st.enter_context(nc.sbuf_tensor("arena", [128, ARENA // 2], BF16))
        banks = [st.enter_context(nc.psum_tensor("ps%d" % i, [128, 512], F32)) for i in range(8)]
        banksb = [b.bitcast(BF16) for b in banks]

        def reg(off, nbytes, dt):
            assert off % 4 == 0 and off + nbytes <= ARENA, (off, nbytes)
            a = arena[:, off // 2:(off + nbytes) // 2]
            return a if dt == BF16 else a.bitcast(dt)

        K = 1024
        P = Prog(nc)
        A = P.add

        def ck(v):
            if upto <= v:
                raise _Stop()

        def finish():
            P.barrier()
            A("sp", lambda e: e.nop())
            P.emit()
            return nc
        cnt = [0]

        def alt(n=2):
            cnt[0] += 1
            return cnt[0] % n

        hT = reg(0, 32 * K, BF16).rearrange("p (k t) -> p k t", t=S)
        ohgT = reg(32 * K, 32 * K, BF16).rearrange("p (k t) -> p k t", t=S)
        osbT = reg(64 * K, 32 * K, BF16).rearrange("p (k t) -> p k t", t=S)
        CO = 184 * K
        cf = reg(CO, CF_N * 4, F32)
        cb = reg(CO + CF_N * 4, CB_N * 2, BF16)
        small = reg(CO + CF_N * 4 + CB_N * 2, 1024, F32)
        ident = cb[:, 0:128]
        triincl = cb[:, 128:256]
        ones = cb[:, 256:384]
        blockones = cb[:, 384:512]
        posmask = cb[:, 512:640]
        zeros = cb[:, 640:768]
        g1bc = cf[:, CF_G1:CF_G1 + 1024]
        g2bc = cf[:, CF_G2:CF_G2 + 1024]
        rstm = cf[:, CF_RST:CF_RST + 512]
        m64 = cf[:, CF_M64:CF_M64 + 128].rearrange("p (a t) -> p a t", t=64)
        epsc = cf[:, CF_EPS:CF_EPS + 1]
        onec = cf[:, CF_ONE:CF_ONE + 1]
        ss1 = small[:, 0:16]
        ln1 = small[:, 16:32]
        rstd1 = small[:, 32:48]
        lb = small[:, 48:56]
        oml = small[:, 56:64]
        gq8 = small[:, 64:72]
        ngk = small[:, 72:80]
        tmp8 = small[:, 80:88]
        ss2 = small[:, 96:112]
        ln2 = small[:, 112:128]
        rstd2 = small[:, 128:144]
        el = small[:, 144:176]

        A("sp", lambda e: e.dma_start(out=cf, in_=cf_d), writes=["cf"], dma_key="cf")
        A("sp", lambda e: e.dma_start(out=cb, in_=cb_d), writes=["cb"], dma_key="cb")
        A("dve", lambda e: e.tensor_tensor(out=tmp8, in0=cf[:, CF_LBL + 8:CF_LBL + 16], in1=cf[:, CF_LBL:CF_LBL + 8], op=ALU.subtract), reads=["cf"], writes=["tmp8"])
        A("act", lambda e: e.activation(out=tmp8, in_=tmp8, func=AF.Exp), reads=["tmp8"], writes=["tmp8"])
        A("dve", lambda e: e.tensor_scalar(out=tmp8, in0=tmp8, scalar1=1.0, scalar2=None, op0=ALU.add), reads=["tmp8"], writes=["tmp8"])
        A("dve", lambda e: e.reciprocal(out=lb, in_=tmp8), reads=["tmp8"], writes=["lb"])
        A("dve", lambda e: e.tensor_scalar(out=oml, in0=lb, scalar1=-1.0, scalar2=1.0, op0=ALU.mult, op1=ALU.add), reads=["lb"], writes=["oml"])
        A("dve", lambda e: e.tensor_scalar(out=gq8, in0=cf[:, CF_GQ:CF_GQ + 8], scalar1=0.125, scalar2=None, op0=ALU.mult), reads=["cf"], writes=["gq8"])
        A("dve", lambda e: e.tensor_scalar(out=ngk, in0=cf[:, CF_GK:CF_GK + 8], scalar1=-1.0, scalar2=None, op0=ALU.mult), reads=["cf"], writes=["ngk"])
        P.barrier()

        SCR = 96 * K

        def rms_rows(src, sscol, lncol, rstdcol, gbc, xn_out, junk, kin, kout):
            A("act", lambda e: e.activation(out=junk, in_=src, func=AF.Square, accum_out=sscol),
              reads=kin, writes=["junk", ("st", id(sscol))])
            A("act", lambda e: e.activation(out=lncol, in_=sscol, func=AF.Ln, bias=epsc, scale=1.0 / D),
              reads=[("st", id(sscol))], writes=[("st", id(lncol))])
            A("act", lambda e: e.activation(out=rstdcol, in_=lncol, func=AF.Exp, scale=-0.5),
              reads=[("st", id(lncol))], writes=[("st", id(rstdcol))])
            A("dve", lambda e: e.scalar_tensor_tensor(out=xn_out, in0=src, scalar=rstdcol, in1=gbc, op0=ALU.mult, op1=ALU.mult),
              reads=kin + [("st", id(rstdcol))], writes=kout)

        AS = SCR + 48 * K
        xt = [reg(AS + i * 4 * K, 4 * K, F32) for i in range(3)]
        xn = [reg(AS + 12 * K + i * 2 * K, 2 * K, BF16) for i in range(2)]
        junk = reg(AS + 16 * K, 2 * K, BF16)
        for tb in range(16):
            s3 = tb % 3
            s2 = tb % 2
            bk = tb % 2
            A("sp", lambda e, tb=tb, s3=s3: e.dma_start(out=xt[s3], in_=x[tb * 128:(tb + 1) * 128, :]),
              writes=[("xt", s3)], dma_key=("xt", s3))
            rms_rows(xt[s3], ss1[:, tb:tb + 1], ln1[:, tb:tb + 1], rstd1[:, tb:tb + 1], g1bc, xn[s2], junk,
                     [("xt", s3)], [("xn", s2)])
            for kc in range(8):
                A("pe", lambda e, kc=kc, s2=s2, bk=bk: e.transpose(out=banksb[bk][:, kc * 128:(kc + 1) * 128], in_=xn[s2][:, kc * 128:(kc + 1) * 128], identity=ident),
                  reads=[("xn", s2)], writes=[("ps", bk)])
            src = banksb[bk][:, :].rearrange("p (k t) -> p k t", t=128)
            dst = hT[:, :, tb * 128:(tb + 1) * 128]
            if tb % 2 == 0:
                A("act", lambda e, src=src, dst=dst: e.copy(out=dst, in_=src), reads=[("ps", bk)], writes=[("hT", tb)])
            else:
                A("dve", lambda e, src=src, dst=dst: e.tensor_copy(out=dst, in_=src), reads=[("ps", bk)], writes=[("hT", tb)])
        if "hT" in dumps or upto <= 1:
            P.barrier()
        if "hT" in dumps:
            A("sp", lambda e: e.dma_start(out=dumps["hT"], in_=reg(0, 32 * K, BF16)), dma_key="dump")
            P.barrier()

        if upto <= 1:
            return finish()
        hTk = lambda tt: [("hT", tt * 4 + i) for i in range(4)]

        def mm(o_, lhsT, rhs, start, stop, reads, writes):
            A("pe", lambda e: e.matmul(o_, lhsT=lhsT, rhs=rhs, start=start, stop=stop), reads=reads, writes=writes)

        def tr(o_, in_, reads, writes):
            A("pe", lambda e: e.transpose(out=o_, in_=in_, identity=ident), reads=reads, writes=writes)

        def act(o_, in_, func, reads, writes, **kw):
            A("act", lambda e: e.activation(out=o_, in_=in_, func=func, **kw), reads=reads, writes=writes)

        def acopy(o_, in_, reads, writes):
            A("act", lambda e: e.copy(out=o_, in_=in_), reads=reads, writes=writes)

        def vcopy(o_, in_, reads, writes):
            A("dve", lambda e: e.tensor_copy(out=o_, in_=in_), reads=reads, writes=writes)

        def tt_(eng, o_, in0, in1, op, reads, writes):
            A(eng, lambda e: e.tensor_tensor(out=o_, in0=in0, in1=in1, op=op), reads=reads, writes=writes)

        def ts_(eng, o_, in0, s1, s2, op0, op1, reads, writes):
            if s2 is None:
                A(eng, lambda e: e.tensor_scalar(out=o_, in0=in0, scalar1=s1, scalar2=None, op0=op0), reads=reads, writes=writes)
            else:
                A(eng, lambda e: e.tensor_scalar(out=o_, in0=in0, scalar1=s1, scalar2=s2, op0=op0, op1=op1), reads=reads, writes=writes)

        def stt_(o_, in0, sc_, in1, op0, op1, reads, writes):
            A("dve", lambda e: e.scalar_tensor_tensor(out=o_, in0=in0, scalar=sc_, in1=in1, op0=op0, op1=op1), reads=reads, writes=writes)

        XI = SCR
        XV = SCR + 4 * K
        wblk = reg(SCR + 8 * K, 8 * K, BF16).rearrange("p (k c) -> p k c", c=512)

        def tokI(h):
            base = 64 * K + (h + 1) * 4 * K if h < 7 else XI
            return reg(base, 4 * K, BF16).rearrange("p (b c) -> p b c", c=128)

        def tokV(p):
            base = 32 * K + (p + 1) * 4 * K if p < 7 else XV
            return reg(base, 4 * K, BF16).rearrange("p (b c) -> p b c", c=128)

        keyI = lambda h: ("osbc", h + 1) if h < 7 else ("XI",)
        keyV = lambda p: ("ohgc", p + 1) if p < 7 else ("XV",)

        for gidx in range(4):
            A("pool", lambda e, gidx=gidx: e.dma_start(out=wblk, in_=wiv_d[gidx]), writes=["wblk"], dma_key="wblk")
            for tb in range(16):
                bk = tb % 2
                for kc in range(8):
                    mm(banks[bk][:, :], hT[:, kc, tb * 128:(tb + 1) * 128], wblk[:, kc, :], kc == 0, kc == 7,
                       [("hT", tb), "wblk"], [("ps", bk)])
                for j in range(4):
                    u = 4 * (gidx % 2) + j
                    dst = (tokI(u) if gidx < 2 else tokV(u))[:, tb, :]
                    kk_ = keyI(u) if gidx < 2 else keyV(u)
                    acopy(dst, banks[bk][:, j * 128:(j + 1) * 128], [("ps", bk)], [kk_])
        P.barrier()
        if upto <= 2:
            return finish()

        PC = SCR + 8 * K
        b_ef, b_fp, b_g, b_kk, b_cum, b_dm, b_ep, b_em, b_sg = [reg(PC + i * 2 * K, 2 * K, F32) for i in range(9)]
        b_ln, b_rs, b_o1, b_eg = b_ef, b_fp, b_g, b_cum
        KEF, KFP, KG, KKK, KCUM, KDM, KEP, KEM, KSG = ["cb%d" % i for i in range(9)]
        o = PC + 18 * K
        wh = [reg(o + i * 6 * K, 6 * K, BF16).rearrange("p (k c) -> p k c", c=384) for i in range(2)]
        o += 12 * K
        qtl = reg(o, K, BF16)
        ktl = reg(o + K, K, BF16)
        ktokE = reg(o + 2 * K, K, BF16).rearrange("p (b d) -> p b d", d=128)
        ktokO = reg(o + 3 * K, K, BF16).rearrange("p (b d) -> p b d", d=128)
        A("pool", lambda e: e.memset(reg(PC + 30 * K + 2 * K, 2 * K, BF16), 0.0), writes=["ktokE", "ktokO"])
        o += 4 * K
        scm = [reg(o + i * K, K, BF16).rearrange("p (a t) -> p a t", t=64) for i in range(2)]
        o += 2 * K
        Sst = [reg(o + i * 512, 512, F32) for i in range(2)]
        o += K
        Pb = [reg(o + i * 256, 256, BF16) for i in range(4)]
        o += K
        sqb = reg(o, K, BF16)
        o += K
        assert o == PC + 39 * K
        BCP, BCS, BCO = 5, 6, 7

        def hgrn_gen(h):
            ws = h % 2
            if h == 0:
                A("pool", lambda e: e.dma_start(out=wh[0], in_=whg_d[0]), writes=[("wh", 0)], dma_key=("wh", 0))
            if h + 1 < 8:
                nws = (h + 1) % 2
                A("pool", lambda e: e.dma_start(out=wh[nws], in_=whg_d[h + 1]), writes=[("wh", nws)], dma_key=("wh", nws))
            lbc = lb[:, h:h + 1]
            omlc = oml[:, h:h + 1]
            gcol = cf[:, CF_GHG + h:CF_GHG + h + 1]
            tokh = tokI(h)
            kI = keyI(h)
            PSP, PSS, PSO = [("ps", BCP)], [("ps", BCS)], [("ps", BCO)]
            cp, cs_, co = banks[BCP], banks[BCS], banks[BCO]
            cpb = banksb[BCP]

            def proj(c0, tt):
                for kc in range(8):
                    mm(cp[:, :], wh[ws][:, kc, c0:c0 + 128], hT[:, kc, tt * 512:(tt + 1) * 512], kc == 0, kc == 7,
                       hTk(tt) + [("wh", ws)], PSP)

            for tt in range(4):
                ts = slice(tt * 512, (tt + 1) * 512)
                proj(128, tt)
                yield
                act(b_ef, cp[:, :], AF.Exp, PSP, [KEF], scale=-1.0)
                yield
                ts_("dve", b_ef, b_ef, 1.0, None, ALU.add, None, [KEF], [KEF])
                yield
                A("dve", lambda e: e.reciprocal(out=b_fp, in_=b_ef), reads=[KEF], writes=[KFP])
                yield
                ts_("dve", b_fp, b_fp, omlc, lbc, ALU.mult, ALU.add, [KFP], [KFP])
                proj(0, tt)
                yield
                act(b_g, b_fp, AF.Ln, [KFP], [KG])
                ts_("pool", b_kk, b_fp, -1.0, 1.0, ALU.mult, ALU.add, [KFP], [KKK])
                yield
                A("dve", lambda e: e.tensor_tensor_scan(out=b_cum, data0=rstm, data1=b_g, initial=0.0, op0=ALU.mult, op1=ALU.add), reads=[KG], writes=[KCUM])
                yield
                cum3 = b_cum.rearrange("p (c t) -> p c t", t=64)
                last = cum3[:, :, 63:64]
                tt_("dve", b_dm.rearrange("p (c t) -> p c t", t=64), cum3, last.broadcast_to([128, 8, 64]), ALU.subtract, [KCUM], [KDM])
                act(el[:, tt * 8:(tt + 1) * 8].unsqueeze(2), last, AF.Exp, [KCUM], [("el", tt)])
                yield
                act(b_ep, b_dm, AF.Exp, [KDM], [KEP])
                act(b_em, b_dm, AF.Exp, [KDM], [KEM], scale=-1.0)
                yield
                tt_("dve", qtl, cp[:, :], b_ep, ALU.mult, PSP + [KEP], ["qtl"])
                yield
                tt_("dve", ktl, b_kk, b_em, ALU.mult, [KKK, KEM], ["ktl"])
                proj(256, tt)
                yield
                act(b_eg, cp[:, :], AF.Exp, PSP, [KCUM], scale=-1.0)
                yield
                ts_("dve", b_eg, b_eg, 1.0, None, ALU.add, None, [KCUM], [KCUM])
                for i in range(4):
                    tr(cpb[:, i * 128:(i + 1) * 128], ktl[:, i * 128:(i + 1) * 128], ["ktl"], PSP)
                yield
                A("dve", lambda e: e.reciprocal(out=b_sg, in_=b_eg), reads=[KCUM], writes=[KSG])
                acopy(ktokE[0:64, :, :], cpb[0:64, 0:512].rearrange("p (b d) -> p b d", d=128), PSP, ["ktokE"])
                yield
                vcopy(ktokO[64:128, :, :], cpb[64:128, 0:512].rearrange("p (b d) -> p b d", d=128), PSP, ["ktokO"])
                sc = alt()
                for c in range(8):
                    mm(cs_[:, c * 64:(c + 1) * 64], ktl[:, (c // 2) * 128:(c // 2 + 1) * 128], qtl[:, c * 64:(c + 1) * 64], True, True,
                       ["ktl", "qtl"], PSS)
                yield
                tt_("dve", scm[sc].rearrange("p (a b) t -> p a b t", b=2), cs_[:, :].rearrange("p (a b t) -> p a b t", b=2, t=64),
                    m64.unsqueeze(1).broadcast_to([128, 4, 2, 64]), ALU.mult, PSS, [("scm", sc)])
                yield
                for half in range(2):
                    for c in range(half * 4, half * 4 + 4):
                        cg = tt * 8 + c
                        tb = cg // 2
                        kt_ = ktokE if cg % 2 == 0 else ktokO
                        mm(cs_[:, (c % 4) * 128:(c % 4 + 1) * 128], kt_[:, c // 2, :], tokh[:, tb, :], True, True,
                           ["ktokE", "ktokO", kI], PSS)
                    yield
                    for c in range(half * 4, half * 4 + 4):
                        cg = tt * 8 + c
                        tb = cg // 2
                        usrc = cs_[:, (c % 4) * 128:(c % 4 + 1) * 128]
                        sn, so = cg % 2, (cg + 1) % 2
                        pbi = cg % 4
                        elc = el[:, cg:cg + 1]
                        if cg == 0:
                            vcopy(Sst[sn], usrc, PSS, [("S", sn)])
                        else:
                            ts_("dve", Pb[pbi], Sst[so], elc, None, ALU.mult, None, [("S", so), ("el", tt)], [("Pb", pbi)])
                            stt_(Sst[sn], Sst[so], elc, usrc, ALU.mult, ALU.add, [("S", so), ("el", tt)] + PSS, [("S", sn)])
                        mm(co[:, c * 64:(c + 1) * 64], tokh[:, tb, :], scm[sc][:, c, :], True, cg == 0, [kI, ("scm", sc)], PSO)
                        if cg > 0:
                            mm(co[:, c * 64:(c + 1) * 64], Pb[pbi], qtl[:, c * 64:(c + 1) * 64], False, True, [("Pb", pbi), "qtl"], PSO)
                        yield
                act(sqb, co[:, :], AF.Square, PSO, ["sqb"])
                yield
                mm(cp[:, :], ones, sqb, True, True, ["sqb"], PSP)
                yield
                act(b_ln, cp[:, :], AF.Ln, PSP, [KEF], bias=epsc, scale=1.0 / 128)
                yield
                act(b_rs, b_ln, AF.Exp, [KEF], [KFP], scale=-0.5)
                yield
                tt_("dve", b_o1, co[:, :], b_rs, ALU.mult, PSO + [KFP], [KG])
                yield
                stt_(ohgT[:, h, ts], b_o1, gcol, b_sg, ALU.mult, ALU.mult, [KG, KSG], [("ohgc", h)])
                yield

        PD = PC + 39 * K
        o = PD
        wq = [reg(o + i * 4 * K, 4 * K, BF16).rearrange("p (k c) -> p k c", c=256) for i in range(2)]
        o += 8 * K
        qnT = reg(o, 4 * K, BF16)
        nk = [reg(o + 4 * K, 4 * K, BF16), reg(o + 8 * K, 4 * K, BF16)]
        A("pool", lambda e: e.memset(reg(PD + 8 * K + 4 * K, 8 * K, BF16), 0.0), writes=[("qk", 1, i) for i in range(4)])
        o += 12 * K
        d_ln = reg(o, 2 * K, F32)
        d_rs = reg(o + 2 * K, 2 * K, F32)
        o += 4 * K
        d_sq = reg(o, 1 * K, BF16)
        o += 1 * K
        EbBase = o
        Eb = [reg(o + i * 2 * K, 2 * K, F32) for i in range(2)]
        o += 4 * K
        SPb = [reg(o + i * K, K, BF16) for i in range(3)]
        SPb.append(reg(EbBase, K, BF16))
        o += 3 * K
        Ab = [reg(o + i * K, K, BF16) for i in range(3)]
        Ab.append(reg(EbBase + 2 * K, K, BF16))
        o += 3 * K
        Ssum = [reg(o + i * K, K, BF16) for i in range(4)]
        o += 4 * K
        assert o <= CO, o

        def attn_pair(p):
            gen = hgrn_gen(p)

            def pull(k):
                for _ in range(k):
                    try:
                        next(gen)
                    except StopIteration:
                        return

            ws = p % 2
            if p == 0:
                A("pool", lambda e: e.dma_start(out=wq[0], in_=wsb_d[0]), writes=[("wq", 0)], dma_key=("wq", 0))
            if p + 1 < 8:
                nws = (p + 1) % 2
                A("pool", lambda e: e.dma_start(out=wq[nws], in_=wsb_d[p + 1]), writes=[("wq", nws)], dma_key=("wq", nws))
            tokp = tokV(p)
            kV = keyV(p)
            units = [(which, tt) for which in range(2) for tt in range(4)]

            def prep_mm(u):
                which, tt = units[u]
                rb = u % 2
                for kc in range(8):
                    mm(banks[rb][:, :], wq[ws][:, kc, which * 128:(which + 1) * 128], hT[:, kc, tt * 512:(tt + 1) * 512], kc == 0, kc == 7,
                       hTk(tt) + [("wq", ws)], [("ps", rb)])

            prep_mm(0)
            pull(1)
            for u in range(8):
                which, tt = units[u]
                ts = slice(tt * 512, (tt + 1) * 512)
                rb = u % 2
                sb_ = 2 + u % 2
                gc = (gq8 if which == 0 else ngk)[:, p:p + 1]
                act(d_sq, banks[rb][:, :], AF.Square, [("ps", rb)], ["dsq"])
                if u + 1 < 8:
                    prep_mm(u + 1)
                pull(1)
                mm(banks[sb_][:, :], blockones, d_sq, True, True, ["dsq"], [("ps", sb_)])
                pull(1)
                act(d_ln, banks[sb_][:, :], AF.Ln, [("ps", sb_)], ["dln"], bias=epsc, scale=1.0 / 64)
                pull(1)
                act(d_rs, d_ln, AF.Exp, ["dln"], ["drs"], scale=-0.5)
                pull(1)
                if which == 0:
                    stt_(qnT[:, ts], banks[rb][:, :], gc, d_rs, ALU.mult, ALU.mult, [("ps", rb), "drs"], [("qk", 0, tt)])
                else:
                    for hd in range(2):
                        ps_ = slice(hd * 64, (hd + 1) * 64)
                        stt_(nk[hd][ps_, ts], banks[rb][ps_, :], gc[ps_, :], d_rs[ps_, :], ALU.mult, ALU.mult, [("ps", rb), "drs"], [("qk", 1, tt)])
                pull(1)
            steps = []
            for qt in range(4):
                for head in range(2):
                    for kb in range(4 * qt + 3, -1, -1):
                        steps.append((head, qt, kb))
            n = len(steps)
            info = {}
            rs_slot = {}
            ws_slot = {}
            cur = 0
            for i_, (head_, qt_, kb_) in enumerate(steps):
                first_ = kb_ == 4 * qt_ + 3
                if first_:
                    cur = (cur + 1) % 4
                rs_slot[i_] = cur
                if kb_ - 4 * qt_ <= 0 and not first_:
                    cur = (cur + 1) % 4
                ws_slot[i_] = cur

            def s1(i):
                head, qt, kb = steps[i]
                d = kb - 4 * qt
                c0 = max(d, 0) * 128
                diag = d >= 0
                zb = i % 2
                sp_i = i % 4
                info[i] = (c0, diag, zb, sp_i)
                qkeys = [("qk", 0, qt), ("qk", 1, kb // 4)]
                lhs = nk[head][:, kb * 128:(kb + 1) * 128]
                rhs = qnT[:, qt * 512 + c0:(qt + 1) * 512]
                mm(banks[zb][:, c0:512], lhs, rhs, True, not diag, qkeys, [("ps", zb)])
                if diag:
                    mm(banks[zb][:, c0:c0 + 128], ident, posmask, False, True, [], [("ps", zb)])
                act(banks[zb][:, c0:512], banks[zb][:, c0:512], AF.Exp, [("ps", zb)], [("ps", zb)], scale=-1.0)
                act(SPb[sp_i][:, c0:512], banks[zb][:, c0:512], AF.Ln, [("ps", zb)], [("SP", sp_i)], bias=1.0)

            def s2(i):
                head, qt, kb = steps[i]
                c0, diag, zb, sp_i = info[i]
                first = kb == 4 * qt + 3
                cbk = 2 + i % 2
                ai = i % 4
                qkeys = [("qk", 0, qt), ("qk", 1, kb // 4)]
                lhs = nk[head][:, kb * 128:(kb + 1) * 128]
                rhs = qnT[:, qt * 512 + c0:(qt + 1) * 512]
                rs_, ws_ = rs_slot[i], ws_slot[i]
                mm(banks[cbk][:, c0:512], triincl, SPb[sp_i][:, c0:512], True, False, [("SP", sp_i)], [("ps", cbk)])
                if not first:
                    mm(banks[cbk][:, c0:512], ones, Ssum[rs_][:, c0:512], False, False, [("Ssum", rs_)], [("ps", cbk)])
                mm(banks[cbk][:, c0:512], lhs, rhs, False, not diag, qkeys, [("ps", cbk)])
                if diag:
                    mm(banks[cbk][:, c0:c0 + 128], ident, posmask, False, True, [], [("ps", cbk)])
                act(Ab[ai][:, c0:512], banks[cbk][:, c0:512], AF.Exp, [("ps", cbk)], [("A", ai)], scale=-1.0)
                if kb > 0:
                    if first:
                        A("pool", lambda e: e.memset(Ssum[ws_][:, 0:384], 0.0), writes=[("Ssum", ws_)])
                        A("pool", lambda e: e.tensor_copy(out=Ssum[ws_][:, 384:512], in_=SPb[sp_i][:, 384:512]), reads=[("SP", sp_i)], writes=[("Ssum", ws_)])
                    else:
                        tt_("dve", Ssum[ws_][:, c0:512], Ssum[rs_][:, c0:512], SPb[sp_i][:, c0:512], ALU.add, [("Ssum", rs_), ("SP", sp_i)], [("Ssum", ws_)])

            def s3(i):
                head, qt, kb = steps[i]
                c0, diag, zb, sp_i = info[i]
                hb = head * 64
                first = kb == 4 * qt + 3
                ai = i % 4
                okey = [("ps4", head)]
                if first:
                    mm(banks[4][hb:hb + 64, :], zeros[:, 0:64], qnT[:, 0:512], True, False, [("qk", 0, 0)], okey)
                mm(banks[4][hb:hb + 64, c0:512], tokp[:, kb, head * 64:(head + 1) * 64], Ab[ai][:, c0:512], False, kb == 0, [kV, ("A", ai)], okey)
                if kb == 0:
                    dst = osbT[hb:hb + 64, p, qt * 512:(qt + 1) * 512]
                    if qt % 2 == 0:
                        vcopy(dst, banks[4][hb:hb + 64, :], okey, [("osbc", p)])
                    else:
                        acopy(dst, banks[4][hb:hb + 64, :], okey, [("osbc", p)])

            for i in range(n + 4):
                if i < n:
                    s1(i)
                pull(1)
                if 0 <= i - 2 < n:
                    s2(i - 2)
                if i % 3 == 0:
                    pull(1)
                if 0 <= i - 4 < n:
                    s3(i - 4)
            for _ in gen:
                pass

        try:
            for p in range(8):
                attn_pair(p)
                ck(2.1 + p * 0.1)
        except _Stop:
            return finish()
        P.barrier()
        if "ohg" in dumps:
            A("sp", lambda e: e.dma_start(out=dumps["ohg"], in_=reg(32 * K, 32 * K, BF16)), dma_key="dump")
            P.barrier()
        if "osb" in dumps:
            A("sp", lambda e: e.dma_start(out=dumps["osb"], in_=reg(64 * K, 32 * K, BF16)), dma_key="dump")
            P.barrier()
        if upto <= 3:
            return finish()
        mixT = reg(SCR, 32 * K, BF16).rearrange("p (k t) -> p k t", t=S)
        wE = [reg(SCR + 32 * K + i * 8 * K, 8 * K, BF16).rearrange("p (k c) -> p k c", c=512) for i in range(2)]
        o = SCR + 48 * K
        e_sa = [reg(o + i * 2 * K, 2 * K, F32) for i in range(2)]
        e_m = [reg(o + 4 * K + i * 2 * K, 2 * K, F32) for i in range(2)]
        for nb in range(8):
            ws = nb % 2
            A("pool", lambda e, nb=nb, ws=ws: e.dma_start(out=wE[ws], in_=we_d[nb]), writes=[("wE", ws)], dma_key=("wE", ws))
            for tt in range(4):
                ts = slice(tt * 512, (tt + 1) * 512)
                for half, (gc0, yc0, src, skey) in enumerate(((0, 256, ohgT, "ohg"), (128, 384, osbT, "osb"))):
                    gb = (4 * half) % 8
                    yb = gb + 1
                    gb2 = gb + (tt % 2) * 2
                    yb2 = yb + (tt % 2) * 2
                    for kc in range(8):
                        A("pe", lambda e, kc=kc, gc0=gc0, gb2=gb2, ts=ts, ws=ws: e.matmul(banks[gb2][:, :], lhsT=wE[ws][:, kc, gc0:gc0 + 128], rhs=hT[:, kc, ts], start=(kc == 0), stop=(kc == 7)),
                          reads=[("wE", ws)], writes=[("ps", gb2)])
                    A("act", lambda e, half=half, gb2=gb2: e.activation(out=e_sa[half], in_=banks[gb2][:, :], func=AF.Sigmoid), reads=[("ps", gb2)], writes=[("sa", half)])
                    for kc in range(8):
                        A("pe", lambda e, kc=kc, yc0=yc0, yb2=yb2, ts=ts, src=src, ws=ws: e.matmul(banks[yb2][:, :], lhsT=wE[ws][:, kc, yc0:yc0 + 128], rhs=src[:, kc, ts], start=(kc == 0), stop=(kc == 7)),
                          reads=[("wE", ws)], writes=[("ps", yb2)])
                    A("dve", lambda e, half=half, yb2=yb2: e.tensor_tensor(out=e_m[half], in0=banks[yb2][:, :], in1=e_sa[half], op=ALU.mult), reads=[("ps", yb2), ("sa", half)], writes=[("m", half)])
                A("pool", lambda e, nb=nb, ts=ts: e.tensor_tensor(out=mixT[:, nb, ts], in0=e_m[0], in1=e_m[1], op=ALU.add), reads=[("m", 0), ("m", 1)], writes=[("mix", nb, tt)])
        P.barrier()
        if "mix" in dumps:
            A("sp", lambda e: e.dma_start(out=dumps["mix"], in_=reg(SCR, 32 * K, BF16)), dma_key="dump")
            P.barrier()

        if upto <= 4:
            return finish()
        wfo = reg(0, 44 * K, BF16).rearrange("p (j c) -> p j c", c=1024)
        wo = reg(44 * K, 16 * K, BF16).rearrange("p (k c) -> p k c", c=1024)
        actT = reg(60 * K, 22 * K, BF16).rearrange("p (j t) -> p j t", t=512)
        o = SCR + 32 * K
        x2t = reg(o, 16 * K, F32).rearrange("p (b c) -> p b c", c=1024)
        o += 16 * K
        h2Tt = reg(o, 8 * K, BF16).rearrange("p (k t) -> p k t", t=512)
        o += 8 * K
        wf = [reg(o + i * 4 * K, 4 * K, BF16).rearrange("p (k c) -> p k c", c=256) for i in range(3)]
        o += 12 * K
        xr = [reg(o + i * 4 * K, 4 * K, F32) for i in range(2)]
        o += 8 * K
        xn2 = [reg(o + i * 2 * K, 2 * K, BF16) for i in range(2)]
        o += 4 * K
        ost = [reg(o + i * 4 * K, 4 * K, F32) for i in range(2)]
        o += 8 * K
        assert o <= CO, o
        junk2 = reg(82 * K, 2 * K, BF16)
        g_sg = [reg(84 * K + i * 2 * K, 2 * K, F32) for i in range(2)]

        for cg in range(2):
            A("pool", lambda e, cg=cg: e.dma_start(out=wo[:, :, cg * 512:(cg + 1) * 512], in_=wo_d[cg]), writes=[("wo", cg)], dma_key=("wo", cg))
        wfo_keys = [("wfo", jj) for jj in range(0, NJ, 2)]
        wfo_todo = list(range(0, NJ, 2))

        def wfo_dma():
            if wfo_todo:
                jj = wfo_todo.pop(0)
                A("pool", lambda e: e.dma_start(out=wfo[:, jj:jj + 2, :], in_=wfo_d[:, jj:jj + 2, :]), writes=[("wfo", jj)], dma_key="wfo")

        x2b1 = [reg(88 * K, 8 * K, F32).rearrange("p (b c) -> p b c", c=1024),
                reg(198 * K, 8 * K, F32).rearrange("p (b c) -> p b c", c=1024)]

        def x2(s_, tbl):
            if s_ == 0:
                return x2t[:, tbl, :]
            return x2b1[tbl // 2][:, tbl % 2, :]

        wfc = [0]

        def Fmm(tt, tbl):
            tb = tt * 4 + tbl
            xs = tb % 2
            s_ = tt % 2
            A("sp", lambda e: e.dma_start(out=xr[xs], in_=x[tb * 128:(tb + 1) * 128, :]), writes=[("xr", xs)], dma_key=("xr", xs))
            for cg in range(2):
                bk = (tbl * 2 + cg) % 4
                cs = slice(cg * 512, (cg + 1) * 512)
                for kc in range(8):
                    mm(banks[bk][:, :], mixT[:, kc, tb * 128:(tb + 1) * 128], wo[:, kc, cs], kc == 0, kc == 7, [("wo", cg)], [("ps", bk)])
                tt_("dve", x2(s_, tbl)[:, cs], banks[bk][:, :], xr[xs][:, cs], ALU.add, [("ps", bk), ("xr", xs)], [("x2", s_, tbl, cg)])

        def Frms(tt, tbl):
            tb = tt * 4 + tbl
            xs = tb % 2
            s_ = tt % 2
            rms_rows(x2(s_, tbl), ss2[:, tb:tb + 1], ln2[:, tb:tb + 1], rstd2[:, tb:tb + 1], g2bc, xn2[xs], junk2,
                     [("x2", s_, tbl, 0), ("x2", s_, tbl, 1)], [("xn2", xs)])

        def Ftr(tt, tbl):
            tb = tt * 4 + tbl
            xs = tb % 2
            bk = 4 + tbl % 2
            for kc in range(8):
                tr(banksb[bk][:, kc * 128:(kc + 1) * 128], xn2[xs][:, kc * 128:(kc + 1) * 128], [("xn2", xs)], [("ps", bk)])
            acopy(h2Tt[:, :, tbl * 128:(tbl + 1) * 128], banksb[bk][:, :].rearrange("p (k t) -> p k t", t=128), [("ps", bk)], [("h2", tbl)])

        h2k = [("h2", i) for i in range(4)]
        actk = [("act", j) for j in range(NJ)]

        def Gin(tt):
            for j in range(NJ):
                ws = wfc[0] % 3
                wfc[0] += 1
                A("pool", lambda e, j=j, ws=ws: e.dma_start(out=wf[ws], in_=wfi_d[j]), writes=[("wf", ws)], dma_key=("wf", ws))
                par = j % 2
                gb_ = (0, 2)[par]
                ub_ = (1, 3)[par]
                for kc in range(8):
                    mm(banks[gb_][:, :], wf[ws][:, kc, 0:128], h2Tt[:, kc, :], kc == 0, kc == 7, h2k + [("wf", ws)], [("ps", gb_)])
                for kc in range(8):
                    mm(banks[ub_][:, :], wf[ws][:, kc, 128:256], h2Tt[:, kc, :], kc == 0, kc == 7, h2k + [("wf", ws)], [("ps", ub_)])
                act(g_sg[par], banks[gb_][:, :], AF.Silu, [("ps", gb_)], [("gsg", par)])
                tt_("dve", actT[:, j, :], banks[ub_][:, :], g_sg[par], ALU.mult, [("ps", ub_), ("gsg", par)], [("act", j)])
                if j % 2 == 1:
                    wfo_dma()

        def Gout(tt, tbl):
            tb = tt * 4 + tbl
            os_ = tb % 2
            s_ = tt % 2
            for cg in range(2):
                bk = 4 + (tbl * 2 + cg) % 4
                cs = slice(cg * 512, (cg + 1) * 512)
                for j in range(NJ):
                    mm(banks[bk][:, :], actT[:, j, tbl * 128:(tbl + 1) * 128], wfo[:, j, cs], j == 0, j == NJ - 1, actk + wfo_keys, [("ps", bk)])
                tt_("dve", ost[os_][:, cs], banks[bk][:, :], x2(s_, tbl)[:, cs], ALU.add, [("ps", bk), ("x2", s_, tbl, cg)], [("ost", os_)])
            A("sp", lambda e: e.dma_start(out=out[tb * 128:(tb + 1) * 128, :], in_=ost[os_]), reads=[("ost", os_)], writes=[("out", tb)], dma_key=("ost", os_))

        for tbl in range(4):
            Fmm(0, tbl)
        Frms(0, 0)
        Frms(0, 1)
        Ftr(0, 0)
        Frms(0, 2)
        Ftr(0, 1)
        Frms(0, 3)
        Ftr(0, 2)
        Ftr(0, 3)
        for tt in range(4):
            Gin(tt)
            n_ = tt + 1
            if n_ < 4:
                for tbl in range(4):
                    Fmm(n_, tbl)
                Frms(n_, 0)
                Frms(n_, 1)
                Gout(tt, 0)
                Ftr(n_, 0)
                Ftr(n_, 1)
                Frms(n_, 2)
                Frms(n_, 3)
                Gout(tt, 1)
                Gout(tt, 2)
                Ftr(n_, 2)
                Ftr(n_, 3)
                Gout(tt, 3)
            else:
                for tbl in range(4):
                    Gout(tt, tbl)
        P.barrier()
        A("sp", lambda e: e.nop())
        P.emit()
    return nc


def _blk(w, cols):
    return np.ascontiguousarray(w[:, cols].reshape(8, 128, -1).transpose(1, 0, 2))


def _prep_weights(w_in, w_hg_out, w_sb_out, w_o, w_ffn_in, w_ffn_out):
    ar = np.arange
    wiv = np.stack([_blk(w_in, ar(2048 + g * 512, 2048 + (g + 1) * 512)) for g in range(2)] +
                   [_blk(w_in, ar(6144 + g * 512, 6144 + (g + 1) * 512)) for g in range(2)])
    whg = np.stack([_blk(w_in, np.concatenate([ar(h * 128, (h + 1) * 128), 1024 + ar(h * 128, (h + 1) * 128), 3072 + ar(h * 128, (h + 1) * 128)])) for h in range(8)])
    wsb = np.stack([_blk(w_in, np.concatenate([4096 + ar(p * 128, (p + 1) * 128), 5120 + ar(p * 128, (p + 1) * 128)])) for p in range(8)])
    we = np.stack([np.concatenate([_blk(w_in, np.concatenate([7168 + ar(nb * 128, (nb + 1) * 128), 8192 + ar(nb * 128, (nb + 1) * 128)])),
                                   _blk(w_hg_out, ar(nb * 128, (nb + 1) * 128)), _blk(w_sb_out, ar(nb * 128, (nb + 1) * 128))], axis=2) for nb in range(8)])
    wo = np.stack([_blk(w_o, ar(cg * 512, (cg + 1) * 512)) for cg in range(2)])
    wfi = np.stack([_blk(w_ffn_in, np.concatenate([ar(j * 128, (j + 1) * 128), FF + ar(j * 128, (j + 1) * 128)])) for j in range(NJ)])
    wfo = np.ascontiguousarray(w_ffn_out.reshape(NJ, 128, 1024).transpose(1, 0, 2))
    return dict(wiv=wiv, whg=whg, wsb=wsb, we=we, wo=wo, wfi=wfi, wfo=wfo)


def _consts(norm1_gain, norm2_gain, lb_logits, hg_out_norm, sb_q_norm, sb_k_norm):
    cf = np.zeros((128, CF_N), np.float32)
    cf[:, CF_G1:CF_G1 + 1024] = norm1_gain.reshape(1, 1024)
    cf[:, CF_G2:CF_G2 + 1024] = norm2_gain.reshape(1, 1024)
    t = np.arange(512)
    cf[:, CF_RST:CF_RST + 512] = (t % 64 != 0).astype(np.float32)[None, :]
    pp = np.arange(128)[:, None]
    tq = np.arange(64)[None, :]
    cf[:, CF_M64:CF_M64 + 64] = ((pp < 64) & (pp <= tq)).astype(np.float32)
    cf[:, CF_M64 + 64:CF_M64 + 128] = ((pp >= 64) & (pp - 64 <= tq)).astype(np.float32)
    cf[:, CF_LBL:CF_LBL + 16] = lb_logits.reshape(2, 8, 128).transpose(2, 0, 1).reshape(128, 16)
    cf[:, CF_GQ:CF_GQ + 8] = sb_q_norm.reshape(8, 128).T
    cf[:, CF_GK:CF_GK + 8] = sb_k_norm.reshape(8, 128).T
    cf[:, CF_GHG:CF_GHG + 8] = hg_out_norm.reshape(8, 128).T
    cf[:, CF_EPS] = EPS
    cf[:, CF_ONE] = 1.0
    cb = np.zeros((128, CB_N), np.float32)
    j = np.arange(128)[:, None]
    s = np.arange(128)[None, :]
    cb[:, 0:128] = (j == s)
    cb[:, 128:256] = (j >= s)
    cb[:, 256:384] = 1.0
    cb[:, 384:512] = (j // 64 == s // 64)
    cb[:, 512:640] = BIG * (j >= s)
    return cf, cb.astype(ml_dtypes.bfloat16)


_NC_CACHE = {}


def kernel(x, norm1_gain, w_in, lb_logits, hg_out_norm, sb_q_norm, sb_k_norm,
           w_hg_out, w_sb_out, w_o, norm2_gain, w_ffn_in, w_ffn_out, _dump=None, _upto=99, _trace=False):
    f = lambda a: np.asarray(a, dtype=np.float32)
    x = f(x)
    wd = _prep_weights(f(w_in)[0], f(w_hg_out)[0], f(w_sb_out)[0], f(w_o)[0], f(w_ffn_in)[0], f(w_ffn_out)[0])
    cf, cb = _consts(f(norm1_gain), f(norm2_gain), f(lb_logits), f(hg_out_norm), f(sb_q_norm), f(sb_k_norm))
    key = (tuple(_dump or ()), _upto)
    if key not in _NC_CACHE:
        _NC_CACHE[key] = build_nc(_dump, _upto)
    nc = _NC_CACHE[key]
    in_maps = []
    for b in range(8):
        m = dict(wd)
        m["x"] = np.ascontiguousarray(x[b])
        m["cf"] = cf
        m["cb"] = cb
        in_maps.append(m)
    if _trace:
        res = run_bass_kernel_spmd(nc, in_maps, core_ids=list(range(8)), trace=True)
        print('exec_time_ns', res.exec_time_ns)
    else:
        res = run_bass_kernel_spmd(nc, in_maps, core_ids=list(range(8)))
    outp = np.stack([res.results[b]["out"] for b in range(8)]).astype(np.float32)
    if _dump:
        return outp, res.results
    return outp
```

```python
import contextlib
import numpy as np
import ml_dtypes
import concourse.bass as bass
import concourse.mybir as mybir
from concourse.bass_utils import run_bass_kernel_spmd

F32 = mybir.dt.float32
BF16 = mybir.dt.bfloat16
AF = mybir.ActivationFunctionType
ALU = mybir.AluOpType

ENGS = ("sp", "act", "dve", "pool", "pe")
S = 2048
D = 1024
FF = 2816
NJ = 22
EPS = 1e-6
BIG = 30000.0


class _Stop(Exception):
    pass


class Op:
    __slots__ = ("eng", "fn", "deps", "signal", "sigval", "dma", "dsem", "dval")

    def __init__(self, eng, fn, dma=False):
        self.eng = eng
        self.fn = fn
        self.deps = []
        self.signal = False
        self.sigval = 0
        self.dma = dma
        self.dsem = None
        self.dval = 0


class Prog:
    def __init__(self, nc):
        self.nc = nc
        self.ops = {e: [] for e in ENGS}
        self.last_w = {}
        self.readers = {}
        self.dma_keys = {}
        self.barrier_deps = {e: [] for e in ENGS}
        self.dma_since_barrier = []

    @staticmethod
    def _need(d, op, raw):
        if d is op:
            return False
        if d.dma or op.dma:
            return True
        if d.eng == op.eng:
            if op.eng == "pe":
                return False
            return raw
        return True

    def add(self, eng, fn, reads=(), writes=(), dma_key=None):
        op = Op(eng, fn, dma=dma_key is not None)
        deps = {}
        for k in reads:
            w = self.last_w.get(k)
            if w is not None:
                deps[id(w)] = (w, True)
        for k in writes:
            w = self.last_w.get(k)
            if w is not None and id(w) not in deps:
                deps[id(w)] = (w, False)
            for r in self.readers.get(k, ()):
                if id(r) not in deps:
                    deps[id(r)] = (r, False)
        for d in self.barrier_deps[eng]:
            if id(d) not in deps:
                deps[id(d)] = (d, True)
        self.barrier_deps[eng] = []
        for d, raw in deps.values():
            if self._need(d, op, raw):
                op.deps.append(d)
                d.signal = True
        for k in reads:
            self.readers.setdefault(k, []).append(op)
        for k in writes:
            self.last_w[k] = op
            self.readers[k] = []
        if op.dma:
            cnt = self.dma_keys.setdefault(dma_key, [0])
            cnt[0] += 16
            op.dsem = dma_key
            op.dval = cnt[0]
            self.dma_since_barrier.append(op)
        self.ops[eng].append(op)
        return op

    def barrier(self):
        lasts = [self.ops[e][-1] for e in ENGS if self.ops[e]]
        lasts += self.dma_since_barrier
        self.dma_since_barrier = []
        for e in ENGS:
            self.barrier_deps[e] = list(lasts)
        self.last_w = {}
        self.readers = {}

    def emit(self):
        nc = self.nc
        for e in ENGS:
            cnt = 0
            for op in self.ops[e]:
                if op.signal and not op.dma:
                    cnt += 1
                    op.sigval = cnt
        with contextlib.ExitStack() as st:
            esem = {e: st.enter_context(nc.semaphore("s_" + e)) for e in ENGS}
            dsem = {k: st.enter_context(nc.semaphore("d_%d" % i))
                    for i, k in enumerate(self.dma_keys)}
            block = st.enter_context(nc.Block())
            prog = self

            def run(engname, engine):
                waited = {}
                for op in prog.ops[engname]:
                    need = {}
                    for d in op.deps:
                        if d.dma:
                            s, v = dsem[d.dsem], d.dval
                        else:
                            s, v = esem[d.eng], d.sigval
                        if need.get(id(s), (None, 0))[1] < v:
                            need[id(s)] = (s, v)
                    for key, (s, v) in need.items():
                        if waited.get(key, 0) >= v:
                            continue
                        waited[key] = v
                        engine.wait_ge(s, v)
                    inst = op.fn(engine)
                    if op.dma:
                        inst.then_inc(dsem[op.dsem], 16)
                    elif op.signal:
                        inst.then_inc(esem[engname], 1)

            @block.sync
            def _(eng):
                run("sp", eng)

            @block.scalar
            def _(eng):
                run("act", eng)

            @block.vector
            def _(eng):
                run("dve", eng)

            @block.gpsimd
            def _(eng):
                run("pool", eng)

            @block.tensor
            def _(eng):
                run("pe", eng)


CF_G1 = 0
CF_G2 = 1024
CF_RST = 2048
CF_M64 = 2560
CF_LBL = 2688
CF_GQ = 2704
CF_GK = 2712
CF_GHG = 2720
CF_EPS = 2728
CF_ONE = 2729
CF_N = 2736
CB_N = 6 * 128


def build_nc(dump=None, upto=99):
    dump = dump or ()
    nc = bass.Bass("TRN2", target_bir_lowering=False)
    dram = lambda n, s, dt=F32: nc.dram_tensor(n, s, dt, kind="ExternalInput").ap()
    x = dram("x", [S, D])
    cf_d = dram("cf", [128, CF_N])
    cb_d = dram("cb", [128, CB_N], BF16)
    wiv_d = dram("wiv", [4, 128, 8, 512])
    whg_d = dram("whg", [8, 128, 8, 384])
    wsb_d = dram("wsb", [8, 128, 8, 256])
    we_d = dram("we", [8, 128, 8, 512])
    wo_d = dram("wo", [2, 128, 8, 512])
    wfi_d = dram("wfi", [NJ, 128, 8, 256])
    wfo_d = dram("wfo", [128, NJ, 1024])
    out = nc.dram_tensor("out", [S, D], F32, kind="ExternalOutput").ap()
    dumps = {}
    for name in dump:
        shp = {"hT": [128, 8 * S], "ohg": [128, 8 * S], "osb": [128, 8 * S],
               "mix": [128, 8 * S]}[name]
        dumps[name] = nc.dram_tensor("dump_" + name, shp, BF16, kind="ExternalOutput").ap()

    with contextlib.ExitStack() as st:
        ARENA = 206 * 1024
        arena = st.enter_context(nc.sbuf_tensor("arena", [128, ARENA // 2], BF16))
        banks = [st.enter_context(nc.psum_tensor("ps%d" % i, [128, 512], F32)) for i in range(8)]
        banksb = [b.bitcast(BF16) for b in banks]

        def reg(off, nbytes, dt):
            assert off % 4 == 0 and off + nbytes <= ARENA, (off, nbytes)
            a = arena[:, off // 2:(off + nbytes) // 2]
            return a if dt == BF16 else a.bitcast(dt)

        K = 1024
        P = Prog(nc)
        A = P.add

        def ck(v):
            if upto <= v:
                raise _Stop()

        def finish():
            P.barrier()
            A("sp", lambda e: e.nop())
            P.emit()
            return nc
        cnt = [0]

        def alt(n=2):
            cnt[0] += 1
            return cnt[0] % n

        hT = reg(0, 32 * K, BF16).rearrange("p (k t) -> p k t", t=S)
        ohgT = reg(32 * K, 32 * K, BF16).rearrange("p (k t) -> p k t", t=S)
        osbT = reg(64 * K, 32 * K, BF16).rearrange("p (k t) -> p k t", t=S)
        CO = 184 * K
        cf = reg(CO, CF_N * 4, F32)
        cb = reg(CO + CF_N * 4, CB_N * 2, BF16)
        small = reg(CO + CF_N * 4 + CB_N * 2, 1024, F32)
        ident = cb[:, 0:128]
        triincl = cb[:, 128:256]
        ones = cb[:, 256:384]
        blockones = cb[:, 384:512]
        posmask = cb[:, 512:640]
        zeros = cb[:, 640:768]
        g1bc = cf[:, CF_G1:CF_G1 + 1024]
        g2bc = cf[:, CF_G2:CF_G2 + 1024]
        rstm = cf[:, CF_RST:CF_RST + 512]
        m64 = cf[:, CF_M64:CF_M64 + 128].rearrange("p (a t) -> p a t", t=64)
        epsc = cf[:, CF_EPS:CF_EPS + 1]
        onec = cf[:, CF_ONE:CF_ONE + 1]
        ss1 = small[:, 0:16]
        ln1 = small[:, 16:32]
        rstd1 = small[:, 32:48]
        lb = small[:, 48:56]
        oml = small[:, 56:64]
        gq8 = small[:, 64:72]
        ngk = small[:, 72:80]
        tmp8 = small[:, 80:88]
        ss2 = small[:, 96:112]
        ln2 = small[:, 112:128]
        rstd2 = small[:, 128:144]
        el = small[:, 144:176]

        A("sp", lambda e: e.dma_start(out=cf, in_=cf_d), writes=["cf"], dma_key="cf")
        A("sp", lambda e: e.dma_start(out=cb, in_=cb_d), writes=["cb"], dma_key="cb")
        A("dve", lambda e: e.tensor_tensor(out=tmp8, in0=cf[:, CF_LBL + 8:CF_LBL + 16], in1=cf[:, CF_LBL:CF_LBL + 8], op=ALU.subtract), reads=["cf"], writes=["tmp8"])
        A("act", lambda e: e.activation(out=tmp8, in_=tmp8, func=AF.Exp), reads=["tmp8"], writes=["tmp8"])
        A("dve", lambda e: e.tensor_scalar(out=tmp8, in0=tmp8, scalar1=1.0, scalar2=None, op0=ALU.add), reads=["tmp8"], writes=["tmp8"])
        A("dve", lambda e: e.reciprocal(out=lb, in_=tmp8), reads=["tmp8"], writes=["lb"])
        A("dve", lambda e: e.tensor_scalar(out=oml, in0=lb, scalar1=-1.0, scalar2=1.0, op0=ALU.mult, op1=ALU.add), reads=["lb"], writes=["oml"])
        A("dve", lambda e: e.tensor_scalar(out=gq8, in0=cf[:, CF_GQ:CF_GQ + 8], scalar1=0.125, scalar2=None, op0=ALU.mult), reads=["cf"], writes=["gq8"])
        A("dve", lambda e: e.tensor_scalar(out=ngk, in0=cf[:, CF_GK:CF_GK + 8], scalar1=-1.0, scalar2=None, op0=ALU.mult), reads=["cf"], writes=["ngk"])
        P.barrier()

        SCR = 96 * K

        def rms_rows(src, sscol, lncol, rstdcol, gbc, xn_out, junk, kin, kout):
            A("act", lambda e: e.activation(out=junk, in_=src, func=AF.Square, accum_out=sscol),
              reads=kin, writes=["junk", ("st", id(sscol))])
            A("act", lambda e: e.activation(out=lncol, in_=sscol, func=AF.Ln, bias=epsc, scale=1.0 / D),
              reads=[("st", id(sscol))], writes=[("st", id(lncol))])
            A("act", lambda e: e.activation(out=rstdcol, in_=lncol, func=AF.Exp, scale=-0.5),
              reads=[("st", id(lncol))], writes=[("st", id(rstdcol))])
            A("dve", lambda e: e.scalar_tensor_tensor(out=xn_out, in0=src, scalar=rstdcol, in1=gbc, op0=ALU.mult, op1=ALU.mult),
              reads=kin + [("st", id(rstdcol))], writes=kout)

        AS = SCR + 48 * K
        xt = [reg(AS + i * 4 * K, 4 * K, F32) for i in range(3)]
        xn = [reg(AS + 12 * K + i * 2 * K, 2 * K, BF16) for i in range(2)]
        junk = reg(AS + 16 * K, 2 * K, BF16)
        for tb in range(16):
            s3 = tb % 3
            s2 = tb % 2
            bk = tb % 2
            A("sp", lambda e, tb=tb, s3=s3: e.dma_start(out=xt[s3], in_=x[tb * 128:(tb + 1) * 128, :]),
              writes=[("xt", s3)], dma_key=("xt", s3))
            rms_rows(xt[s3], ss1[:, tb:tb + 1], ln1[:, tb:tb + 1], rstd1[:, tb:tb + 1], g1bc, xn[s2], junk,
                     [("xt", s3)], [("xn", s2)])
            for kc in range(8):
                A("pe", lambda e, kc=kc, s2=s2, bk=bk: e.transpose(out=banksb[bk][:, kc * 128:(kc + 1) * 128], in_=xn[s2][:, kc * 128:(kc + 1) * 128], identity=ident),
                  reads=[("xn", s2)], writes=[("ps", bk)])
            src = banksb[bk][:, :].rearrange("p (k t) -> p k t", t=128)
            dst = hT[:, :, tb * 128:(tb + 1) * 128]
            if tb % 2 == 0:
                A("act", lambda e, src=src, dst=dst: e.copy(out=dst, in_=src), reads=[("ps", bk)], writes=[("hT", tb)])
            else:
                A("dve", lambda e, src=src, dst=dst: e.tensor_copy(out=dst, in_=src), reads=[("ps", bk)], writes=[("hT", tb)])
        if "hT" in dumps or upto <= 1:
            P.barrier()
        if "hT" in dumps:
            A("sp", lambda e: e.dma_start(out=dumps["hT"], in_=reg(0, 32 * K, BF16)), dma_key="dump")
            P.barrier()

        if upto <= 1:
            return finish()
        hTk = lambda tt: [("hT", tt * 4 + i) for i in range(4)]

        def mm(o_, lhsT, rhs, start, stop, reads, writes):
            A("pe", lambda e: e.matmul(o_, lhsT=lhsT, rhs=rhs, start=start, stop=stop), reads=reads, writes=writes)

        def tr(o_, in_, reads, writes):
            A("pe", lambda e: e.transpose(out=o_, in_=in_, identity=ident), reads=reads, writes=writes)

        def act(o_, in_, func, reads, writes, **kw):
            A("act", lambda e: e.activation(out=o_, in_=in_, func=func, **kw), reads=reads, writes=writes)

        def acopy(o_, in_, reads, writes):
            A("act", lambda e: e.copy(out=o_, in_=in_), reads=reads, writes=writes)

        def vcopy(o_, in_, reads, writes):
            A("dve", lambda e: e.tensor_copy(out=o_, in_=in_), reads=reads, writes=writes)

        def tt_(eng, o_, in0, in1, op, reads, writes):
            A(eng, lambda e: e.tensor_tensor(out=o_, in0=in0, in1=in1, op=op), reads=reads, writes=writes)

        def ts_(eng, o_, in0, s1, s2, op0, op1, reads, writes):
            if s2 is None:
                A(eng, lambda e: e.tensor_scalar(out=o_, in0=in0, scalar1=s1, scalar2=None, op0=op0), reads=reads, writes=writes)
            else:
                A(eng, lambda e: e.tensor_scalar(out=o_, in0=in0, scalar1=s1, scalar2=s2, op0=op0, op1=op1), reads=reads, writes=writes)

        def stt_(o_, in0, sc_, in1, op0, op1, reads, writes):
            A("dve", lambda e: e.scalar_tensor_tensor(out=o_, in0=in0, scalar=sc_, in1=in1, op0=op0, op1=op1), reads=reads, writes=writes)

        XI = SCR
        XV = SCR + 4 * K
        wblk = reg(SCR + 8 * K, 8 * K, BF16).rearrange("p (k c) -> p k c", c=512)

        def tokI(h):
            base = 64 * K + (h + 1) * 4 * K if h < 7 else XI
            return reg(base, 4 * K, BF16).rearrange("p (b c) -> p b c", c=128)

        def tokV(p):
            base = 32 * K + (p + 1) * 4 * K if p < 7 else XV
            return reg(base, 4 * K, BF16).rearrange("p (b c) -> p b c", c=128)

        keyI = lambda h: ("osbc", h + 1) if h < 7 else ("XI",)
        keyV = lambda p: ("ohgc", p + 1) if p < 7 else ("XV",)

        for gidx in range(4):
            A("pool", lambda e, gidx=gidx: e.dma_start(out=wblk, in_=wiv_d[gidx]), writes=["wblk"], dma_key="wblk")
            for tb in range(16):
                bk = tb % 2
                for kc in range(8):
                    mm(banks[bk][:, :], hT[:, kc, tb * 128:(tb + 1) * 128], wblk[:, kc, :], kc == 0, kc == 7,
                       [("hT", tb), "wblk"], [("ps", bk)])
                for j in range(4):
                    u = 4 * (gidx % 2) + j
                    dst = (tokI(u) if gidx < 2 else tokV(u))[:, tb, :]
                    kk_ = keyI(u) if gidx < 2 else keyV(u)
                    acopy(dst, banks[bk][:, j * 128:(j + 1) * 128], [("ps", bk)], [kk_])
        P.barrier()
        if upto <= 2:
            return finish()

        PC = SCR + 8 * K
        b_ef, b_fp, b_g, b_kk, b_cum, b_dm, b_ep, b_em, b_sg = [reg(PC + i * 2 * K, 2 * K, F32) for i in range(9)]
        b_ln, b_rs, b_o1, b_eg = b_ef, b_fp, b_g, b_cum
        KEF, KFP, KG, KKK, KCUM, KDM, KEP, KEM, KSG = ["cb%d" % i for i in range(9)]
        o = PC + 18 * K
        wh = [reg(o + i * 6 * K, 6 * K, BF16).rearrange("p (k c) -> p k c", c=384) for i in range(2)]
        o += 12 * K
        qtl = reg(o, K, BF16)
        ktl = reg(o + K, K, BF16)
        ktokE = reg(o + 2 * K, K, BF16).rearrange("p (b d) -> p b d", d=128)
        ktokO = reg(o + 3 * K, K, BF16).rearrange("p (b d) -> p b d", d=128)
        A("pool", lambda e: e.memset(reg(PC + 30 * K + 2 * K, 2 * K, BF16), 0.0), writes=["ktokE", "ktokO"])
        o += 4 * K
        scm = [reg(o + i * K, K, BF16).rearrange("p (a t) -> p a t", t=64) for i in range(2)]
        o += 2 * K
        Sst = [reg(o + i * 512, 512, F32) for i in range(2)]
        o += K
        Pb = [reg(o + i * 256, 256, BF16) for i in range(4)]
        o += K
        sqb = reg(o, K, BF16)
        o += K
        assert o == PC + 39 * K
        BCP, BCS, BCO = 5, 6, 7

        def hgrn_gen(h):
            ws = h % 2
            if h == 0:
                A("pool", lambda e: e.dma_start(out=wh[0], in_=whg_d[0]), writes=[("wh", 0)], dma_key=("wh", 0))
            if h + 1 < 8:
                nws = (h + 1) % 2
                A("pool", lambda e: e.dma_start(out=wh[nws], in_=whg_d[h + 1]), writes=[("wh", nws)], dma_key=("wh", nws))
            lbc = lb[:, h:h + 1]
            omlc = oml[:, h:h + 1]
            gcol = cf[:, CF_GHG + h:CF_GHG + h + 1]
            tokh = tokI(h)
            kI = keyI(h)
            PSP, PSS, PSO = [("ps", BCP)], [("ps", BCS)], [("ps", BCO)]
            cp, cs_, co = banks[BCP], banks[BCS], banks[BCO]
            cpb = banksb[BCP]

            def proj(c0, tt):
                for kc in range(8):
                    mm(cp[:, :], wh[ws][:, kc, c0:c0 + 128], hT[:, kc, tt * 512:(tt + 1) * 512], kc == 0, kc == 7,
                       hTk(tt) + [("wh", ws)], PSP)

            for tt in range(4):
                ts = slice(tt * 512, (tt + 1) * 512)
                proj(128, tt)
                yield
                act(b_ef, cp[:, :], AF.Exp, PSP, [KEF], scale=-1.0)
                yield
                ts_("dve", b_ef, b_ef, 1.0, None, ALU.add, None, [KEF], [KEF])
                yield
                A("dve", lambda e: e.reciprocal(out=b_fp, in_=b_ef), reads=[KEF], writes=[KFP])
                yield
                ts_("dve", b_fp, b_fp, omlc, lbc, ALU.mult, ALU.add, [KFP], [KFP])
                proj(0, tt)
                yield
                act(b_g, b_fp, AF.Ln, [KFP], [KG])
                ts_("pool", b_kk, b_fp, -1.0, 1.0, ALU.mult, ALU.add, [KFP], [KKK])
                yield
                A("dve", lambda e: e.tensor_tensor_scan(out=b_cum, data0=rstm, data1=b_g, initial=0.0, op0=ALU.mult, op1=ALU.add), reads=[KG], writes=[KCUM])
                yield
                cum3 = b_cum.rearrange("p (c t) -> p c t", t=64)
                last = cum3[:, :, 63:64]
                tt_("dve", b_dm.rearrange("p (c t) -> p c t", t=64), cum3, last.broadcast_to([128, 8, 64]), ALU.subtract, [KCUM], [KDM])
                act(el[:, tt * 8:(tt + 1) * 8].unsqueeze(2), last, AF.Exp, [KCUM], [("el", tt)])
                yield
                act(b_ep, b_dm, AF.Exp, [KDM], [KEP])
                act(b_em, b_dm, AF.Exp, [KDM], [KEM], scale=-1.0)
                yield
                tt_("dve", qtl, cp[:, :], b_ep, ALU.mult, PSP + [KEP], ["qtl"])
                yield
                tt_("dve", ktl, b_kk, b_em, ALU.mult, [KKK, KEM], ["ktl"])
                proj(256, tt)
                yield
                act(b_eg, cp[:, :], AF.Exp, PSP, [KCUM], scale=-1.0)
                yield
                ts_("dve", b_eg, b_eg, 1.0, None, ALU.add, None, [KCUM], [KCUM])
                for i in range(4):
                    tr(cpb[:, i * 128:(i + 1) * 128], ktl[:, i * 128:(i + 1) * 128], ["ktl"], PSP)
                yield
                A("dve", lambda e: e.reciprocal(out=b_sg, in_=b_eg), reads=[KCUM], writes=[KSG])
                acopy(ktokE[0:64, :, :], cpb[0:64, 0:512].rearrange("p (b d) -> p b d", d=128), PSP, ["ktokE"])
                yield
                vcopy(ktokO[64:128, :, :], cpb[64:128, 0:512].rearrange("p (b d) -> p b d", d=128), PSP, ["ktokO"])
                sc = alt()
                for c in range(8):
                    mm(cs_[:, c * 64:(c + 1) * 64], ktl[:, (c // 2) * 128:(c // 2 + 1) * 128], qtl[:, c * 64:(c + 1) * 64], True, True,
                       ["ktl", "qtl"], PSS)
                yield
                tt_("dve", scm[sc].rearrange("p (a b) t -> p a b t", b=2), cs_[:, :].rearrange("p (a b t) -> p a b t", b=2, t=64),
                    m64.unsqueeze(1).broadcast_to([128, 4, 2, 64]), ALU.mult, PSS, [("scm", sc)])
                yield
                for half in range(2):
                    for c in range(half * 4, half * 4 + 4):
                        cg = tt * 8 + c
                        tb = cg // 2
                        kt_ = ktokE if cg % 2 == 0 else ktokO
                        mm(cs_[:, (c % 4) * 128:(c % 4 + 1) * 128], kt_[:, c // 2, :], tokh[:, tb, :], True, True,
                           ["ktokE", "ktokO", kI], PSS)
                    yield
                    for c in range(half * 4, half * 4 + 4):
                        cg = tt * 8 + c
                        tb = cg // 2
                        usrc = cs_[:, (c % 4) * 128:(c % 4 + 1) * 128]
                        sn, so = cg % 2, (cg + 1) % 2
                        pbi = cg % 4
                        elc = el[:, cg:cg + 1]
                        if cg == 0:
                            vcopy(Sst[sn], usrc, PSS, [("S", sn)])
                        else:
                            ts_("dve", Pb[pbi], Sst[so], elc, None, ALU.mult, None, [("S", so), ("el", tt)], [("Pb", pbi)])
                            stt_(Sst[sn], Sst[so], elc, usrc, ALU.mult, ALU.add, [("S", so), ("el", tt)] + PSS, [("S", sn)])
                        mm(co[:, c * 64:(c + 1) * 64], tokh[:, tb, :], scm[sc][:, c, :], True, cg == 0, [kI, ("scm", sc)], PSO)
                        if cg > 0:
                            mm(co[:, c * 64:(c + 1) * 64], Pb[pbi], qtl[:, c * 64:(c + 1) * 64], False, True, [("Pb", pbi), "qtl"], PSO)
                        yield
                act(sqb, co[:, :], AF.Square, PSO, ["sqb"])
                yield
                mm(cp[:, :], ones, sqb, True, True, ["sqb"], PSP)
                yield
                act(b_ln, cp[:, :], AF.Ln, PSP, [KEF], bias=epsc, scale=1.0 / 128)
                yield
                act(b_rs, b_ln, AF.Exp, [KEF], [KFP], scale=-0.5)
                yield
                tt_("dve", b_o1, co[:, :], b_rs, ALU.mult, PSO + [KFP], [KG])
                yield
                stt_(ohgT[:, h, ts], b_o1, gcol, b_sg, ALU.mult, ALU.mult, [KG, KSG], [("ohgc", h)])
                yield

        PD = PC + 39 * K
        o = PD
        wq = [reg(o + i * 4 * K, 4 * K, BF16).rearrange("p (k c) -> p k c", c=256) for i in range(2)]
        o += 8 * K
        qnT = reg(o, 4 * K, BF16)
        nk = [reg(o + 4 * K, 4 * K, BF16), reg(o + 8 * K, 4 * K, BF16)]
        A("pool", lambda e: e.memset(reg(PD + 8 * K + 4 * K, 8 * K, BF16), 0.0), writes=[("qk", 1, i) for i in range(4)])
        o += 12 * K
        d_ln = reg(o, 2 * K, F32)
        d_rs = reg(o + 2 * K, 2 * K, F32)
        o += 4 * K
        d_sq = reg(o, 1 * K, BF16)
        o += 1 * K
        EbBase = o
        Eb = [reg(o + i * 2 * K, 2 * K, F32) for i in range(2)]
        o += 4 * K
        SPb = [reg(o + i * K, K, BF16) for i in range(3)]
        SPb.append(reg(EbBase, K, BF16))
        o += 3 * K
        Ab = [reg(o + i * K, K, BF16) for i in range(3)]
        Ab.append(reg(EbBase + 2 * K, K, BF16))
        o += 3 * K
        Ssum = [reg(o + i * K, K, BF16) for i in range(4)]
        o += 4 * K
        assert o <= CO, o

        def attn_pair(p):
            gen = hgrn_gen(p)

            def pull(k):
                for _ in range(k):
                    try:
                        next(gen)
                    except StopIteration:
                        return

            ws = p % 2
            if p == 0:
                A("pool", lambda e: e.dma_start(out=wq[0], in_=wsb_d[0]), writes=[("wq", 0)], dma_key=("wq", 0))
            if p + 1 < 8:
                nws = (p + 1) % 2
                A("pool", lambda e: e.dma_start(out=wq[nws], in_=wsb_d[p + 1]), writes=[("wq", nws)], dma_key=("wq", nws))
            tokp = tokV(p)
            kV = keyV(p)
            for which, gcolv in ((0, gq8), (1, ngk)):
                gc = gcolv[:, p:p + 1]
                for tt in range(4):
                    ts = slice(tt * 512, (tt + 1) * 512)
                    rb = (which * 4 + tt) % 2
                    sb_ = 1 - rb
                    for kc in range(8):
                        mm(banks[rb][:, :], wq[ws][:, kc, which * 128:(which + 1) * 128], hT[:, kc, ts], kc == 0, kc == 7,
                           hTk(tt) + [("wq", ws)], [("ps", rb)])
                    pull(1)
                    act(d_sq, banks[rb][:, :], AF.Square, [("ps", rb)], ["dsq"])
                    pull(1)
                    mm(banks[sb_][:, :], blockones, d_sq, True, True, ["dsq"], [("ps", sb_)])
                    pull(1)
                    act(d_ln, banks[sb_][:, :], AF.Ln, [("ps", sb_)], ["dln"], bias=epsc, scale=1.0 / 64)
                    pull(1)
                    act(d_rs, d_ln, AF.Exp, ["dln"], ["drs"], scale=-0.5)
                    pull(1)
                    if which == 0:
                        stt_(qnT[:, ts], banks[rb][:, :], gc, d_rs, ALU.mult, ALU.mult, [("ps", rb), "drs"], [("qk", 0, tt)])
                    else:
                        for hd in range(2):
                            ps_ = slice(hd * 64, (hd + 1) * 64)
                            stt_(nk[hd][ps_, ts], banks[rb][ps_, :], gc[ps_, :], d_rs[ps_, :], ALU.mult, ALU.mult, [("ps", rb), "drs"], [("qk", 1, tt)])
                    pull(1)
            steps = []
            for qt in range(4):
                for head in range(2):
                    for kb in range(4 * qt + 3, -1, -1):
                        steps.append((head, qt, kb))
            n = len(steps)
            info = {}
            rs_slot = {}
            ws_slot = {}
            cur = 0
            for i_, (head_, qt_, kb_) in enumerate(steps):
                first_ = kb_ == 4 * qt_ + 3
                if first_:
                    cur = (cur + 1) % 4
                rs_slot[i_] = cur
                if kb_ - 4 * qt_ <= 0 and not first_:
                    cur = (cur + 1) % 4
                ws_slot[i_] = cur

            def s1(i):
                head, qt, kb = steps[i]
                d = kb - 4 * qt
                c0 = max(d, 0) * 128
                diag = d >= 0
                zb = i % 2
                sp_i = i % 4
                info[i] = (c0, diag, zb, sp_i)
                qkeys = [("qk", 0, qt), ("qk", 1, kb // 4)]
                lhs = nk[head][:, kb * 128:(kb + 1) * 128]
                rhs = qnT[:, qt * 512 + c0:(qt + 1) * 512]
                mm(banks[zb][:, c0:512], lhs, rhs, True, not diag, qkeys, [("ps", zb)])
                if diag:
                    mm(banks[zb][:, c0:c0 + 128], ident, posmask, False, True, [], [("ps", zb)])
                act(banks[zb][:, c0:512], banks[zb][:, c0:512], AF.Exp, [("ps", zb)], [("ps", zb)], scale=-1.0)
                act(SPb[sp_i][:, c0:512], banks[zb][:, c0:512], AF.Ln, [("ps", zb)], [("SP", sp_i)], bias=1.0)

            def s2(i):
                head, qt, kb = steps[i]
                c0, diag, zb, sp_i = info[i]
                first = kb == 4 * qt + 3
                cbk = 2 + i % 2
                ai = i % 4
                qkeys = [("qk", 0, qt), ("qk", 1, kb // 4)]
                lhs = nk[head][:, kb * 128:(kb + 1) * 128]
                rhs = qnT[:, qt * 512 + c0:(qt + 1) * 512]
                rs_, ws_ = rs_slot[i], ws_slot[i]
                mm(banks[cbk][:, c0:512], triincl, SPb[sp_i][:, c0:512], True, False, [("SP", sp_i)], [("ps", cbk)])
                if not first:
                    mm(banks[cbk][:, c0:512], ones, Ssum[rs_][:, c0:512], False, False, [("Ssum", rs_)], [("ps", cbk)])
                mm(banks[cbk][:, c0:512], lhs, rhs, False, not diag, qkeys, [("ps", cbk)])
                if diag:
                    mm(banks[cbk][:, c0:c0 + 128], ident, posmask, False, True, [], [("ps", cbk)])
                act(Ab[ai][:, c0:512], banks[cbk][:, c0:512], AF.Exp, [("ps", cbk)], [("A", ai)], scale=-1.0)
                if kb > 0:
                    if first:
                        A("pool", lambda e: e.memset(Ssum[ws_][:, 0:384], 0.0), writes=[("Ssum", ws_)])
                        A("pool", lambda e: e.tensor_copy(out=Ssum[ws_][:, 384:512], in_=SPb[sp_i][:, 384:512]), reads=[("SP", sp_i)], writes=[("Ssum", ws_)])
                    else:
                        tt_("dve", Ssum[ws_][:, c0:512], Ssum[rs_][:, c0:512], SPb[sp_i][:, c0:512], ALU.add, [("Ssum", rs_), ("SP", sp_i)], [("Ssum", ws_)])

            def s3(i):
                head, qt, kb = steps[i]
                c0, diag, zb, sp_i = info[i]
                hb = head * 64
                first = kb == 4 * qt + 3
                ai = i % 4
                okey = [("ps4", head)]
                if first:
                    mm(banks[4][hb:hb + 64, :], zeros[:, 0:64], qnT[:, 0:512], True, False, [("qk", 0, 0)], okey)
                mm(banks[4][hb:hb + 64, c0:512], tokp[:, kb, head * 64:(head + 1) * 64], Ab[ai][:, c0:512], False, kb == 0, [kV, ("A", ai)], okey)
                if kb == 0:
                    dst = osbT[hb:hb + 64, p, qt * 512:(qt + 1) * 512]
                    if qt % 2 == 0:
                        vcopy(dst, banks[4][hb:hb + 64, :], okey, [("osbc", p)])
                    else:
                        acopy(dst, banks[4][hb:hb + 64, :], okey, [("osbc", p)])

            for i in range(n + 4):
                if i < n:
                    s1(i)
                pull(1)
                if 0 <= i - 2 < n:
                    s2(i - 2)
                if i % 3 == 0:
                    pull(1)
                if 0 <= i - 4 < n:
                    s3(i - 4)
            for _ in gen:
                pass

        try:
            for p in range(8):
                if p == 7:
                    wE0 = reg(198 * K, 8 * K, BF16).rearrange("p (k c) -> p k c", c=512)
                    A("pool", lambda e: e.dma_start(out=wE0, in_=we_d[0]), writes=[("wE", 0)], dma_key=("wE", 0))
                attn_pair(p)
                ck(2.1 + p * 0.1)
        except _Stop:
            return finish()
        P.barrier()
        if "ohg" in dumps:
            A("sp", lambda e: e.dma_start(out=dumps["ohg"], in_=reg(32 * K, 32 * K, BF16)), dma_key="dump")
            P.barrier()
        if "osb" in dumps:
            A("sp", lambda e: e.dma_start(out=dumps["osb"], in_=reg(64 * K, 32 * K, BF16)), dma_key="dump")
            P.barrier()
        if upto <= 3:
            return finish()
        mixT = reg(SCR, 32 * K, BF16).rearrange("p (k t) -> p k t", t=S)
        wE = [reg(198 * K, 8 * K, BF16).rearrange("p (k c) -> p k c", c=512),
              reg(SCR + 40 * K, 8 * K, BF16).rearrange("p (k c) -> p k c", c=512)]
        o = SCR + 48 * K
        e_sa = [reg(o + i * 2 * K, 2 * K, F32) for i in range(2)]
        e_m = [reg(o + 4 * K + i * 2 * K, 2 * K, F32) for i in range(2)]
        for nb in range(8):
            ws = nb % 2
            if nb > 0:
                A("pool", lambda e, nb=nb, ws=ws: e.dma_start(out=wE[ws], in_=we_d[nb]), writes=[("wE", ws)], dma_key=("wE", ws))
            for tt in range(4):
                ts = slice(tt * 512, (tt + 1) * 512)
                for half, (gc0, yc0, src, skey) in enumerate(((0, 256, ohgT, "ohg"), (128, 384, osbT, "osb"))):
                    gb = (4 * half) % 8
                    yb = gb + 1
                    gb2 = gb + (tt % 2) * 2
                    yb2 = yb + (tt % 2) * 2
                    for kc in range(8):
                        A("pe", lambda e, kc=kc, gc0=gc0, gb2=gb2, ts=ts, ws=ws: e.matmul(banks[gb2][:, :], lhsT=wE[ws][:, kc, gc0:gc0 + 128], rhs=hT[:, kc, ts], start=(kc == 0), stop=(kc == 7)),
                          reads=[("wE", ws)], writes=[("ps", gb2)])
                    A("act", lambda e, half=half, gb2=gb2: e.activation(out=e_sa[half], in_=banks[gb2][:, :], func=AF.Sigmoid), reads=[("ps", gb2)], writes=[("sa", half)])
                    for kc in range(8):
                        A("pe", lambda e, kc=kc, yc0=yc0, yb2=yb2, ts=ts, src=src, ws=ws: e.matmul(banks[yb2][:, :], lhsT=wE[ws][:, kc, yc0:yc0 + 128], rhs=src[:, kc, ts], start=(kc == 0), stop=(kc == 7)),
                          reads=[("wE", ws)], writes=[("ps", yb2)])
                    A("dve", lambda e, half=half, yb2=yb2: e.tensor_tensor(out=e_m[half], in0=banks[yb2][:, :], in1=e_sa[half], op=ALU.mult), reads=[("ps", yb2), ("sa", half)], writes=[("m", half)])
                A("pool", lambda e, nb=nb, ts=ts: e.tensor_tensor(out=mixT[:, nb, ts], in0=e_m[0], in1=e_m[1], op=ALU.add), reads=[("m", 0), ("m", 1)], writes=[("mix", nb, tt)])
        P.barrier()
        if "mix" in dumps:
            A("sp", lambda e: e.dma_start(out=dumps["mix"], in_=reg(SCR, 32 * K, BF16)), dma_key="dump")
            P.barrier()

        if upto <= 4:
            return finish()
        wfo = reg(0, 44 * K, BF16).rearrange("p (j c) -> p j c", c=1024)
        wo = reg(44 * K, 16 * K, BF16).rearrange("p (k c) -> p k c", c=1024)
        actT = reg(60 * K, 22 * K, BF16).rearrange("p (j t) -> p j t", t=512)
        o = SCR + 32 * K
        x2t = reg(o, 16 * K, F32).rearrange("p (b c) -> p b c", c=1024)
        o += 16 * K
        h2Tt = reg(o, 8 * K, BF16).rearrange("p (k t) -> p k t", t=512)
        o += 8 * K
        wf = [reg(o + i * 4 * K, 4 * K, BF16).rearrange("p (k c) -> p k c", c=256) for i in range(3)]
        o += 12 * K
        xr = [reg(o + i * 4 * K, 4 * K, F32) for i in range(2)]
        o += 8 * K
        xn2 = [reg(o + i * 2 * K, 2 * K, BF16) for i in range(2)]
        o += 4 * K
        ost = [reg(o + i * 4 * K, 4 * K, F32) for i in range(2)]
        o += 8 * K
        assert o <= CO, o
        junk2 = reg(82 * K, 2 * K, BF16)
        g_sg = [reg(84 * K + i * 2 * K, 2 * K, F32) for i in range(2)]

        for cg in range(2):
            A("pool", lambda e, cg=cg: e.dma_start(out=wo[:, :, cg * 512:(cg + 1) * 512], in_=wo_d[cg]), writes=[("wo", cg)], dma_key=("wo", cg))
        wfo_keys = [("wfo", jj) for jj in range(0, NJ, 2)]
        wfo_todo = list(range(0, NJ, 2))

        def wfo_dma():
            if wfo_todo:
                jj = wfo_todo.pop(0)
                A("pool", lambda e: e.dma_start(out=wfo[:, jj:jj + 2, :], in_=wfo_d[:, jj:jj + 2, :]), writes=[("wfo", jj)], dma_key="wfo")

        x2b1 = [reg(88 * K, 8 * K, F32).rearrange("p (b c) -> p b c", c=1024),
                reg(198 * K, 8 * K, F32).rearrange("p (b c) -> p b c", c=1024)]

        def x2(s_, tbl):
            if s_ == 0:
                return x2t[:, tbl, :]
            return x2b1[tbl // 2][:, tbl % 2, :]

        wfc = [0]

        def Fmm(tt, tbl):
            tb = tt * 4 + tbl
            xs = tb % 2
            s_ = tt % 2
            A("sp", lambda e: e.dma_start(out=xr[xs], in_=x[tb * 128:(tb + 1) * 128, :]), writes=[("xr", xs)], dma_key=("xr", xs))
            for cg in range(2):
                bk = (tbl * 2 + cg) % 4
                cs = slice(cg * 512, (cg + 1) * 512)
                for kc in range(8):
                    mm(banks[bk][:, :], mixT[:, kc, tb * 128:(tb + 1) * 128], wo[:, kc, cs], kc == 0, kc == 7, [("wo", cg)], [("ps", bk)])
                tt_("dve", x2(s_, tbl)[:, cs], banks[bk][:, :], xr[xs][:, cs], ALU.add, [("ps", bk), ("xr", xs)], [("x2", s_, tbl, cg)])

        def Frms(tt, tbl):
            tb = tt * 4 + tbl
            xs = tb % 2
            s_ = tt % 2
            rms_rows(x2(s_, tbl), ss2[:, tb:tb + 1], ln2[:, tb:tb + 1], rstd2[:, tb:tb + 1], g2bc, xn2[xs], junk2,
                     [("x2", s_, tbl, 0), ("x2", s_, tbl, 1)], [("xn2", xs)])

        def Ftr(tt, tbl):
            tb = tt * 4 + tbl
            xs = tb % 2
            bk = 4 + tbl % 2
            for kc in range(8):
                tr(banksb[bk][:, kc * 128:(kc + 1) * 128], xn2[xs][:, kc * 128:(kc + 1) * 128], [("xn2", xs)], [("ps", bk)])
            acopy(h2Tt[:, :, tbl * 128:(tbl + 1) * 128], banksb[bk][:, :].rearrange("p (k t) -> p k t", t=128), [("ps", bk)], [("h2", tbl)])

        h2k = [("h2", i) for i in range(4)]
        actk = [("act", j) for j in range(NJ)]

        def Gin(tt):
            for j in range(NJ):
                ws = wfc[0] % 3
                wfc[0] += 1
                A("pool", lambda e, j=j, ws=ws: e.dma_start(out=wf[ws], in_=wfi_d[j]), writes=[("wf", ws)], dma_key=("wf", ws))
                par = j % 2
                gb_ = (0, 2)[par]
                ub_ = (1, 3)[par]
                for kc in range(8):
                    mm(banks[gb_][:, :], wf[ws][:, kc, 0:128], h2Tt[:, kc, :], kc == 0, kc == 7, h2k + [("wf", ws)], [("ps", gb_)])
                for kc in range(8):
                    mm(banks[ub_][:, :], wf[ws][:, kc, 128:256], h2Tt[:, kc, :], kc == 0, kc == 7, h2k + [("wf", ws)], [("ps", ub_)])
                act(g_sg[par], banks[gb_][:, :], AF.Silu, [("ps", gb_)], [("gsg", par)])
                tt_("dve", actT[:, j, :], banks[ub_][:, :], g_sg[par], ALU.mult, [("ps", ub_), ("gsg", par)], [("act", j)])
                if j % 2 == 1:
                    wfo_dma()

        def Gout(tt, tbl):
            tb = tt * 4 + tbl
            os_ = tb % 2
            s_ = tt % 2
            for cg in range(2):
                bk = 4 + (tbl * 2 + cg) % 4
                cs = slice(cg * 512, (cg + 1) * 512)
                for j in range(NJ):
                    mm(banks[bk][:, :], actT[:, j, tbl * 128:(tbl + 1) * 128], wfo[:, j, cs], j == 0, j == NJ - 1, actk + wfo_keys, [("ps", bk)])
                tt_("dve", ost[os_][:, cs], banks[bk][:, :], x2(s_, tbl)[:, cs], ALU.add, [("ps", bk), ("x2", s_, tbl, cg)], [("ost", os_)])
            A("sp", lambda e: e.dma_start(out=out[tb * 128:(tb + 1) * 128, :], in_=ost[os_]), reads=[("ost", os_)], writes=[("out", tb)], dma_key=("ost", os_))

        for tbl in range(4):
            Fmm(0, tbl)
        Frms(0, 0)
        Frms(0, 1)
        Ftr(0, 0)
        Frms(0, 2)
        Ftr(0, 1)
        Frms(0, 3)
        Ftr(0, 2)
        Ftr(0, 3)
        for tt in range(4):
            Gin(tt)
            n_ = tt + 1
            if n_ < 4:
                for tbl in range(4):
                    Fmm(n_, tbl)
                Frms(n_, 0)
                Frms(n_, 1)
                Gout(tt, 0)
                Ftr(n_, 0)
                Ftr(n_, 1)
                Frms(n_, 2)
                Frms(n_, 3)
                Gout(tt, 1)
                Gout(tt, 2)
                Ftr(n_, 2)
                Ftr(n_, 3)
                Gout(tt, 3)
            else:
                for tbl in range(4):
                    Gout(tt, tbl)
        P.barrier()
        A("sp", lambda e: e.nop())
        P.emit()
    return nc


def _blk(w, cols):
    return np.ascontiguousarray(w[:, cols].reshape(8, 128, -1).transpose(1, 0, 2))


def _prep_weights(w_in, w_hg_out, w_sb_out, w_o, w_ffn_in, w_ffn_out):
    ar = np.arange
    wiv = np.stack([_blk(w_in, ar(2048 + g * 512, 2048 + (g + 1) * 512)) for g in range(2)] +
                   [_blk(w_in, ar(6144 + g * 512, 6144 + (g + 1) * 512)) for g in range(2)])
    whg = np.stack([_blk(w_in, np.concatenate([ar(h * 128, (h + 1) * 128), 1024 + ar(h * 128, (h + 1) * 128), 3072 + ar(h * 128, (h + 1) * 128)])) for h in range(8)])
    wsb = np.stack([_blk(w_in, np.concatenate([4096 + ar(p * 128, (p + 1) * 128), 5120 + ar(p * 128, (p + 1) * 128)])) for p in range(8)])
    we = np.stack([np.concatenate([_blk(w_in, np.concatenate([7168 + ar(nb * 128, (nb + 1) * 128), 8192 + ar(nb * 128, (nb + 1) * 128)])),
                                   _blk(w_hg_out, ar(nb * 128, (nb + 1) * 128)), _blk(w_sb_out, ar(nb * 128, (nb + 1) * 128))], axis=2) for nb in range(8)])
    wo = np.stack([_blk(w_o, ar(cg * 512, (cg + 1) * 512)) for cg in range(2)])
    wfi = np.stack([_blk(w_ffn_in, np.concatenate([ar(j * 128, (j + 1) * 128), FF + ar(j * 128, (j + 1) * 128)])) for j in range(NJ)])
    wfo = np.ascontiguousarray(w_ffn_out.reshape(NJ, 128, 1024).transpose(1, 0, 2))
    return dict(wiv=wiv, whg=whg, wsb=wsb, we=we, wo=wo, wfi=wfi, wfo=wfo)


def _consts(norm1_gain, norm2_gain, lb_logits, hg_out_norm, sb_q_norm, sb_k_norm):
    cf = np.zeros((128, CF_N), np.float32)
    cf[:, CF_G1:CF_G1 + 1024] = norm1_gain.reshape(1, 1024)
    cf[:, CF_G2:CF_G2 + 1024] = norm2_gain.reshape(1, 1024)
    t = np.arange(512)
    cf[:, CF_RST:CF_RST + 512] = (t % 64 != 0).astype(np.float32)[None, :]
    pp = np.arange(128)[:, None]
    tq = np.arange(64)[None, :]
    cf[:, CF_M64:CF_M64 + 64] = ((pp < 64) & (pp <= tq)).astype(np.float32)
    cf[:, CF_M64 + 64:CF_M64 + 128] = ((pp >= 64) & (pp - 64 <= tq)).astype(np.float32)
    cf[:, CF_LBL:CF_LBL + 16] = lb_logits.reshape(2, 8, 128).transpose(2, 0, 1).reshape(128, 16)
    cf[:, CF_GQ:CF_GQ + 8] = sb_q_norm.reshape(8, 128).T
    cf[:, CF_GK:CF_GK + 8] = sb_k_norm.reshape(8, 128).T
    cf[:, CF_GHG:CF_GHG + 8] = hg_out_norm.reshape(8, 128).T
    cf[:, CF_EPS] = EPS
    cf[:, CF_ONE] = 1.0
    cb = np.zeros((128, CB_N), np.float32)
    j = np.arange(128)[:, None]
    s = np.arange(128)[None, :]
    cb[:, 0:128] = (j == s)
    cb[:, 128:256] = (j >= s)
    cb[:, 256:384] = 1.0
    cb[:, 384:512] = (j // 64 == s // 64)
    cb[:, 512:640] = BIG * (j >= s)
    return cf, cb.astype(ml_dtypes.bfloat16)


_NC_CACHE = {}


def kernel(x, norm1_gain, w_in, lb_logits, hg_out_norm, sb_q_norm, sb_k_norm,
           w_hg_out, w_sb_out, w_o, norm2_gain, w_ffn_in, w_ffn_out, _dump=None, _upto=99, _trace=False):
    f = lambda a: np.asarray(a, dtype=np.float32)
    x = f(x)
    wd = _prep_weights(f(w_in)[0], f(w_hg_out)[0], f(w_sb_out)[0], f(w_o)[0], f(w_ffn_in)[0], f(w_ffn_out)[0])
    cf, cb = _consts(f(norm1_gain), f(norm2_gain), f(lb_logits), f(hg_out_norm), f(sb_q_norm), f(sb_k_norm))
    key = (tuple(_dump or ()), _upto)
    if key not in _NC_CACHE:
        _NC_CACHE[key] = build_nc(_dump, _upto)
    nc = _NC_CACHE[key]
    in_maps = []
    for b in range(8):
        m = dict(wd)
        m["x"] = np.ascontiguousarray(x[b])
        m["cf"] = cf
        m["cb"] = cb
        in_maps.append(m)
    if _trace:
        res = run_bass_kernel_spmd(nc, in_maps, core_ids=list(range(8)), trace=True)
        print('exec_time_ns', res.exec_time_ns)
    else:
        res = run_bass_kernel_spmd(nc, in_maps, core_ids=list(range(8)))
    outp = np.stack([res.results[b]["out"] for b in range(8)]).astype(np.float32)
    if _dump:
        return outp, res.results
    return outp
```

```python
import contextlib
import numpy as np
import ml_dtypes
import concourse.bass as bass
import concourse.mybir as mybir
from concourse.bass_utils import run_bass_kernel_spmd

F32 = mybir.dt.float32
BF16 = mybir.dt.bfloat16
AF = mybir.ActivationFunctionType
ALU = mybir.AluOpType

ENGS = ("sp", "act", "dve", "pool", "pe")
S = 2048
D = 1024
FF = 2816
NJ = 22
EPS = 1e-6
BIG = 30000.0


class _Stop(Exception):
    pass


class Op:
    __slots__ = ("eng", "fn", "deps", "signal", "sigval", "dma", "dsem", "dval")

    def __init__(self, eng, fn, dma=False):
        self.eng = eng
        self.fn = fn
        self.deps = []
        self.signal = False
        self.sigval = 0
        self.dma = dma
        self.dsem = None
        self.dval = 0


class Prog:
    def __init__(self, nc):
        self.nc = nc
        self.ops = {e: [] for e in ENGS}
        self.last_w = {}
        self.readers = {}
        self.dma_keys = {}
        self.barrier_deps = {e: [] for e in ENGS}
        self.dma_since_barrier = []

    @staticmethod
    def _need(d, op, raw):
        if d is op:
            return False
        if d.dma or op.dma:
            return True
        if d.eng == op.eng:
            if op.eng == "pe":
                return False
            return raw
        return True

    def add(self, eng, fn, reads=(), writes=(), dma_key=None):
        op = Op(eng, fn, dma=dma_key is not None)
        deps = {}
        for k in reads:
            w = self.last_w.get(k)
            if w is not None:
                deps[id(w)] = (w, True)
        for k in writes:
            w = self.last_w.get(k)
            if w is not None and id(w) not in deps:
                deps[id(w)] = (w, False)
            for r in self.readers.get(k, ()):
                if id(r) not in deps:
                    deps[id(r)] = (r, False)
        for d in self.barrier_deps[eng]:
            if id(d) not in deps:
                deps[id(d)] = (d, True)
        self.barrier_deps[eng] = []
        for d, raw in deps.values():
            if self._need(d, op, raw):
                op.deps.append(d)
                d.signal = True
        for k in reads:
            self.readers.setdefault(k, []).append(op)
        for k in writes:
            self.last_w[k] = op
            self.readers[k] = []
        if op.dma:
            cnt = self.dma_keys.setdefault(dma_key, [0])
            cnt[0] += 16
            op.dsem = dma_key
            op.dval = cnt[0]
            self.dma_since_barrier.append(op)
        self.ops[eng].append(op)
        return op

    def barrier(self):
        lasts = [self.ops[e][-1] for e in ENGS if self.ops[e]]
        lasts += self.dma_since_barrier
        self.dma_since_barrier = []
        for e in ENGS:
            self.barrier_deps[e] = list(lasts)
        self.last_w = {}
        self.readers = {}

    def emit(self):
        nc = self.nc
        for e in ENGS:
            cnt = 0
            for op in self.ops[e]:
                if op.signal and not op.dma:
                    cnt += 1
                    op.sigval = cnt
        with contextlib.ExitStack() as st:
            esem = {e: st.enter_context(nc.semaphore("s_" + e)) for e in ENGS}
            dsem = {k: st.enter_context(nc.semaphore("d_%d" % i))
                    for i, k in enumerate(self.dma_keys)}
            block = st.enter_context(nc.Block())
            prog = self

            def run(engname, engine):
                waited = {}
                for op in prog.ops[engname]:
                    need = {}
                    for d in op.deps:
                        if d.dma:
                            s, v = dsem[d.dsem], d.dval
                        else:
                            s, v = esem[d.eng], d.sigval
                        if need.get(id(s), (None, 0))[1] < v:
                            need[id(s)] = (s, v)
                    for key, (s, v) in need.items():
                        if waited.get(key, 0) >= v:
                            continue
                        waited[key] = v
                        engine.wait_ge(s, v)
                    inst = op.fn(engine)
                    if op.dma:
                        inst.then_inc(dsem[op.dsem], 16)
                    elif op.signal:
                        inst.then_inc(esem[engname], 1)

            @block.sync
            def _(eng):
                run("sp", eng)

            @block.scalar
            def _(eng):
                run("act", eng)

            @block.vector
            def _(eng):
                run("dve", eng)

            @block.gpsimd
            def _(eng):
                run("pool", eng)

            @block.tensor
            def _(eng):
                run("pe", eng)


CF_G1 = 0
CF_G2 = 1024
CF_RST = 2048
CF_M64 = 2560
CF_LBL = 2688
CF_GQ = 2704
CF_GK = 2712
CF_GHG = 2720
CF_EPS = 2728
CF_ONE = 2729
CF_N = 2736
CB_N = 6 * 128


def build_nc(dump=None, upto=99):
    dump = dump or ()
    nc = bass.Bass("TRN2", target_bir_lowering=False)
    dram = lambda n, s, dt=F32: nc.dram_tensor(n, s, dt, kind="ExternalInput").ap()
    x = dram("x", [S, D])
    cf_d = dram("cf", [128, CF_N])
    cb_d = dram("cb", [128, CB_N], BF16)
    wiv_d = dram("wiv", [4, 128, 8, 512])
    whg_d = dram("whg", [8, 128, 8, 384])
    wsb_d = dram("wsb", [8, 128, 8, 256])
    we_d = dram("we", [8, 128, 8, 512])
    wo_d = dram("wo", [2, 128, 8, 512])
    wfi_d = dram("wfi", [NJ, 128, 8, 256])
    wfo_d = dram("wfo", [128, NJ, 1024])
    out = nc.dram_tensor("out", [S, D], F32, kind="ExternalOutput").ap()
    dumps = {}
    for name in dump:
        shp = {"hT": [128, 8 * S], "ohg": [128, 8 * S], "osb": [128, 8 * S],
               "mix": [128, 8 * S]}[name]
        dumps[name] = nc.dram_tensor("dump_" + name, shp, BF16, kind="ExternalOutput").ap()

    with contextlib.ExitStack() as st:
        ARENA = 206 * 1024
        arena = st.enter_context(nc.sbuf_tensor("arena", [128, ARENA // 2], BF16))
        banks = [st.enter_context(nc.psum_tensor("ps%d" % i, [128, 512], F32)) for i in range(8)]
        banksb = [b.bitcast(BF16) for b in banks]

        def reg(off, nbytes, dt):
            assert off % 4 == 0 and off + nbytes <= ARENA, (off, nbytes)
            a = arena[:, off // 2:(off + nbytes) // 2]
            return a if dt == BF16 else a.bitcast(dt)

        K = 1024
        P = Prog(nc)
        A = P.add

        def ck(v):
            if upto <= v:
                raise _Stop()

        def finish():
            P.barrier()
            A("sp", lambda e: e.nop())
            P.emit()
            return nc
        cnt = [0]

        def alt(n=2):
            cnt[0] += 1
            return cnt[0] % n

        hT = reg(0, 32 * K, BF16).rearrange("p (k t) -> p k t", t=S)
        ohgT = reg(32 * K, 32 * K, BF16).rearrange("p (k t) -> p k t", t=S)
        osbT = reg(64 * K, 32 * K, BF16).rearrange("p (k t) -> p k t", t=S)
        CO = 184 * K
        cf = reg(CO, CF_N * 4, F32)
        cb = reg(CO + CF_N * 4, CB_N * 2, BF16)
        small = reg(CO + CF_N * 4 + CB_N * 2, 1024, F32)
        ident = cb[:, 0:128]
        triincl = cb[:, 128:256]
        ones = cb[:, 256:384]
        blockones = cb[:, 384:512]
        posmask = cb[:, 512:640]
        zeros = cb[:, 640:768]
        g1bc = cf[:, CF_G1:CF_G1 + 1024]
        g2bc = cf[:, CF_G2:CF_G2 + 1024]
        rstm = cf[:, CF_RST:CF_RST + 512]
        m64 = cf[:, CF_M64:CF_M64 + 128].rearrange("p (a t) -> p a t", t=64)
        epsc = cf[:, CF_EPS:CF_EPS + 1]
        onec = cf[:, CF_ONE:CF_ONE + 1]
        ss1 = small[:, 0:16]
        ln1 = small[:, 16:32]
        rstd1 = small[:, 32:48]
        lb = small[:, 48:56]
        oml = small[:, 56:64]
        gq8 = small[:, 64:72]
        ngk = small[:, 72:80]
        tmp8 = small[:, 80:88]
        ss2 = small[:, 96:112]
        ln2 = small[:, 112:128]
        rstd2 = small[:, 128:144]
        el = small[:, 144:176]

        A("sp", lambda e: e.dma_start(out=cf, in_=cf_d), writes=["cf"], dma_key="cf")
        A("sp", lambda e: e.dma_start(out=cb, in_=cb_d), writes=["cb"], dma_key="cb")
        A("dve", lambda e: e.tensor_tensor(out=tmp8, in0=cf[:, CF_LBL + 8:CF_LBL + 16], in1=cf[:, CF_LBL:CF_LBL + 8], op=ALU.subtract), reads=["cf"], writes=["tmp8"])
        A("act", lambda e: e.activation(out=tmp8, in_=tmp8, func=AF.Exp), reads=["tmp8"], writes=["tmp8"])
        A("dve", lambda e: e.tensor_scalar(out=tmp8, in0=tmp8, scalar1=1.0, scalar2=None, op0=ALU.add), reads=["tmp8"], writes=["tmp8"])
        A("dve", lambda e: e.reciprocal(out=lb, in_=tmp8), reads=["tmp8"], writes=["lb"])
        A("dve", lambda e: e.tensor_scalar(out=oml, in0=lb, scalar1=-1.0, scalar2=1.0, op0=ALU.mult, op1=ALU.add), reads=["lb"], writes=["oml"])
        A("dve", lambda e: e.tensor_scalar(out=gq8, in0=cf[:, CF_GQ:CF_GQ + 8], scalar1=0.125, scalar2=None, op0=ALU.mult), reads=["cf"], writes=["gq8"])
        A("dve", lambda e: e.tensor_scalar(out=ngk, in0=cf[:, CF_GK:CF_GK + 8], scalar1=-1.0, scalar2=None, op0=ALU.mult), reads=["cf"], writes=["ngk"])
        P.barrier()

        SCR = 96 * K

        def rms_rows(src, sscol, lncol, rstdcol, gbc, xn_out, junk, kin, kout):
            A("act", lambda e: e.activation(out=junk, in_=src, func=AF.Square, accum_out=sscol),
              reads=kin, writes=["junk", ("st", id(sscol))])
            A("act", lambda e: e.activation(out=lncol, in_=sscol, func=AF.Ln, bias=epsc, scale=1.0 / D),
              reads=[("st", id(sscol))], writes=[("st", id(lncol))])
            A("act", lambda e: e.activation(out=rstdcol, in_=lncol, func=AF.Exp, scale=-0.5),
              reads=[("st", id(lncol))], writes=[("st", id(rstdcol))])
            A("dve", lambda e: e.scalar_tensor_tensor(out=xn_out, in0=src, scalar=rstdcol, in1=gbc, op0=ALU.mult, op1=ALU.mult),
              reads=kin + [("st", id(rstdcol))], writes=kout)

        AS = SCR + 48 * K
        xt = [reg(AS + i * 4 * K, 4 * K, F32) for i in range(3)]
        xn = [reg(AS + 12 * K + i * 2 * K, 2 * K, BF16) for i in range(2)]
        junk = reg(AS + 16 * K, 2 * K, BF16)
        for tb in range(16):
            s3 = tb % 3
            s2 = tb % 2
            bk = tb % 2
            A("sp", lambda e, tb=tb, s3=s3: e.dma_start(out=xt[s3], in_=x[tb * 128:(tb + 1) * 128, :]),
              writes=[("xt", s3)], dma_key=("xt", s3))
            rms_rows(xt[s3], ss1[:, tb:tb + 1], ln1[:, tb:tb + 1], rstd1[:, tb:tb + 1], g1bc, xn[s2], junk,
                     [("xt", s3)], [("xn", s2)])
            for kc in range(8):
                A("pe", lambda e, kc=kc, s2=s2, bk=bk: e.transpose(out=banksb[bk][:, kc * 128:(kc + 1) * 128], in_=xn[s2][:, kc * 128:(kc + 1) * 128], identity=ident),
                  reads=[("xn", s2)], writes=[("ps", bk)])
            src = banksb[bk][:, :].rearrange("p (k t) -> p k t", t=128)
            dst = hT[:, :, tb * 128:(tb + 1) * 128]
            if tb % 2 == 0:
                A("act", lambda e, src=src, dst=dst: e.copy(out=dst, in_=src), reads=[("ps", bk)], writes=[("hT", tb)])
            else:
                A("dve", lambda e, src=src, dst=dst: e.tensor_copy(out=dst, in_=src), reads=[("ps", bk)], writes=[("hT", tb)])
        if "hT" in dumps or upto <= 1:
            P.barrier()
        if "hT" in dumps:
            A("sp", lambda e: e.dma_start(out=dumps["hT"], in_=reg(0, 32 * K, BF16)), dma_key="dump")
            P.barrier()

        if upto <= 1:
            return finish()
        hTk = lambda tt: [("hT", tt * 4 + i) for i in range(4)]

        def mm(o_, lhsT, rhs, start, stop, reads, writes):
            A("pe", lambda e: e.matmul(o_, lhsT=lhsT, rhs=rhs, start=start, stop=stop), reads=reads, writes=writes)

        def tr(o_, in_, reads, writes):
            A("pe", lambda e: e.transpose(out=o_, in_=in_, identity=ident), reads=reads, writes=writes)

        def act(o_, in_, func, reads, writes, **kw):
            A("act", lambda e: e.activation(out=o_, in_=in_, func=func, **kw), reads=reads, writes=writes)

        def acopy(o_, in_, reads, writes):
            A("act", lambda e: e.copy(out=o_, in_=in_), reads=reads, writes=writes)

        def vcopy(o_, in_, reads, writes):
            A("dve", lambda e: e.tensor_copy(out=o_, in_=in_), reads=reads, writes=writes)

        def tt_(eng, o_, in0, in1, op, reads, writes):
            A(eng, lambda e: e.tensor_tensor(out=o_, in0=in0, in1=in1, op=op), reads=reads, writes=writes)

        def ts_(eng, o_, in0, s1, s2, op0, op1, reads, writes):
            if s2 is None:
                A(eng, lambda e: e.tensor_scalar(out=o_, in0=in0, scalar1=s1, scalar2=None, op0=op0), reads=reads, writes=writes)
            else:
                A(eng, lambda e: e.tensor_scalar(out=o_, in0=in0, scalar1=s1, scalar2=s2, op0=op0, op1=op1), reads=reads, writes=writes)

        def stt_(o_, in0, sc_, in1, op0, op1, reads, writes):
            A("dve", lambda e: e.scalar_tensor_tensor(out=o_, in0=in0, scalar=sc_, in1=in1, op0=op0, op1=op1), reads=reads, writes=writes)

        XI = SCR
        XV = SCR + 4 * K
        wblk = reg(SCR + 8 * K, 8 * K, BF16).rearrange("p (k c) -> p k c", c=512)

        def tokI(h):
            base = 64 * K + (h + 1) * 4 * K if h < 7 else XI
            return reg(base, 4 * K, BF16).rearrange("p (b c) -> p b c", c=128)

        def tokV(p):
            base = 32 * K + (p + 1) * 4 * K if p < 7 else XV
            return reg(base, 4 * K, BF16).rearrange("p (b c) -> p b c", c=128)

        keyI = lambda h: ("osbc", h + 1) if h < 7 else ("XI",)
        keyV = lambda p: ("ohgc", p + 1) if p < 7 else ("XV",)

        for gidx in range(4):
            A("pool", lambda e, gidx=gidx: e.dma_start(out=wblk, in_=wiv_d[gidx]), writes=["wblk"], dma_key="wblk")
            for tb in range(16):
                bk = tb % 2
                for kc in range(8):
                    mm(banks[bk][:, :], hT[:, kc, tb * 128:(tb + 1) * 128], wblk[:, kc, :], kc == 0, kc == 7,
                       [("hT", tb), "wblk"], [("ps", bk)])
                for j in range(4):
                    u = 4 * (gidx % 2) + j
                    dst = (tokI(u) if gidx < 2 else tokV(u))[:, tb, :]
                    kk_ = keyI(u) if gidx < 2 else keyV(u)
                    acopy(dst, banks[bk][:, j * 128:(j + 1) * 128], [("ps", bk)], [kk_])
        P.barrier()
        if upto <= 2:
            return finish()

        PC = SCR + 8 * K
        b_ef, b_fp, b_g, b_kk, b_cum, b_dm, b_ep, b_em, b_sg = [reg(PC + i * 2 * K, 2 * K, F32) for i in range(9)]
        b_ln, b_rs, b_o1, b_eg = b_ef, b_fp, b_g, b_cum
        KEF, KFP, KG, KKK, KCUM, KDM, KEP, KEM, KSG = ["cb%d" % i for i in range(9)]
        o = PC + 18 * K
        wh = [reg(o + i * 6 * K, 6 * K, BF16).rearrange("p (k c) -> p k c", c=384) for i in range(2)]
        o += 12 * K
        qtl = reg(o, K, BF16)
        ktl = reg(o + K, K, BF16)
        ktokE = reg(o + 2 * K, K, BF16).rearrange("p (b d) -> p b d", d=128)
        ktokO = reg(o + 3 * K, K, BF16).rearrange("p (b d) -> p b d", d=128)
        A("pool", lambda e: e.memset(reg(PC + 30 * K + 2 * K, 2 * K, BF16), 0.0), writes=["ktokE", "ktokO"])
        o += 4 * K
        scm = [reg(o + i * K, K, BF16).rearrange("p (a t) -> p a t", t=64) for i in range(2)]
        o += 2 * K
        Sst = [reg(o + i * 512, 512, F32) for i in range(2)]
        o += K
        Pb = [reg(o + i * 256, 256, BF16) for i in range(4)]
        o += K
        sqb = reg(o, K, BF16)
        o += K
        assert o == PC + 39 * K
        BCP, BCS, BCO = 5, 6, 7

        def hgrn_gen(h):
            ws = h % 2
            if h == 0:
                A("pool", lambda e: e.dma_start(out=wh[0], in_=whg_d[0]), writes=[("wh", 0)], dma_key=("wh", 0))
            if h + 1 < 8:
                nws = (h + 1) % 2
                A("pool", lambda e: e.dma_start(out=wh[nws], in_=whg_d[h + 1]), writes=[("wh", nws)], dma_key=("wh", nws))
            lbc = lb[:, h:h + 1]
            omlc = oml[:, h:h + 1]
            gcol = cf[:, CF_GHG + h:CF_GHG + h + 1]
            tokh = tokI(h)
            kI = keyI(h)
            PSP, PSS, PSO = [("ps", BCP)], [("ps", BCS)], [("ps", BCO)]
            cp, cs_, co = banks[BCP], banks[BCS], banks[BCO]
            cpb = banksb[BCP]

            def proj(c0, tt):
                for kc in range(8):
                    mm(cp[:, :], wh[ws][:, kc, c0:c0 + 128], hT[:, kc, tt * 512:(tt + 1) * 512], kc == 0, kc == 7,
                       hTk(tt) + [("wh", ws)], PSP)

            for tt in range(4):
                ts = slice(tt * 512, (tt + 1) * 512)
                proj(128, tt)
                yield
                act(b_ef, cp[:, :], AF.Exp, PSP, [KEF], scale=-1.0)
                yield
                ts_("dve", b_ef, b_ef, 1.0, None, ALU.add, None, [KEF], [KEF])
                yield
                A("dve", lambda e: e.reciprocal(out=b_fp, in_=b_ef), reads=[KEF], writes=[KFP])
                yield
                ts_("dve", b_fp, b_fp, omlc, lbc, ALU.mult, ALU.add, [KFP], [KFP])
                proj(0, tt)
                yield
                act(b_g, b_fp, AF.Ln, [KFP], [KG])
                ts_("pool", b_kk, b_fp, -1.0, 1.0, ALU.mult, ALU.add, [KFP], [KKK])
                yield
                A("dve", lambda e: e.tensor_tensor_scan(out=b_cum, data0=rstm, data1=b_g, initial=0.0, op0=ALU.mult, op1=ALU.add), reads=[KG], writes=[KCUM])
                yield
                cum3 = b_cum.rearrange("p (c t) -> p c t", t=64)
                last = cum3[:, :, 63:64]
                tt_("dve", b_dm.rearrange("p (c t) -> p c t", t=64), cum3, last.broadcast_to([128, 8, 64]), ALU.subtract, [KCUM], [KDM])
                act(el[:, tt * 8:(tt + 1) * 8].unsqueeze(2), last, AF.Exp, [KCUM], [("el", tt)])
                yield
                act(b_ep, b_dm, AF.Exp, [KDM], [KEP])
                act(b_em, b_dm, AF.Exp, [KDM], [KEM], scale=-1.0)
                yield
                tt_("dve", qtl, cp[:, :], b_ep, ALU.mult, PSP + [KEP], ["qtl"])
                yield
                tt_("dve", ktl, b_kk, b_em, ALU.mult, [KKK, KEM], ["ktl"])
                proj(256, tt)
                yield
                act(b_eg, cp[:, :], AF.Exp, PSP, [KCUM], scale=-1.0)
                yield
                ts_("dve", b_eg, b_eg, 1.0, None, ALU.add, None, [KCUM], [KCUM])
                for i in range(4):
                    tr(cpb[:, i * 128:(i + 1) * 128], ktl[:, i * 128:(i + 1) * 128], ["ktl"], PSP)
                yield
                A("dve", lambda e: e.reciprocal(out=b_sg, in_=b_eg), reads=[KCUM], writes=[KSG])
                acopy(ktokE[0:64, :, :], cpb[0:64, 0:512].rearrange("p (b d) -> p b d", d=128), PSP, ["ktokE"])
                yield
                vcopy(ktokO[64:128, :, :], cpb[64:128, 0:512].rearrange("p (b d) -> p b d", d=128), PSP, ["ktokO"])
                sc = alt()
                for c in range(8):
                    mm(cs_[:, c * 64:(c + 1) * 64], ktl[:, (c // 2) * 128:(c // 2 + 1) * 128], qtl[:, c * 64:(c + 1) * 64], True, True,
                       ["ktl", "qtl"], PSS)
                yield
                tt_("dve", scm[sc].rearrange("p (a b) t -> p a b t", b=2), cs_[:, :].rearrange("p (a b t) -> p a b t", b=2, t=64),
                    m64.unsqueeze(1).broadcast_to([128, 4, 2, 64]), ALU.mult, PSS, [("scm", sc)])
                yield
                for half in range(2):
                    for c in range(half * 4, half * 4 + 4):
                        cg = tt * 8 + c
                        tb = cg // 2
                        kt_ = ktokE if cg % 2 == 0 else ktokO
                        mm(cs_[:, (c % 4) * 128:(c % 4 + 1) * 128], kt_[:, c // 2, :], tokh[:, tb, :], True, True,
                           ["ktokE", "ktokO", kI], PSS)
                    yield
                    for c in range(half * 4, half * 4 + 4):
                        cg = tt * 8 + c
                        tb = cg // 2
                        usrc = cs_[:, (c % 4) * 128:(c % 4 + 1) * 128]
                        sn, so = cg % 2, (cg + 1) % 2
                        pbi = cg % 4
                        elc = el[:, cg:cg + 1]
                        if cg == 0:
                            vcopy(Sst[sn], usrc, PSS, [("S", sn)])
                        else:
                            ts_("dve", Pb[pbi], Sst[so], elc, None, ALU.mult, None, [("S", so), ("el", tt)], [("Pb", pbi)])
                            stt_(Sst[sn], Sst[so], elc, usrc, ALU.mult, ALU.add, [("S", so), ("el", tt)] + PSS, [("S", sn)])
                        mm(co[:, c * 64:(c + 1) * 64], tokh[:, tb, :], scm[sc][:, c, :], True, cg == 0, [kI, ("scm", sc)], PSO)
                        if cg > 0:
                            mm(co[:, c * 64:(c + 1) * 64], Pb[pbi], qtl[:, c * 64:(c + 1) * 64], False, True, [("Pb", pbi), "qtl"], PSO)
                        yield
                act(sqb, co[:, :], AF.Square, PSO, ["sqb"])
                yield
                mm(cp[:, :], ones, sqb, True, True, ["sqb"], PSP)
                yield
                act(b_ln, cp[:, :], AF.Ln, PSP, [KEF], bias=epsc, scale=1.0 / 128)
                yield
                act(b_rs, b_ln, AF.Exp, [KEF], [KFP], scale=-0.5)
                yield
                tt_("dve", b_o1, co[:, :], b_rs, ALU.mult, PSO + [KFP], [KG])
                yield
                stt_(ohgT[:, h, ts], b_o1, gcol, b_sg, ALU.mult, ALU.mult, [KG, KSG], [("ohgc", h)])
                yield

        PD = PC + 39 * K
        o = PD
        wq = [reg(o + i * 4 * K, 4 * K, BF16).rearrange("p (k c) -> p k c", c=256) for i in range(2)]
        o += 8 * K
        qnT = reg(o, 4 * K, BF16)
        nk = [reg(o + 4 * K, 4 * K, BF16), reg(o + 8 * K, 4 * K, BF16)]
        A("pool", lambda e: e.memset(reg(PD + 8 * K + 4 * K, 8 * K, BF16), 0.0), writes=[("qk", 1, i) for i in range(4)])
        o += 12 * K
        d_ln = reg(o, 2 * K, F32)
        d_rs = reg(o + 2 * K, 2 * K, F32)
        o += 4 * K
        d_sq = reg(o, 1 * K, BF16)
        o += 1 * K
        EbBase = o
        Eb = [reg(o + i * 2 * K, 2 * K, F32) for i in range(2)]
        o += 4 * K
        SPb = [reg(o + i * K, K, BF16) for i in range(3)]
        SPb.append(reg(EbBase, K, BF16))
        o += 3 * K
        Ab = [reg(o + i * K, K, BF16) for i in range(3)]
        Ab.append(reg(EbBase + 2 * K, K, BF16))
        o += 3 * K
        Ssum = [reg(o + i * K, K, BF16) for i in range(4)]
        o += 4 * K
        assert o <= CO, o

        def attn_pair(p):
            gen = hgrn_gen(p)

            def pull(k):
                for _ in range(k):
                    try:
                        next(gen)
                    except StopIteration:
                        return

            ws = p % 2
            if p == 0:
                A("pool", lambda e: e.dma_start(out=wq[0], in_=wsb_d[0]), writes=[("wq", 0)], dma_key=("wq", 0))
            if p + 1 < 8:
                nws = (p + 1) % 2
                A("pool", lambda e: e.dma_start(out=wq[nws], in_=wsb_d[p + 1]), writes=[("wq", nws)], dma_key=("wq", nws))
            tokp = tokV(p)
            kV = keyV(p)
            for which, gcolv in ((0, gq8), (1, ngk)):
                gc = gcolv[:, p:p + 1]
                for tt in range(4):
                    ts = slice(tt * 512, (tt + 1) * 512)
                    rb = (which * 4 + tt) % 2
                    sb_ = 1 - rb
                    for kc in range(8):
                        mm(banks[rb][:, :], wq[ws][:, kc, which * 128:(which + 1) * 128], hT[:, kc, ts], kc == 0, kc == 7,
                           hTk(tt) + [("wq", ws)], [("ps", rb)])
                    pull(1)
                    act(d_sq, banks[rb][:, :], AF.Square, [("ps", rb)], ["dsq"])
                    pull(1)
                    mm(banks[sb_][:, :], blockones, d_sq, True, True, ["dsq"], [("ps", sb_)])
                    pull(1)
                    act(d_ln, banks[sb_][:, :], AF.Ln, [("ps", sb_)], ["dln"], bias=epsc, scale=1.0 / 64)
                    pull(1)
                    act(d_rs, d_ln, AF.Exp, ["dln"], ["drs"], scale=-0.5)
                    pull(1)
                    if which == 0:
                        stt_(qnT[:, ts], banks[rb][:, :], gc, d_rs, ALU.mult, ALU.mult, [("ps", rb), "drs"], [("qk", 0, tt)])
                    else:
                        for hd in range(2):
                            ps_ = slice(hd * 64, (hd + 1) * 64)
                            stt_(nk[hd][ps_, ts], banks[rb][ps_, :], gc[ps_, :], d_rs[ps_, :], ALU.mult, ALU.mult, [("ps", rb), "drs"], [("qk", 1, tt)])
                    pull(1)
            steps = []
            for qt in range(4):
                for head in range(2):
                    for kb in range(4 * qt + 3, -1, -1):
                        steps.append((head, qt, kb))
            n = len(steps)
            info = {}
            rs_slot = {}
            ws_slot = {}
            cur = 0
            for i_, (head_, qt_, kb_) in enumerate(steps):
                first_ = kb_ == 4 * qt_ + 3
                if first_:
                    cur = (cur + 1) % 4
                rs_slot[i_] = cur
                if kb_ - 4 * qt_ <= 0 and not first_:
                    cur = (cur + 1) % 4
                ws_slot[i_] = cur

            def s1(i):
                head, qt, kb = steps[i]
                d = kb - 4 * qt
                c0 = max(d, 0) * 128
                diag = d >= 0
                zb = i % 2
                sp_i = i % 4
                info[i] = (c0, diag, zb, sp_i)
                qkeys = [("qk", 0, qt), ("qk", 1, kb // 4)]
                lhs = nk[head][:, kb * 128:(kb + 1) * 128]
                rhs = qnT[:, qt * 512 + c0:(qt + 1) * 512]
                mm(banks[zb][:, c0:512], lhs, rhs, True, not diag, qkeys, [("ps", zb)])
                if diag:
                    mm(banks[zb][:, c0:c0 + 128], ident, posmask, False, True, [], [("ps", zb)])
                act(banks[zb][:, c0:512], banks[zb][:, c0:512], AF.Exp, [("ps", zb)], [("ps", zb)], scale=-1.0)
                act(SPb[sp_i][:, c0:512], banks[zb][:, c0:512], AF.Ln, [("ps", zb)], [("SP", sp_i)], bias=1.0)

            def s2(i):
                head, qt, kb = steps[i]
                c0, diag, zb, sp_i = info[i]
                first = kb == 4 * qt + 3
                cbk = 2 + i % 2
                ai = i % 4
                qkeys = [("qk", 0, qt), ("qk", 1, kb // 4)]
                lhs = nk[head][:, kb * 128:(kb + 1) * 128]
                rhs = qnT[:, qt * 512 + c0:(qt + 1) * 512]
                rs_, ws_ = rs_slot[i], ws_slot[i]
                mm(banks[cbk][:, c0:512], triincl, SPb[sp_i][:, c0:512], True, False, [("SP", sp_i)], [("ps", cbk)])
                if not first:
                    mm(banks[cbk][:, c0:512], ones, Ssum[rs_][:, c0:512], False, False, [("Ssum", rs_)], [("ps", cbk)])
                mm(banks[cbk][:, c0:512], lhs, rhs, False, not diag, qkeys, [("ps", cbk)])
                if diag:
                    mm(banks[cbk][:, c0:c0 + 128], ident, posmask, False, True, [], [("ps", cbk)])
                act(Ab[ai][:, c0:512], banks[cbk][:, c0:512], AF.Exp, [("ps", cbk)], [("A", ai)], scale=-1.0)
                if kb > 0:
                    if first:
                        A("pool", lambda e: e.memset(Ssum[ws_][:, 0:384], 0.0), writes=[("Ssum", ws_)])
                        A("pool", lambda e: e.tensor_copy(out=Ssum[ws_][:, 384:512], in_=SPb[sp_i][:, 384:512]), reads=[("SP", sp_i)], writes=[("Ssum", ws_)])
                    else:
                        tt_("dve", Ssum[ws_][:, c0:512], Ssum[rs_][:, c0:512], SPb[sp_i][:, c0:512], ALU.add, [("Ssum", rs_), ("SP", sp_i)], [("Ssum", ws_)])

            def s3(i):
                head, qt, kb = steps[i]
                c0, diag, zb, sp_i = info[i]
                hb = head * 64
                first = kb == 4 * qt + 3
                ai = i % 4
                okey = [("ps4", head)]
                if first:
                    mm(banks[4][hb:hb + 64, :], zeros[:, 0:64], qnT[:, 0:512], True, False, [("qk", 0, 0)], okey)
                mm(banks[4][hb:hb + 64, c0:512], tokp[:, kb, head * 64:(head + 1) * 64], Ab[ai][:, c0:512], False, kb == 0, [kV, ("A", ai)], okey)
                if kb == 0:
                    dst = osbT[hb:hb + 64, p, qt * 512:(qt + 1) * 512]
                    if qt % 2 == 0:
                        vcopy(dst, banks[4][hb:hb + 64, :], okey, [("osbc", p)])
                    else:
                        acopy(dst, banks[4][hb:hb + 64, :], okey, [("osbc", p)])

            for i in range(n + 4):
                if i < n:
                    s1(i)
                pull(1)
                if 0 <= i - 2 < n:
                    s2(i - 2)
                if i % 3 == 0:
                    pull(1)
                if 0 <= i - 4 < n:
                    s3(i - 4)
            for _ in gen:
                pass

        try:
            for p in range(8):
                if p == 7:
                    wE0 = reg(198 * K, 8 * K, BF16).rearrange("p (k c) -> p k c", c=512)
                    A("pool", lambda e: e.dma_start(out=wE0, in_=we_d[0]), writes=[("wE", 0)], dma_key=("wE", 0))
                attn_pair(p)
                ck(2.1 + p * 0.1)
        except _Stop:
            return finish()
        P.barrier()
        if "ohg" in dumps:
            A("sp", lambda e: e.dma_start(out=dumps["ohg"], in_=reg(32 * K, 32 * K, BF16)), dma_key="dump")
            P.barrier()
        if "osb" in dumps:
            A("sp", lambda e: e.dma_start(out=dumps["osb"], in_=reg(64 * K, 32 * K, BF16)), dma_key="dump")
            P.barrier()
        if upto <= 3:
            return finish()
        mixT = reg(SCR, 32 * K, BF16).rearrange("p (k t) -> p k t", t=S)
        wE = [reg(198 * K, 8 * K, BF16).rearrange("p (k c) -> p k c", c=512),
              reg(SCR + 40 * K, 8 * K, BF16).rearrange("p (k c) -> p k c", c=512)]
        o = SCR + 48 * K
        e_sa = [reg(o + i * 2 * K, 2 * K, F32) for i in range(2)]
        e_m = [reg(o + 4 * K + i * 2 * K, 2 * K, F32) for i in range(2)]
        for nb in range(8):
            ws = nb % 2
            if nb > 0:
                A("pool", lambda e, nb=nb, ws=ws: e.dma_start(out=wE[ws], in_=we_d[nb]), writes=[("wE", ws)], dma_key=("wE", ws))
            if nb == 6:
                wo_pre = reg(SCR + 56 * K, 16 * K, BF16).rearrange("p (k c) -> p k c", c=1024)
                for cg in range(2):
                    A("pool", lambda e, cg=cg: e.dma_start(out=wo_pre[:, :, cg * 512:(cg + 1) * 512], in_=wo_d[cg]), writes=[("wo", cg)], dma_key=("wo", cg))
            for tt in range(4):
                ts = slice(tt * 512, (tt + 1) * 512)
                for half, (gc0, yc0, src, skey) in enumerate(((0, 256, ohgT, "ohg"), (128, 384, osbT, "osb"))):
                    gb = (4 * half) % 8
                    yb = gb + 1
                    gb2 = gb + (tt % 2) * 2
                    yb2 = yb + (tt % 2) * 2
                    for kc in range(8):
                        A("pe", lambda e, kc=kc, gc0=gc0, gb2=gb2, ts=ts, ws=ws: e.matmul(banks[gb2][:, :], lhsT=wE[ws][:, kc, gc0:gc0 + 128], rhs=hT[:, kc, ts], start=(kc == 0), stop=(kc == 7)),
                          reads=[("wE", ws)], writes=[("ps", gb2)])
                    A("act", lambda e, half=half, gb2=gb2: e.activation(out=e_sa[half], in_=banks[gb2][:, :], func=AF.Sigmoid), reads=[("ps", gb2)], writes=[("sa", half)])
                    for kc in range(8):
                        A("pe", lambda e, kc=kc, yc0=yc0, yb2=yb2, ts=ts, src=src, ws=ws: e.matmul(banks[yb2][:, :], lhsT=wE[ws][:, kc, yc0:yc0 + 128], rhs=src[:, kc, ts], start=(kc == 0), stop=(kc == 7)),
                          reads=[("wE", ws)], writes=[("ps", yb2)])
                    A("dve", lambda e, half=half, yb2=yb2: e.tensor_tensor(out=e_m[half], in0=banks[yb2][:, :], in1=e_sa[half], op=ALU.mult), reads=[("ps", yb2), ("sa", half)], writes=[("m", half)])
                A("pool", lambda e, nb=nb, ts=ts: e.tensor_tensor(out=mixT[:, nb, ts], in0=e_m[0], in1=e_m[1], op=ALU.add), reads=[("m", 0), ("m", 1)], writes=[("mix", nb, tt)])
        P.barrier()
        if "mix" in dumps:
            A("sp", lambda e: e.dma_start(out=dumps["mix"], in_=reg(SCR, 32 * K, BF16)), dma_key="dump")
            P.barrier()

        if upto <= 4:
            return finish()
        wfo = reg(0, 44 * K, BF16).rearrange("p (j c) -> p j c", c=1024)
        wo = reg(SCR + 56 * K, 16 * K, BF16).rearrange("p (k c) -> p k c", c=1024)
        actT = reg(60 * K, 22 * K, BF16).rearrange("p (j t) -> p j t", t=512)
        o = SCR + 32 * K
        x2t = reg(o, 16 * K, F32).rearrange("p (b c) -> p b c", c=1024)
        o += 16 * K
        h2Tt = reg(o, 8 * K, BF16).rearrange("p (k t) -> p k t", t=512)
        o += 8 * K
        wf = [reg(44 * K + i * 4 * K, 4 * K, BF16).rearrange("p (k c) -> p k c", c=256) for i in range(3)]
        xn2 = [reg(56 * K + i * 2 * K, 2 * K, BF16) for i in range(2)]
        o += 16 * K
        xr = [reg(o + i * 4 * K, 4 * K, F32) for i in range(2)]
        o += 8 * K
        ost = [reg(o + i * 4 * K, 4 * K, F32) for i in range(2)]
        o += 8 * K
        assert o <= CO, o
        junk2 = reg(82 * K, 2 * K, BF16)
        g_sg = [reg(84 * K + i * 2 * K, 2 * K, F32) for i in range(2)]

        wfo_keys = [("wfo", jj) for jj in range(0, NJ, 2)]
        wfo_todo = list(range(0, NJ, 2))

        def wfo_dma():
            if wfo_todo:
                jj = wfo_todo.pop(0)
                A("pool", lambda e: e.dma_start(out=wfo[:, jj:jj + 2, :], in_=wfo_d[:, jj:jj + 2, :]), writes=[("wfo", jj)], dma_key="wfo")

        x2b1 = [reg(88 * K, 8 * K, F32).rearrange("p (b c) -> p b c", c=1024),
                reg(198 * K, 8 * K, F32).rearrange("p (b c) -> p b c", c=1024)]

        def x2(s_, tbl):
            if s_ == 0:
                return x2t[:, tbl, :]
            return x2b1[tbl // 2][:, tbl % 2, :]

        wfc = [0]

        def Fmm(tt, tbl):
            tb = tt * 4 + tbl
            xs = tb % 2
            s_ = tt % 2
            A("sp", lambda e: e.dma_start(out=xr[xs], in_=x[tb * 128:(tb + 1) * 128, :]), writes=[("xr", xs)], dma_key=("xr", xs))
            for cg in range(2):
                bk = (tbl * 2 + cg) % 4
                cs = slice(cg * 512, (cg + 1) * 512)
                for kc in range(8):
                    mm(banks[bk][:, :], mixT[:, kc, tb * 128:(tb + 1) * 128], wo[:, kc, cs], kc == 0, kc == 7, [("wo", cg)], [("ps", bk)])
                tt_("dve", x2(s_, tbl)[:, cs], banks[bk][:, :], xr[xs][:, cs], ALU.add, [("ps", bk), ("xr", xs)], [("x2", s_, tbl, cg)])

        def Frms(tt, tbl):
            tb = tt * 4 + tbl
            xs = tb % 2
            s_ = tt % 2
            rms_rows(x2(s_, tbl), ss2[:, tb:tb + 1], ln2[:, tb:tb + 1], rstd2[:, tb:tb + 1], g2bc, xn2[xs], junk2,
                     [("x2", s_, tbl, 0), ("x2", s_, tbl, 1)], [("xn2", xs)])

        def Ftr(tt, tbl):
            tb = tt * 4 + tbl
            xs = tb % 2
            bk = 4 + tbl % 2
            for kc in range(8):
                tr(banksb[bk][:, kc * 128:(kc + 1) * 128], xn2[xs][:, kc * 128:(kc + 1) * 128], [("xn2", xs)], [("ps", bk)])
            acopy(h2Tt[:, :, tbl * 128:(tbl + 1) * 128], banksb[bk][:, :].rearrange("p (k t) -> p k t", t=128), [("ps", bk)], [("h2", tbl)])

        h2k = [("h2", i) for i in range(4)]
        actk = [("act", j) for j in range(NJ)]

        def Gin(tt):
            for j in range(NJ):
                ws = wfc[0] % 3
                wfc[0] += 1
                A("pool", lambda e, j=j, ws=ws: e.dma_start(out=wf[ws], in_=wfi_d[j]), writes=[("wf", ws)], dma_key=("wf", ws))
                par = j % 2
                gb_ = (0, 2)[par]
                ub_ = (1, 3)[par]
                for kc in range(8):
                    mm(banks[gb_][:, :], wf[ws][:, kc, 0:128], h2Tt[:, kc, :], kc == 0, kc == 7, h2k + [("wf", ws)], [("ps", gb_)])
                for kc in range(8):
                    mm(banks[ub_][:, :], wf[ws][:, kc, 128:256], h2Tt[:, kc, :], kc == 0, kc == 7, h2k + [("wf", ws)], [("ps", ub_)])
                act(g_sg[par], banks[gb_][:, :], AF.Silu, [("ps", gb_)], [("gsg", par)])
                tt_("dve", actT[:, j, :], banks[ub_][:, :], g_sg[par], ALU.mult, [("ps", ub_), ("gsg", par)], [("act", j)])
                if j % 2 == 1:
                    wfo_dma()

        def Gout(tt, tbl):
            tb = tt * 4 + tbl
            os_ = tb % 2
            s_ = tt % 2
            for cg in range(2):
                bk = 4 + (tbl * 2 + cg) % 4
                cs = slice(cg * 512, (cg + 1) * 512)
                for j in range(NJ):
                    mm(banks[bk][:, :], actT[:, j, tbl * 128:(tbl + 1) * 128], wfo[:, j, cs], j == 0, j == NJ - 1, actk + wfo_keys, [("ps", bk)])
                tt_("dve", ost[os_][:, cs], banks[bk][:, :], x2(s_, tbl)[:, cs], ALU.add, [("ps", bk), ("x2", s_, tbl, cg)], [("ost", os_)])
            A("sp", lambda e: e.dma_start(out=out[tb * 128:(tb + 1) * 128, :], in_=ost[os_]), reads=[("ost", os_)], writes=[("out", tb)], dma_key=("ost", os_))

        for tbl in range(4):
            Fmm(0, tbl)
        Frms(0, 0)
        Frms(0, 1)
        Ftr(0, 0)
        Frms(0, 2)
        Ftr(0, 1)
        Frms(0, 3)
        Ftr(0, 2)
        Ftr(0, 3)
        for tt in range(4):
            Gin(tt)
            n_ = tt + 1
            if n_ < 4:
                for tbl in range(4):
                    Fmm(n_, tbl)
                Frms(n_, 0)
                Frms(n_, 1)
                Gout(tt, 0)
                Ftr(n_, 0)
                Ftr(n_, 1)
                Frms(n_, 2)
                Frms(n_, 3)
                Gout(tt, 1)
                Gout(tt, 2)
                Ftr(n_, 2)
                Ftr(n_, 3)
                Gout(tt, 3)
            else:
                for tbl in range(4):
                    Gout(tt, tbl)
        P.barrier()
        A("sp", lambda e: e.nop())
        P.emit()
    return nc


def _blk(w, cols):
    return np.ascontiguousarray(w[:, cols].reshape(8, 128, -1).transpose(1, 0, 2))


def _prep_weights(w_in, w_hg_out, w_sb_out, w_o, w_ffn_in, w_ffn_out):
    ar = np.arange
    wiv = np.stack([_blk(w_in, ar(2048 + g * 512, 2048 + (g + 1) * 512)) for g in range(2)] +
                   [_blk(w_in, ar(6144 + g * 512, 6144 + (g + 1) * 512)) for g in range(2)])
    whg = np.stack([_blk(w_in, np.concatenate([ar(h * 128, (h + 1) * 128), 1024 + ar(h * 128, (h + 1) * 128), 3072 + ar(h * 128, (h + 1) * 128)])) for h in range(8)])
    wsb = np.stack([_blk(w_in, np.concatenate([4096 + ar(p * 128, (p + 1) * 128), 5120 + ar(p * 128, (p + 1) * 128)])) for p in range(8)])
    we = np.stack([np.concatenate([_blk(w_in, np.concatenate([7168 + ar(nb * 128, (nb + 1) * 128), 8192 + ar(nb * 128, (nb + 1) * 128)])),
                                   _blk(w_hg_out, ar(nb * 128, (nb + 1) * 128)), _blk(w_sb_out, ar(nb * 128, (nb + 1) * 128))], axis=2) for nb in range(8)])
    wo = np.stack([_blk(w_o, ar(cg * 512, (cg + 1) * 512)) for cg in range(2)])
    wfi = np.stack([_blk(w_ffn_in, np.concatenate([ar(j * 128, (j + 1) * 128), FF + ar(j * 128, (j + 1) * 128)])) for j in range(NJ)])
    wfo = np.ascontiguousarray(w_ffn_out.reshape(NJ, 128, 1024).transpose(1, 0, 2))
    return dict(wiv=wiv, whg=whg, wsb=wsb, we=we, wo=wo, wfi=wfi, wfo=wfo)


def _consts(norm1_gain, norm2_gain, lb_logits, hg_out_norm, sb_q_norm, sb_k_norm):
    cf = np.zeros((128, CF_N), np.float32)
    cf[:, CF_G1:CF_G1 + 1024] = norm1_gain.reshape(1, 1024)
    cf[:, CF_G2:CF_G2 + 1024] = norm2_gain.reshape(1, 1024)
    t = np.arange(512)
    cf[:, CF_RST:CF_RST + 512] = (t % 64 != 0).astype(np.float32)[None, :]
    pp = np.arange(128)[:, None]
    tq = np.arange(64)[None, :]
    cf[:, CF_M64:CF_M64 + 64] = ((pp < 64) & (pp <= tq)).astype(np.float32)
    cf[:, CF_M64 + 64:CF_M64 + 128] = ((pp >= 64) & (pp - 64 <= tq)).astype(np.float32)
    cf[:, CF_LBL:CF_LBL + 16] = lb_logits.reshape(2, 8, 128).transpose(2, 0, 1).reshape(128, 16)
    cf[:, CF_GQ:CF_GQ + 8] = sb_q_norm.reshape(8, 128).T
    cf[:, CF_GK:CF_GK + 8] = sb_k_norm.reshape(8, 128).T
    cf[:, CF_GHG:CF_GHG + 8] = hg_out_norm.reshape(8, 128).T
    cf[:, CF_EPS] = EPS
    cf[:, CF_ONE] = 1.0
    cb = np.zeros((128, CB_N), np.float32)
    j = np.arange(128)[:, None]
    s = np.arange(128)[None, :]
    cb[:, 0:128] = (j == s)
    cb[:, 128:256] = (j >= s)
    cb[:, 256:384] = 1.0
    cb[:, 384:512] = (j // 64 == s // 64)
    cb[:, 512:640] = BIG * (j >= s)
    return cf, cb.astype(ml_dtypes.bfloat16)


_NC_CACHE = {}


def kernel(x, norm1_gain, w_in, lb_logits, hg_out_norm, sb_q_norm, sb_k_norm,
           w_hg_out, w_sb_out, w_o, norm2_gain, w_ffn_in, w_ffn_out, _dump=None, _upto=99, _trace=False):
    f = lambda a: np.asarray(a, dtype=np.float32)
    x = f(x)
    wd = _prep_weights(f(w_in)[0], f(w_hg_out)[0], f(w_sb_out)[0], f(w_o)[0], f(w_ffn_in)[0], f(w_ffn_out)[0])
    cf, cb = _consts(f(norm1_gain), f(norm2_gain), f(lb_logits), f(hg_out_norm), f(sb_q_norm), f(sb_k_norm))
    key = (tuple(_dump or ()), _upto)
    if key not in _NC_CACHE:
        _NC_CACHE[key] = build_nc(_dump, _upto)
    nc = _NC_CACHE[key]
    in_maps = []
    for b in range(8):
        m = dict(wd)
        m["x"] = np.ascontiguousarray(x[b])
        m["cf"] = cf
        m["cb"] = cb
        in_maps.append(m)
    if _trace:
        res = run_bass_kernel_spmd(nc, in_maps, core_ids=list(range(8)), trace=True)
        print('exec_time_ns', res.exec_time_ns)
    else:
        res = run_bass_kernel_spmd(nc, in_maps, core_ids=list(range(8)))
    outp = np.stack([res.results[b]["out"] for b in range(8)]).astype(np.float32)
    if _dump:
        return outp, res.results
    return outp
```

```python
import contextlib
import numpy as np
import ml_dtypes
import concourse.bass as bass
import concourse.mybir as mybir
from concourse.bass_utils import run_bass_kernel_spmd

F32 = mybir.dt.float32
BF16 = mybir.dt.bfloat16
AF = mybir.ActivationFunctionType
ALU = mybir.AluOpType

ENGS = ("sp", "act", "dve", "pool", "pe")
S = 2048
D = 1024
FF = 2816
NJ = 22
EPS = 1e-6
BIG = 30000.0


class _Stop(Exception):
    pass


class Op:
    __slots__ = ("eng", "fn", "deps", "signal", "sigval", "dma", "dsem", "dval")

    def __init__(self, eng, fn, dma=False):
        self.eng = eng
        self.fn = fn
        self.deps = []
        self.signal = False
        self.sigval = 0
        self.dma = dma
        self.dsem = None
        self.dval = 0


class Prog:
    def __init__(self, nc):
        self.nc = nc
        self.ops = {e: [] for e in ENGS}
        self.last_w = {}
        self.readers = {}
        self.dma_keys = {}
        self.barrier_deps = {e: [] for e in ENGS}
        self.dma_since_barrier = []

    @staticmethod
    def _need(d, op, raw):
        if d is op:
            return False
        if d.dma or op.dma:
            return True
        if d.eng == op.eng:
            if op.eng == "pe":
                return False
            return raw
        return True

    def add(self, eng, fn, reads=(), writes=(), dma_key=None):
        op = Op(eng, fn, dma=dma_key is not None)
        deps = {}
        for k in reads:
            w = self.last_w.get(k)
            if w is not None:
                deps[id(w)] = (w, True)
        for k in writes:
            w = self.last_w.get(k)
            if w is not None and id(w) not in deps:
                deps[id(w)] = (w, False)
            for r in self.readers.get(k, ()):
                if id(r) not in deps:
                    deps[id(r)] = (r, False)
        for d in self.barrier_deps[eng]:
            if id(d) not in deps:
                deps[id(d)] = (d, True)
        self.barrier_deps[eng] = []
        for d, raw in deps.values():
            if self._need(d, op, raw):
                op.deps.append(d)
                d.signal = True
        for k in reads:
            self.readers.setdefault(k, []).append(op)
        for k in writes:
            self.last_w[k] = op
            self.readers[k] = []
        if op.dma:
            cnt = self.dma_keys.setdefault(dma_key, [0])
            cnt[0] += 16
            op.dsem = dma_key
            op.dval = cnt[0]
            self.dma_since_barrier.append(op)
        self.ops[eng].append(op)
        return op

    def barrier(self):
        lasts = [self.ops[e][-1] for e in ENGS if self.ops[e]]
        lasts += self.dma_since_barrier
        self.dma_since_barrier = []
        for e in ENGS:
            self.barrier_deps[e] = list(lasts)
        self.last_w = {}
        self.readers = {}

    def emit(self):
        nc = self.nc
        for e in ENGS:
            cnt = 0
            for op in self.ops[e]:
                if op.signal and not op.dma:
                    cnt += 1
                    op.sigval = cnt
        with contextlib.ExitStack() as st:
            esem = {e: st.enter_context(nc.semaphore("s_" + e)) for e in ENGS}
            dsem = {k: st.enter_context(nc.semaphore("d_%d" % i))
                    for i, k in enumerate(self.dma_keys)}
            block = st.enter_context(nc.Block())
            prog = self

            def run(engname, engine):
                waited = {}
                for op in prog.ops[engname]:
                    need = {}
                    for d in op.deps:
                        if d.dma:
                            s, v = dsem[d.dsem], d.dval
                        else:
                            s, v = esem[d.eng], d.sigval
                        if need.get(id(s), (None, 0))[1] < v:
                            need[id(s)] = (s, v)
                    for key, (s, v) in need.items():
                        if waited.get(key, 0) >= v:
                            continue
                        waited[key] = v
                        engine.wait_ge(s, v)
                    inst = op.fn(engine)
                    if op.dma:
                        inst.then_inc(dsem[op.dsem], 16)
                    elif op.signal:
                        inst.then_inc(esem[engname], 1)

            @block.sync
            def _(eng):
                run("sp", eng)

            @block.scalar
            def _(eng):
                run("act", eng)

            @block.vector
            def _(eng):
                run("dve", eng)

            @block.gpsimd
            def _(eng):
                run("pool", eng)

            @block.tensor
            def _(eng):
                run("pe", eng)


CF_G1 = 0
CF_G2 = 1024
CF_RST = 2048
CF_M64 = 2560
CF_LBL = 2688
CF_GQ = 2704
CF_GK = 2712
CF_GHG = 2720
CF_EPS = 2728
CF_ONE = 2729
CF_N = 2736
CB_N = 6 * 128


def build_nc(dump=None, upto=99):
    dump = dump or ()
    nc = bass.Bass("TRN2", target_bir_lowering=False)
    dram = lambda n, s, dt=F32: nc.dram_tensor(n, s, dt, kind="ExternalInput").ap()
    x = dram("x", [S, D])
    cf_d = dram("cf", [128, CF_N])
    cb_d = dram("cb", [128, CB_N], BF16)
    wiv_d = dram("wiv", [4, 128, 8, 512])
    whg_d = dram("whg", [8, 128, 8, 384])
    wsb_d = dram("wsb", [8, 128, 8, 256])
    we_d = dram("we", [8, 128, 8, 512])
    wo_d = dram("wo", [2, 128, 8, 512])
    wfi_d = dram("wfi", [NJ, 128, 8, 256])
    wfo_d = dram("wfo", [128, NJ, 1024])
    out = nc.dram_tensor("out", [S, D], F32, kind="ExternalOutput").ap()
    dumps = {}
    for name in dump:
        shp = {"hT": [128, 8 * S], "ohg": [128, 8 * S], "osb": [128, 8 * S],
               "mix": [128, 8 * S]}[name]
        dumps[name] = nc.dram_tensor("dump_" + name, shp, BF16, kind="ExternalOutput").ap()

    with contextlib.ExitStack() as st:
        ARENA = 206 * 1024
        arena = st.enter_context(nc.sbuf_tensor("arena", [128, ARENA // 2], BF16))
        banks = [st.enter_context(nc.psum_tensor("ps%d" % i, [128, 512], F32)) for i in range(8)]
        banksb = [b.bitcast(BF16) for b in banks]

        def reg(off, nbytes, dt):
            assert off % 4 == 0 and off + nbytes <= ARENA, (off, nbytes)
            a = arena[:, off // 2:(off + nbytes) // 2]
            return a if dt == BF16 else a.bitcast(dt)

        K = 1024
        P = Prog(nc)
        A = P.add

        def ck(v):
            if upto <= v:
                raise _Stop()

        def finish():
            P.barrier()
            A("sp", lambda e: e.nop())
            P.emit()
            return nc
        cnt = [0]

        def alt(n=2):
            cnt[0] += 1
            return cnt[0] % n

        hT = reg(0, 32 * K, BF16).rearrange("p (k t) -> p k t", t=S)
        ohgT = reg(32 * K, 32 * K, BF16).rearrange("p (k t) -> p k t", t=S)
        osbT = reg(64 * K, 32 * K, BF16).rearrange("p (k t) -> p k t", t=S)
        CO = 184 * K
        cf = reg(CO, CF_N * 4, F32)
        cb = reg(CO + CF_N * 4, CB_N * 2, BF16)
        small = reg(CO + CF_N * 4 + CB_N * 2, 1024, F32)
        ident = cb[:, 0:128]
        triincl = cb[:, 128:256]
        ones = cb[:, 256:384]
        blockones = cb[:, 384:512]
        posmask = cb[:, 512:640]
        zeros = cb[:, 640:768]
        g1bc = cf[:, CF_G1:CF_G1 + 1024]
        g2bc = cf[:, CF_G2:CF_G2 + 1024]
        rstm = cf[:, CF_RST:CF_RST + 512]
        m64 = cf[:, CF_M64:CF_M64 + 128].rearrange("p (a t) -> p a t", t=64)
        epsc = cf[:, CF_EPS:CF_EPS + 1]
        onec = cf[:, CF_ONE:CF_ONE + 1]
        ss1 = small[:, 0:16]
        ln1 = small[:, 16:32]
        rstd1 = small[:, 32:48]
        lb = small[:, 48:56]
        oml = small[:, 56:64]
        gq8 = small[:, 64:72]
        ngk = small[:, 72:80]
        tmp8 = small[:, 80:88]
        ss2 = small[:, 96:112]
        ln2 = small[:, 112:128]
        rstd2 = small[:, 128:144]
        el = small[:, 144:176]

        A("sp", lambda e: e.dma_start(out=cf, in_=cf_d), writes=["cf"], dma_key="cf")
        A("sp", lambda e: e.dma_start(out=cb, in_=cb_d), writes=["cb"], dma_key="cb")
        A("dve", lambda e: e.tensor_tensor(out=tmp8, in0=cf[:, CF_LBL + 8:CF_LBL + 16], in1=cf[:, CF_LBL:CF_LBL + 8], op=ALU.subtract), reads=["cf"], writes=["tmp8"])
        A("act", lambda e: e.activation(out=tmp8, in_=tmp8, func=AF.Exp), reads=["tmp8"], writes=["tmp8"])
        A("dve", lambda e: e.tensor_scalar(out=tmp8, in0=tmp8, scalar1=1.0, scalar2=None, op0=ALU.add), reads=["tmp8"], writes=["tmp8"])
        A("dve", lambda e: e.reciprocal(out=lb, in_=tmp8), reads=["tmp8"], writes=["lb"])
        A("dve", lambda e: e.tensor_scalar(out=oml, in0=lb, scalar1=-1.0, scalar2=1.0, op0=ALU.mult, op1=ALU.add), reads=["lb"], writes=["oml"])
        A("dve", lambda e: e.tensor_scalar(out=gq8, in0=cf[:, CF_GQ:CF_GQ + 8], scalar1=0.125, scalar2=None, op0=ALU.mult), reads=["cf"], writes=["gq8"])
        A("dve", lambda e: e.tensor_scalar(out=ngk, in0=cf[:, CF_GK:CF_GK + 8], scalar1=-1.0, scalar2=None, op0=ALU.mult), reads=["cf"], writes=["ngk"])
        P.barrier()

        SCR = 96 * K

        def rms_rows(src, sscol, lncol, rstdcol, gbc, xn_out, junk, kin, kout):
            A("act", lambda e: e.activation(out=junk, in_=src, func=AF.Square, accum_out=sscol),
              reads=kin, writes=["junk", ("st", id(sscol))])
            A("act", lambda e: e.activation(out=lncol, in_=sscol, func=AF.Ln, bias=epsc, scale=1.0 / D),
              reads=[("st", id(sscol))], writes=[("st", id(lncol))])
            A("act", lambda e: e.activation(out=rstdcol, in_=lncol, func=AF.Exp, scale=-0.5),
              reads=[("st", id(lncol))], writes=[("st", id(rstdcol))])
            A("dve", lambda e: e.scalar_tensor_tensor(out=xn_out, in0=src, scalar=rstdcol, in1=gbc, op0=ALU.mult, op1=ALU.mult),
              reads=kin + [("st", id(rstdcol))], writes=kout)

        AS = SCR + 48 * K
        xt = [reg(AS + i * 4 * K, 4 * K, F32) for i in range(3)]
        xn = [reg(AS + 12 * K + i * 2 * K, 2 * K, BF16) for i in range(2)]
        junk = reg(AS + 16 * K, 2 * K, BF16)
        for tb in range(16):
            s3 = tb % 3
            s2 = tb % 2
            bk = tb % 2
            A("sp", lambda e, tb=tb, s3=s3: e.dma_start(out=xt[s3], in_=x[tb * 128:(tb + 1) * 128, :]),
              writes=[("xt", s3)], dma_key=("xt", s3))
            rms_rows(xt[s3], ss1[:, tb:tb + 1], ln1[:, tb:tb + 1], rstd1[:, tb:tb + 1], g1bc, xn[s2], junk,
                     [("xt", s3)], [("xn", s2)])
            for kc in range(8):
                A("pe", lambda e, kc=kc, s2=s2, bk=bk: e.transpose(out=banksb[bk][:, kc * 128:(kc + 1) * 128], in_=xn[s2][:, kc * 128:(kc + 1) * 128], identity=ident),
                  reads=[("xn", s2)], writes=[("ps", bk)])
            src = banksb[bk][:, :].rearrange("p (k t) -> p k t", t=128)
            dst = hT[:, :, tb * 128:(tb + 1) * 128]
            if tb % 2 == 0:
                A("act", lambda e, src=src, dst=dst: e.copy(out=dst, in_=src), reads=[("ps", bk)], writes=[("hT", tb)])
            else:
                A("dve", lambda e, src=src, dst=dst: e.tensor_copy(out=dst, in_=src), reads=[("ps", bk)], writes=[("hT", tb)])
        if "hT" in dumps or upto <= 1:
            P.barrier()
        if "hT" in dumps:
            A("sp", lambda e: e.dma_start(out=dumps["hT"], in_=reg(0, 32 * K, BF16)), dma_key="dump")
            P.barrier()

        if upto <= 1:
            return finish()
        hTk = lambda tt: [("hT", tt * 4 + i) for i in range(4)]

        def mm(o_, lhsT, rhs, start, stop, reads, writes):
            A("pe", lambda e: e.matmul(o_, lhsT=lhsT, rhs=rhs, start=start, stop=stop), reads=reads, writes=writes)

        def tr(o_, in_, reads, writes):
            A("pe", lambda e: e.transpose(out=o_, in_=in_, identity=ident), reads=reads, writes=writes)

        def act(o_, in_, func, reads, writes, **kw):
            A("act", lambda e: e.activation(out=o_, in_=in_, func=func, **kw), reads=reads, writes=writes)

        def acopy(o_, in_, reads, writes):
            A("act", lambda e: e.copy(out=o_, in_=in_), reads=reads, writes=writes)

        def vcopy(o_, in_, reads, writes):
            A("dve", lambda e: e.tensor_copy(out=o_, in_=in_), reads=reads, writes=writes)

        def tt_(eng, o_, in0, in1, op, reads, writes):
            A(eng, lambda e: e.tensor_tensor(out=o_, in0=in0, in1=in1, op=op), reads=reads, writes=writes)

        def ts_(eng, o_, in0, s1, s2, op0, op1, reads, writes):
            if s2 is None:
                A(eng, lambda e: e.tensor_scalar(out=o_, in0=in0, scalar1=s1, scalar2=None, op0=op0), reads=reads, writes=writes)
            else:
                A(eng, lambda e: e.tensor_scalar(out=o_, in0=in0, scalar1=s1, scalar2=s2, op0=op0, op1=op1), reads=reads, writes=writes)

        def stt_(o_, in0, sc_, in1, op0, op1, reads, writes):
            A("dve", lambda e: e.scalar_tensor_tensor(out=o_, in0=in0, scalar=sc_, in1=in1, op0=op0, op1=op1), reads=reads, writes=writes)

        XI = SCR
        XV = SCR + 4 * K
        wblk = reg(SCR + 8 * K, 8 * K, BF16).rearrange("p (k c) -> p k c", c=512)

        def tokI(h):
            base = 64 * K + (h + 1) * 4 * K if h < 7 else XI
            return reg(base, 4 * K, BF16).rearrange("p (b c) -> p b c", c=128)

        def tokV(p):
            base = 32 * K + (p + 1) * 4 * K if p < 7 else XV
            return reg(base, 4 * K, BF16).rearrange("p (b c) -> p b c", c=128)

        keyI = lambda h: ("osbc", h + 1) if h < 7 else ("XI",)
        keyV = lambda p: ("ohgc", p + 1) if p < 7 else ("XV",)

        for gidx in range(4):
            A("pool", lambda e, gidx=gidx: e.dma_start(out=wblk, in_=wiv_d[gidx]), writes=["wblk"], dma_key="wblk")
            if gidx == 0:
                wh0_pre = reg(SCR + 26 * K, 6 * K, BF16).rearrange("p (k c) -> p k c", c=384)
                A("pool", lambda e: e.dma_start(out=wh0_pre, in_=whg_d[0]), writes=[("wh", 0)], dma_key=("wh", 0))
            for tb in range(16):
                bk = tb % 2
                for kc in range(8):
                    mm(banks[bk][:, :], hT[:, kc, tb * 128:(tb + 1) * 128], wblk[:, kc, :], kc == 0, kc == 7,
                       [("hT", tb), "wblk"], [("ps", bk)])
                for j in range(4):
                    u = 4 * (gidx % 2) + j
                    dst = (tokI(u) if gidx < 2 else tokV(u))[:, tb, :]
                    kk_ = keyI(u) if gidx < 2 else keyV(u)
                    acopy(dst, banks[bk][:, j * 128:(j + 1) * 128], [("ps", bk)], [kk_])
        P.barrier()
        if upto <= 2:
            return finish()

        PC = SCR + 8 * K
        b_ef, b_fp, b_g, b_kk, b_cum, b_dm, b_ep, b_em, b_sg = [reg(PC + i * 2 * K, 2 * K, F32) for i in range(9)]
        b_ln, b_rs, b_o1, b_eg = b_ef, b_fp, b_g, b_cum
        KEF, KFP, KG, KKK, KCUM, KDM, KEP, KEM, KSG = ["cb%d" % i for i in range(9)]
        o = PC + 18 * K
        wh = [reg(o + i * 6 * K, 6 * K, BF16).rearrange("p (k c) -> p k c", c=384) for i in range(2)]
        o += 12 * K
        qtl = reg(o, K, BF16)
        ktl = reg(o + K, K, BF16)
        ktokE = reg(o + 2 * K, K, BF16).rearrange("p (b d) -> p b d", d=128)
        ktokO = reg(o + 3 * K, K, BF16).rearrange("p (b d) -> p b d", d=128)
        A("pool", lambda e: e.memset(reg(PC + 30 * K + 2 * K, 2 * K, BF16), 0.0), writes=["ktokE", "ktokO"])
        o += 4 * K
        scm = [reg(o + i * K, K, BF16).rearrange("p (a t) -> p a t", t=64) for i in range(2)]
        o += 2 * K
        Sst = [reg(o + i * 512, 512, F32) for i in range(2)]
        o += K
        Pb = [reg(o + i * 256, 256, BF16) for i in range(4)]
        o += K
        sqb = reg(o, K, BF16)
        o += K
        assert o == PC + 39 * K
        BCP, BCS, BCO = 5, 6, 7

        def hgrn_gen(h):
            ws = h % 2
            if h + 1 < 8:
                nws = (h + 1) % 2
                A("pool", lambda e: e.dma_start(out=wh[nws], in_=whg_d[h + 1]), writes=[("wh", nws)], dma_key=("wh", nws))
            lbc = lb[:, h:h + 1]
            omlc = oml[:, h:h + 1]
            gcol = cf[:, CF_GHG + h:CF_GHG + h + 1]
            tokh = tokI(h)
            kI = keyI(h)
            PSP, PSS, PSO = [("ps", BCP)], [("ps", BCS)], [("ps", BCO)]
            cp, cs_, co = banks[BCP], banks[BCS], banks[BCO]
            cpb = banksb[BCP]

            def proj(c0, tt):
                for kc in range(8):
                    mm(cp[:, :], wh[ws][:, kc, c0:c0 + 128], hT[:, kc, tt * 512:(tt + 1) * 512], kc == 0, kc == 7,
                       hTk(tt) + [("wh", ws)], PSP)

            for tt in range(4):
                ts = slice(tt * 512, (tt + 1) * 512)
                proj(128, tt)
                yield
                act(b_ef, cp[:, :], AF.Exp, PSP, [KEF], scale=-1.0)
                yield
                ts_("dve", b_ef, b_ef, 1.0, None, ALU.add, None, [KEF], [KEF])
                yield
                A("dve", lambda e: e.reciprocal(out=b_fp, in_=b_ef), reads=[KEF], writes=[KFP])
                yield
                ts_("dve", b_fp, b_fp, omlc, lbc, ALU.mult, ALU.add, [KFP], [KFP])
                proj(0, tt)
                yield
                act(b_g, b_fp, AF.Ln, [KFP], [KG])
                ts_("pool", b_kk, b_fp, -1.0, 1.0, ALU.mult, ALU.add, [KFP], [KKK])
                yield
                A("dve", lambda e: e.tensor_tensor_scan(out=b_cum, data0=rstm, data1=b_g, initial=0.0, op0=ALU.mult, op1=ALU.add), reads=[KG], writes=[KCUM])
                yield
                cum3 = b_cum.rearrange("p (c t) -> p c t", t=64)
                last = cum3[:, :, 63:64]
                tt_("dve", b_dm.rearrange("p (c t) -> p c t", t=64), cum3, last.broadcast_to([128, 8, 64]), ALU.subtract, [KCUM], [KDM])
                act(el[:, tt * 8:(tt + 1) * 8].unsqueeze(2), last, AF.Exp, [KCUM], [("el", tt)])
                yield
                act(b_ep, b_dm, AF.Exp, [KDM], [KEP])
                act(b_em, b_dm, AF.Exp, [KDM], [KEM], scale=-1.0)
                yield
                tt_("dve", qtl, cp[:, :], b_ep, ALU.mult, PSP + [KEP], ["qtl"])
                yield
                tt_("dve", ktl, b_kk, b_em, ALU.mult, [KKK, KEM], ["ktl"])
                proj(256, tt)
                yield
                act(b_eg, cp[:, :], AF.Exp, PSP, [KCUM], scale=-1.0)
                yield
                ts_("dve", b_eg, b_eg, 1.0, None, ALU.add, None, [KCUM], [KCUM])
                for i in range(4):
                    tr(cpb[:, i * 128:(i + 1) * 128], ktl[:, i * 128:(i + 1) * 128], ["ktl"], PSP)
                yield
                A("dve", lambda e: e.reciprocal(out=b_sg, in_=b_eg), reads=[KCUM], writes=[KSG])
                acopy(ktokE[0:64, :, :], cpb[0:64, 0:512].rearrange("p (b d) -> p b d", d=128), PSP, ["ktokE"])
                yield
                vcopy(ktokO[64:128, :, :], cpb[64:128, 0:512].rearrange("p (b d) -> p b d", d=128), PSP, ["ktokO"])
                sc = alt()
                for c in range(8):
                    mm(cs_[:, c * 64:(c + 1) * 64], ktl[:, (c // 2) * 128:(c // 2 + 1) * 128], qtl[:, c * 64:(c + 1) * 64], True, True,
                       ["ktl", "qtl"], PSS)
                yield
                tt_("dve", scm[sc].rearrange("p (a b) t -> p a b t", b=2), cs_[:, :].rearrange("p (a b t) -> p a b t", b=2, t=64),
                    m64.unsqueeze(1).broadcast_to([128, 4, 2, 64]), ALU.mult, PSS, [("scm", sc)])
                yield
                for half in range(2):
                    for c in range(half * 4, half * 4 + 4):
                        cg = tt * 8 + c
                        tb = cg // 2
                        kt_ = ktokE if cg % 2 == 0 else ktokO
                        mm(cs_[:, (c % 4) * 128:(c % 4 + 1) * 128], kt_[:, c // 2, :], tokh[:, tb, :], True, True,
                           ["ktokE", "ktokO", kI], PSS)
                    yield
                    for c in range(half * 4, half * 4 + 4):
                        cg = tt * 8 + c
                        tb = cg // 2
                        usrc = cs_[:, (c % 4) * 128:(c % 4 + 1) * 128]
                        sn, so = cg % 2, (cg + 1) % 2
                        pbi = cg % 4
                        elc = el[:, cg:cg + 1]
                        if cg == 0:
                            vcopy(Sst[sn], usrc, PSS, [("S", sn)])
                        else:
                            ts_("dve", Pb[pbi], Sst[so], elc, None, ALU.mult, None, [("S", so), ("el", tt)], [("Pb", pbi)])
                            stt_(Sst[sn], Sst[so], elc, usrc, ALU.mult, ALU.add, [("S", so), ("el", tt)] + PSS, [("S", sn)])
                        mm(co[:, c * 64:(c + 1) * 64], tokh[:, tb, :], scm[sc][:, c, :], True, cg == 0, [kI, ("scm", sc)], PSO)
                        if cg > 0:
                            mm(co[:, c * 64:(c + 1) * 64], Pb[pbi], qtl[:, c * 64:(c + 1) * 64], False, True, [("Pb", pbi), "qtl"], PSO)
                        yield
                act(sqb, co[:, :], AF.Square, PSO, ["sqb"])
                yield
                mm(cp[:, :], ones, sqb, True, True, ["sqb"], PSP)
                yield
                act(b_ln, cp[:, :], AF.Ln, PSP, [KEF], bias=epsc, scale=1.0 / 128)
                yield
                act(b_rs, b_ln, AF.Exp, [KEF], [KFP], scale=-0.5)
                yield
                tt_("dve", b_o1, co[:, :], b_rs, ALU.mult, PSO + [KFP], [KG])
                yield
                stt_(ohgT[:, h, ts], b_o1, gcol, b_sg, ALU.mult, ALU.mult, [KG, KSG], [("ohgc", h)])
                yield

        PD = PC + 39 * K
        o = PD
        wq = [reg(o + i * 4 * K, 4 * K, BF16).rearrange("p (k c) -> p k c", c=256) for i in range(2)]
        o += 8 * K
        qnT = reg(o, 4 * K, BF16)
        nk = [reg(o + 4 * K, 4 * K, BF16), reg(o + 8 * K, 4 * K, BF16)]
        A("pool", lambda e: e.memset(reg(PD + 8 * K + 4 * K, 8 * K, BF16), 0.0), writes=[("qk", 1, i) for i in range(4)])
        o += 12 * K
        d_ln = reg(o, 2 * K, F32)
        d_rs = reg(o + 2 * K, 2 * K, F32)
        o += 4 * K
        d_sq = reg(o, 1 * K, BF16)
        o += 1 * K
        EbBase = o
        Eb = [reg(o + i * 2 * K, 2 * K, F32) for i in range(2)]
        o += 4 * K
        SPb = [reg(o + i * K, K, BF16) for i in range(3)]
        SPb.append(reg(EbBase, K, BF16))
        o += 3 * K
        Ab = [reg(o + i * K, K, BF16) for i in range(3)]
        Ab.append(reg(EbBase + 2 * K, K, BF16))
        o += 3 * K
        Ssum = [reg(o + i * K, K, BF16) for i in range(4)]
        o += 4 * K
        assert o <= CO, o

        def attn_pair(p):
            gen = hgrn_gen(p)

            def pull(k):
                for _ in range(k):
                    try:
                        next(gen)
                    except StopIteration:
                        return

            ws = p % 2
            if p == 0:
                A("pool", lambda e: e.dma_start(out=wq[0], in_=wsb_d[0]), writes=[("wq", 0)], dma_key=("wq", 0))
            if p + 1 < 8:
                nws = (p + 1) % 2
                A("pool", lambda e: e.dma_start(out=wq[nws], in_=wsb_d[p + 1]), writes=[("wq", nws)], dma_key=("wq", nws))
            tokp = tokV(p)
            kV = keyV(p)
            for which, gcolv in ((0, gq8), (1, ngk)):
                gc = gcolv[:, p:p + 1]
                for tt in range(4):
                    ts = slice(tt * 512, (tt + 1) * 512)
                    rb = (which * 4 + tt) % 2
                    sb_ = 1 - rb
                    for kc in range(8):
                        mm(banks[rb][:, :], wq[ws][:, kc, which * 128:(which + 1) * 128], hT[:, kc, ts], kc == 0, kc == 7,
                           hTk(tt) + [("wq", ws)], [("ps", rb)])
                    pull(1)
                    act(d_sq, banks[rb][:, :], AF.Square, [("ps", rb)], ["dsq"])
                    pull(1)
                    mm(banks[sb_][:, :], blockones, d_sq, True, True, ["dsq"], [("ps", sb_)])
                    pull(1)
                    act(d_ln, banks[sb_][:, :], AF.Ln, [("ps", sb_)], ["dln"], bias=epsc, scale=1.0 / 64)
                    pull(1)
                    act(d_rs, d_ln, AF.Exp, ["dln"], ["drs"], scale=-0.5)
                    pull(1)
                    if which == 0:
                        stt_(qnT[:, ts], banks[rb][:, :], gc, d_rs, ALU.mult, ALU.mult, [("ps", rb), "drs"], [("qk", 0, tt)])
                    else:
                        for hd in range(2):
                            ps_ = slice(hd * 64, (hd + 1) * 64)
                            stt_(nk[hd][ps_, ts], banks[rb][ps_, :], gc[ps_, :], d_rs[ps_, :], ALU.mult, ALU.mult, [("ps", rb), "drs"], [("qk", 1, tt)])
                    pull(1)
            steps = []
            for qt in range(4):
                for head in range(2):
                    for kb in range(4 * qt + 3, -1, -1):
                        steps.append((head, qt, kb))
            n = len(steps)
            info = {}
            rs_slot = {}
            ws_slot = {}
            cur = 0
            for i_, (head_, qt_, kb_) in enumerate(steps):
                first_ = kb_ == 4 * qt_ + 3
                if first_:
                    cur = (cur + 1) % 4
                rs_slot[i_] = cur
                if kb_ - 4 * qt_ <= 0 and not first_:
                    cur = (cur + 1) % 4
                ws_slot[i_] = cur

            def s1(i):
                head, qt, kb = steps[i]
                d = kb - 4 * qt
                c0 = max(d, 0) * 128
                diag = d >= 0
                zb = i % 2
                sp_i = i % 4
                info[i] = (c0, diag, zb, sp_i)
                qkeys = [("qk", 0, qt), ("qk", 1, kb // 4)]
                lhs = nk[head][:, kb * 128:(kb + 1) * 128]
                rhs = qnT[:, qt * 512 + c0:(qt + 1) * 512]
                mm(banks[zb][:, c0:512], lhs, rhs, True, not diag, qkeys, [("ps", zb)])
                if diag:
                    mm(banks[zb][:, c0:c0 + 128], ident, posmask, False, True, [], [("ps", zb)])
                act(banks[zb][:, c0:512], banks[zb][:, c0:512], AF.Exp, [("ps", zb)], [("ps", zb)], scale=-1.0)
                act(SPb[sp_i][:, c0:512], banks[zb][:, c0:512], AF.Ln, [("ps", zb)], [("SP", sp_i)], bias=1.0)

            def s2(i):
                head, qt, kb = steps[i]
                c0, diag, zb, sp_i = info[i]
                first = kb == 4 * qt + 3
                cbk = 2 + i % 2
                ai = i % 4
                qkeys = [("qk", 0, qt), ("qk", 1, kb // 4)]
                lhs = nk[head][:, kb * 128:(kb + 1) * 128]
                rhs = qnT[:, qt * 512 + c0:(qt + 1) * 512]
                rs_, ws_ = rs_slot[i], ws_slot[i]
                mm(banks[cbk][:, c0:512], triincl, SPb[sp_i][:, c0:512], True, False, [("SP", sp_i)], [("ps", cbk)])
                if not first:
                    mm(banks[cbk][:, c0:512], ones, Ssum[rs_][:, c0:512], False, False, [("Ssum", rs_)], [("ps", cbk)])
                mm(banks[cbk][:, c0:512], lhs, rhs, False, not diag, qkeys, [("ps", cbk)])
                if diag:
                    mm(banks[cbk][:, c0:c0 + 128], ident, posmask, False, True, [], [("ps", cbk)])
                act(Ab[ai][:, c0:512], banks[cbk][:, c0:512], AF.Exp, [("ps", cbk)], [("A", ai)], scale=-1.0)
                if kb > 0:
                    if first:
                        A("pool", lambda e: e.memset(Ssum[ws_][:, 0:384], 0.0), writes=[("Ssum", ws_)])
                        A("pool", lambda e: e.tensor_copy(out=Ssum[ws_][:, 384:512], in_=SPb[sp_i][:, 384:512]), reads=[("SP", sp_i)], writes=[("Ssum", ws_)])
                    else:
                        tt_("dve", Ssum[ws_][:, c0:512], Ssum[rs_][:, c0:512], SPb[sp_i][:, c0:512], ALU.add, [("Ssum", rs_), ("SP", sp_i)], [("Ssum", ws_)])

            def s3(i):
                head, qt, kb = steps[i]
                c0, diag, zb, sp_i = info[i]
                hb = head * 64
                first = kb == 4 * qt + 3
                ai = i % 4
                okey = [("ps4", head)]
                if first:
                    mm(banks[4][hb:hb + 64, :], zeros[:, 0:64], qnT[:, 0:512], True, False, [("qk", 0, 0)], okey)
                mm(banks[4][hb:hb + 64, c0:512], tokp[:, kb, head * 64:(head + 1) * 64], Ab[ai][:, c0:512], False, kb == 0, [kV, ("A", ai)], okey)
                if kb == 0:
                    dst = osbT[hb:hb + 64, p, qt * 512:(qt + 1) * 512]
                    if qt % 2 == 0:
                        vcopy(dst, banks[4][hb:hb + 64, :], okey, [("osbc", p)])
                    else:
                        acopy(dst, banks[4][hb:hb + 64, :], okey, [("osbc", p)])

            for i in range(n + 4):
                if i < n:
                    s1(i)
                pull(1)
                if 0 <= i - 2 < n:
                    s2(i - 2)
                if i % 3 == 0:
                    pull(1)
                if 0 <= i - 4 < n:
                    s3(i - 4)
            for _ in gen:
                pass

        try:
            for p in range(8):
                if p == 7:
                    wE0 = reg(198 * K, 8 * K, BF16).rearrange("p (k c) -> p k c", c=512)
                    A("pool", lambda e: e.dma_start(out=wE0, in_=we_d[0]), writes=[("wE", 0)], dma_key=("wE", 0))
                attn_pair(p)
                ck(2.1 + p * 0.1)
        except _Stop:
            return finish()
        P.barrier()
        if "ohg" in dumps:
            A("sp", lambda e: e.dma_start(out=dumps["ohg"], in_=reg(32 * K, 32 * K, BF16)), dma_key="dump")
            P.barrier()
        if "osb" in dumps:
            A("sp", lambda e: e.dma_start(out=dumps["osb"], in_=reg(64 * K, 32 * K, BF16)), dma_key="dump")
            P.barrier()
        if upto <= 3:
            return finish()
        mixT = reg(SCR, 32 * K, BF16).rearrange("p (k t) -> p k t", t=S)
        wE = [reg(198 * K, 8 * K, BF16).rearrange("p (k c) -> p k c", c=512),
              reg(SCR + 40 * K, 8 * K, BF16).rearrange("p (k c) -> p k c", c=512)]
        o = SCR + 48 * K
        e_sa = [reg(o + i * 2 * K, 2 * K, F32) for i in range(2)]
        e_m = [reg(o + 4 * K + i * 2 * K, 2 * K, F32) for i in range(2)]
        for nb in range(8):
            ws = nb % 2
            if nb > 0:
                A("pool", lambda e, nb=nb, ws=ws: e.dma_start(out=wE[ws], in_=we_d[nb]), writes=[("wE", ws)], dma_key=("wE", ws))
            if nb == 6:
                wo_pre = reg(SCR + 56 * K, 16 * K, BF16).rearrange("p (k c) -> p k c", c=1024)
                for cg in range(2):
                    A("pool", lambda e, cg=cg: e.dma_start(out=wo_pre[:, :, cg * 512:(cg + 1) * 512], in_=wo_d[cg]), writes=[("wo", cg)], dma_key=("wo", cg))
            for tt in range(4):
                ts = slice(tt * 512, (tt + 1) * 512)
                for half, (gc0, yc0, src, skey) in enumerate(((0, 256, ohgT, "ohg"), (128, 384, osbT, "osb"))):
                    gb = (4 * half) % 8
                    yb = gb + 1
                    gb2 = gb + (tt % 2) * 2
                    yb2 = yb + (tt % 2) * 2
                    for kc in range(8):
                        A("pe", lambda e, kc=kc, gc0=gc0, gb2=gb2, ts=ts, ws=ws: e.matmul(banks[gb2][:, :], lhsT=wE[ws][:, kc, gc0:gc0 + 128], rhs=hT[:, kc, ts], start=(kc == 0), stop=(kc == 7)),
                          reads=[("wE", ws)], writes=[("ps", gb2)])
                    A("act", lambda e, half=half, gb2=gb2: e.activation(out=e_sa[half], in_=banks[gb2][:, :], func=AF.Sigmoid), reads=[("ps", gb2)], writes=[("sa", half)])
                    for kc in range(8):
                        A("pe", lambda e, kc=kc, yc0=yc0, yb2=yb2, ts=ts, src=src, ws=ws: e.matmul(banks[yb2][:, :], lhsT=wE[ws][:, kc, yc0:yc0 + 128], rhs=src[:, kc, ts], start=(kc == 0), stop=(kc == 7)),
                          reads=[("wE", ws)], writes=[("ps", yb2)])
                    A("dve", lambda e, half=half, yb2=yb2: e.tensor_tensor(out=e_m[half], in0=banks[yb2][:, :], in1=e_sa[half], op=ALU.mult), reads=[("ps", yb2), ("sa", half)], writes=[("m", half)])
                A("pool", lambda e, nb=nb, ts=ts: e.tensor_tensor(out=mixT[:, nb, ts], in0=e_m[0], in1=e_m[1], op=ALU.add), reads=[("m", 0), ("m", 1)], writes=[("mix", nb, tt)])
        P.barrier()
        if "mix" in dumps:
            A("sp", lambda e: e.dma_start(out=dumps["mix"], in_=reg(SCR, 32 * K, BF16)), dma_key="dump")
            P.barrier()

        if upto <= 4:
            return finish()
        wfo = reg(0, 44 * K, BF16).rearrange("p (j c) -> p j c", c=1024)
        wo = reg(SCR + 56 * K, 16 * K, BF16).rearrange("p (k c) -> p k c", c=1024)
        actT = reg(60 * K, 22 * K, BF16).rearrange("p (j t) -> p j t", t=512)
        o = SCR + 32 * K
        x2t = reg(o, 16 * K, F32).rearrange("p (b c) -> p b c", c=1024)
        o += 16 * K
        h2Tt = reg(o, 8 * K, BF16).rearrange("p (k t) -> p k t", t=512)
        o += 8 * K
        wf = [reg(44 * K + i * 4 * K, 4 * K, BF16).rearrange("p (k c) -> p k c", c=256) for i in range(3)]
        xn2 = [reg(56 * K + i * 2 * K, 2 * K, BF16) for i in range(2)]
        o += 16 * K
        xr = [reg(o + i * 4 * K, 4 * K, F32) for i in range(2)]
        o += 8 * K
        ost = [reg(o + i * 4 * K, 4 * K, F32) for i in range(2)]
        o += 8 * K
        assert o <= CO, o
        junk2 = reg(82 * K, 2 * K, BF16)
        g_sg = [reg(84 * K + i * 2 * K, 2 * K, F32) for i in range(2)]

        wfo_keys = [("wfo", jj) for jj in range(0, NJ, 2)]
        wfo_todo = list(range(0, NJ, 2))

        def wfo_dma():
            if wfo_todo:
                jj = wfo_todo.pop(0)
                A("pool", lambda e: e.dma_start(out=wfo[:, jj:jj + 2, :], in_=wfo_d[:, jj:jj + 2, :]), writes=[("wfo", jj)], dma_key="wfo")

        x2b1 = [reg(88 * K, 8 * K, F32).rearrange("p (b c) -> p b c", c=1024),
                reg(198 * K, 8 * K, F32).rearrange("p (b c) -> p b c", c=1024)]

        def x2(s_, tbl):
            if s_ == 0:
                return x2t[:, tbl, :]
            return x2b1[tbl // 2][:, tbl % 2, :]

        wfc = [0]

        def Fmm(tt, tbl):
            tb = tt * 4 + tbl
            xs = tb % 2
            s_ = tt % 2
            A("sp", lambda e: e.dma_start(out=xr[xs], in_=x[tb * 128:(tb + 1) * 128, :]), writes=[("xr", xs)], dma_key=("xr", xs))
            for cg in range(2):
                bk = (tbl * 2 + cg) % 4
                cs = slice(cg * 512, (cg + 1) * 512)
                for kc in range(8):
                    mm(banks[bk][:, :], mixT[:, kc, tb * 128:(tb + 1) * 128], wo[:, kc, cs], kc == 0, kc == 7, [("wo", cg)], [("ps", bk)])
                tt_("dve", x2(s_, tbl)[:, cs], banks[bk][:, :], xr[xs][:, cs], ALU.add, [("ps", bk), ("xr", xs)], [("x2", s_, tbl, cg)])

        def Frms(tt, tbl):
            tb = tt * 4 + tbl
            xs = tb % 2
            s_ = tt % 2
            rms_rows(x2(s_, tbl), ss2[:, tb:tb + 1], ln2[:, tb:tb + 1], rstd2[:, tb:tb + 1], g2bc, xn2[xs], junk2,
                     [("x2", s_, tbl, 0), ("x2", s_, tbl, 1)], [("xn2", xs)])

        def Ftr(tt, tbl):
            tb = tt * 4 + tbl
            xs = tb % 2
            bk = 4 + tbl % 2
            for kc in range(8):
                tr(banksb[bk][:, kc * 128:(kc + 1) * 128], xn2[xs][:, kc * 128:(kc + 1) * 128], [("xn2", xs)], [("ps", bk)])
            acopy(h2Tt[:, :, tbl * 128:(tbl + 1) * 128], banksb[bk][:, :].rearrange("p (k t) -> p k t", t=128), [("ps", bk)], [("h2", tbl)])

        h2k = [("h2", i) for i in range(4)]
        actk = [("act", j) for j in range(NJ)]

        def Gin(tt):
            for j in range(NJ):
                ws = wfc[0] % 3
                wfc[0] += 1
                A("pool", lambda e, j=j, ws=ws: e.dma_start(out=wf[ws], in_=wfi_d[j]), writes=[("wf", ws)], dma_key=("wf", ws))
                par = j % 2
                gb_ = (0, 2)[par]
                ub_ = (1, 3)[par]
                for kc in range(8):
                    mm(banks[gb_][:, :], wf[ws][:, kc, 0:128], h2Tt[:, kc, :], kc == 0, kc == 7, h2k + [("wf", ws)], [("ps", gb_)])
                for kc in range(8):
                    mm(banks[ub_][:, :], wf[ws][:, kc, 128:256], h2Tt[:, kc, :], kc == 0, kc == 7, h2k + [("wf", ws)], [("ps", ub_)])
                act(g_sg[par], banks[gb_][:, :], AF.Silu, [("ps", gb_)], [("gsg", par)])
                tt_("dve", actT[:, j, :], banks[ub_][:, :], g_sg[par], ALU.mult, [("ps", ub_), ("gsg", par)], [("act", j)])
                if j % 2 == 1:
                    wfo_dma()

        def Gout(tt, tbl):
            tb = tt * 4 + tbl
            os_ = tb % 2
            s_ = tt % 2
            for cg in range(2):
                bk = 4 + (tbl * 2 + cg) % 4
                cs = slice(cg * 512, (cg + 1) * 512)
                for j in range(NJ):
                    mm(banks[bk][:, :], actT[:, j, tbl * 128:(tbl + 1) * 128], wfo[:, j, cs], j == 0, j == NJ - 1, actk + wfo_keys, [("ps", bk)])
                tt_("dve", ost[os_][:, cs], banks[bk][:, :], x2(s_, tbl)[:, cs], ALU.add, [("ps", bk), ("x2", s_, tbl, cg)], [("ost", os_)])
            A("sp", lambda e: e.dma_start(out=out[tb * 128:(tb + 1) * 128, :], in_=ost[os_]), reads=[("ost", os_)], writes=[("out", tb)], dma_key=("ost", os_))

        for tbl in range(4):
            Fmm(0, tbl)
        Frms(0, 0)
        Frms(0, 1)
        Ftr(0, 0)
        Frms(0, 2)
        Ftr(0, 1)
        Frms(0, 3)
        Ftr(0, 2)
        Ftr(0, 3)
        for tt in range(4):
            Gin(tt)
            n_ = tt + 1
            if n_ < 4:
                for tbl in range(4):
                    Fmm(n_, tbl)
                Frms(n_, 0)
                Frms(n_, 1)
                Gout(tt, 0)
                Ftr(n_, 0)
                Ftr(n_, 1)
                Frms(n_, 2)
                Frms(n_, 3)
                Gout(tt, 1)
                Gout(tt, 2)
                Ftr(n_, 2)
                Ftr(n_, 3)
                Gout(tt, 3)
            else:
                for tbl in range(4):
                    Gout(tt, tbl)
        P.barrier()
        A("sp", lambda e: e.nop())
        P.emit()
    return nc


def _blk(w, cols):
    return np.ascontiguousarray(w[:, cols].reshape(8, 128, -1).transpose(1, 0, 2))


def _prep_weights(w_in, w_hg_out, w_sb_out, w_o, w_ffn_in, w_ffn_out):
    ar = np.arange
    wiv = np.stack([_blk(w_in, ar(2048 + g * 512, 2048 + (g + 1) * 512)) for g in range(2)] +
                   [_blk(w_in, ar(6144 + g * 512, 6144 + (g + 1) * 512)) for g in range(2)])
    whg = np.stack([_blk(w_in, np.concatenate([ar(h * 128, (h + 1) * 128), 1024 + ar(h * 128, (h + 1) * 128), 3072 + ar(h * 128, (h + 1) * 128)])) for h in range(8)])
    wsb = np.stack([_blk(w_in, np.concatenate([4096 + ar(p * 128, (p + 1) * 128), 5120 + ar(p * 128, (p + 1) * 128)])) for p in range(8)])
    we = np.stack([np.concatenate([_blk(w_in, np.concatenate([7168 + ar(nb * 128, (nb + 1) * 128), 8192 + ar(nb * 128, (nb + 1) * 128)])),
                                   _blk(w_hg_out, ar(nb * 128, (nb + 1) * 128)), _blk(w_sb_out, ar(nb * 128, (nb + 1) * 128))], axis=2) for nb in range(8)])
    wo = np.stack([_blk(w_o, ar(cg * 512, (cg + 1) * 512)) for cg in range(2)])
    wfi = np.stack([_blk(w_ffn_in, np.concatenate([ar(j * 128, (j + 1) * 128), FF + ar(j * 128, (j + 1) * 128)])) for j in range(NJ)])
    wfo = np.ascontiguousarray(w_ffn_out.reshape(NJ, 128, 1024).transpose(1, 0, 2))
    return dict(wiv=wiv, whg=whg, wsb=wsb, we=we, wo=wo, wfi=wfi, wfo=wfo)


def _consts(norm1_gain, norm2_gain, lb_logits, hg_out_norm, sb_q_norm, sb_k_norm):
    cf = np.zeros((128, CF_N), np.float32)
    cf[:, CF_G1:CF_G1 + 1024] = norm1_gain.reshape(1, 1024)
    cf[:, CF_G2:CF_G2 + 1024] = norm2_gain.reshape(1, 1024)
    t = np.arange(512)
    cf[:, CF_RST:CF_RST + 512] = (t % 64 != 0).astype(np.float32)[None, :]
    pp = np.arange(128)[:, None]
    tq = np.arange(64)[None, :]
    cf[:, CF_M64:CF_M64 + 64] = ((pp < 64) & (pp <= tq)).astype(np.float32)
    cf[:, CF_M64 + 64:CF_M64 + 128] = ((pp >= 64) & (pp - 64 <= tq)).astype(np.float32)
    cf[:, CF_LBL:CF_LBL + 16] = lb_logits.reshape(2, 8, 128).transpose(2, 0, 1).reshape(128, 16)
    cf[:, CF_GQ:CF_GQ + 8] = sb_q_norm.reshape(8, 128).T
    cf[:, CF_GK:CF_GK + 8] = sb_k_norm.reshape(8, 128).T
    cf[:, CF_GHG:CF_GHG + 8] = hg_out_norm.reshape(8, 128).T
    cf[:, CF_EPS] = EPS
    cf[:, CF_ONE] = 1.0
    cb = np.zeros((128, CB_N), np.float32)
    j = np.arange(128)[:, None]
    s = np.arange(128)[None, :]
    cb[:, 0:128] = (j == s)
    cb[:, 128:256] = (j >= s)
    cb[:, 256:384] = 1.0
    cb[:, 384:512] = (j // 64 == s // 64)
    cb[:, 512:640] = BIG * (j >= s)
    return cf, cb.astype(ml_dtypes.bfloat16)


_NC_CACHE = {}


def kernel(x, norm1_gain, w_in, lb_logits, hg_out_norm, sb_q_norm, sb_k_norm,
           w_hg_out, w_sb_out, w_o, norm2_gain, w_ffn_in, w_ffn_out, _dump=None, _upto=99, _trace=False):
    f = lambda a: np.asarray(a, dtype=np.float32)
    x = f(x)
    wd = _prep_weights(f(w_in)[0], f(w_hg_out)[0], f(w_sb_out)[0], f(w_o)[0], f(w_ffn_in)[0], f(w_ffn_out)[0])
    cf, cb = _consts(f(norm1_gain), f(norm2_gain), f(lb_logits), f(hg_out_norm), f(sb_q_norm), f(sb_k_norm))
    key = (tuple(_dump or ()), _upto)
    if key not in _NC_CACHE:
        _NC_CACHE[key] = build_nc(_dump, _upto)
    nc = _NC_CACHE[key]
    in_maps = []
    for b in range(8):
        m = dict(wd)
        m["x"] = np.ascontiguousarray(x[b])
        m["cf"] = cf
        m["cb"] = cb
        in_maps.append(m)
    if _trace:
        res = run_bass_kernel_spmd(nc, in_maps, core_ids=list(range(8)), trace=True)
        print('exec_time_ns', res.exec_time_ns)
    else:
        res = run_bass_kernel_spmd(nc, in_maps, core_ids=list(range(8)))
    outp = np.stack([res.results[b]["out"] for b in range(8)]).astype(np.float32)
    if _dump:
        return outp, res.results
    return outp
```
